# Optimizing a Trainium2 kernel written in Bass

```python
import jax, jax.numpy as jnp
from jax import lax
import numpy as np

D_MODEL = 1024
BATCH = 8
SEQ = 8192
DEPTH = 2

N_MIXERS = 2
N_A_LAYERS = (DEPTH + 1) // 2
N_B_LAYERS = DEPTH // 2
CHUNK = 128
GM_WIDTH = 2 * D_MODEL
GM_GROUPS = 8
GM_GROUP_DIM = GM_WIDTH // GM_GROUPS
N_HEADS = 16
HEAD_DIM = D_MODEL // N_HEADS
Q_BLOCK = 128
FFN_DIM = 2 * D_MODEL
CONV_WIDTH = 3
RMS_EPS = 1e-6
LN_EPS = 1e-5
FORGET_BIAS_INIT = 4.0

kernel_name = "hybrid_gmlp_fox_convffn"


def rmsnorm(x, g):
    xf = x.astype(jnp.float32)
    y = xf * lax.rsqrt(jnp.mean(xf * xf, axis=-1, keepdims=True) + RMS_EPS)
    return (y * g.astype(jnp.float32)).astype(x.dtype)


def layernorm(x, g, b):
    xf = x.astype(jnp.float32)
    mu = jnp.mean(xf, axis=-1, keepdims=True)
    xc = xf - mu
    y = xc * lax.rsqrt(jnp.mean(xc * xc, axis=-1, keepdims=True) + LN_EPS)
    return (y * g.astype(jnp.float32) + b.astype(jnp.float32)).astype(x.dtype)


def chunked_spatial_gating_mixer(h, w_in, ln_g, ln_b, w_s, b_s, w_out):
    bsz, seq, _ = h.shape
    z = jax.nn.gelu(h @ w_in)
    u, v = jnp.split(z, 2, axis=-1)
    v = layernorm(v, ln_g, ln_b)
    n_chunks = seq // CHUNK
    v = v.reshape(bsz, n_chunks, CHUNK, GM_GROUPS, GM_GROUP_DIM)
    causal = jnp.tril(jnp.ones((CHUNK, CHUNK), dtype=bool))
    w_causal = jnp.where(causal[None], w_s, jnp.zeros((), w_s.dtype))
    s = jnp.einsum("gts,bnsgc->bntgc", w_causal, v)
    s = s + b_s.T[None, None, :, :, None]
    s = s.reshape(bsz, seq, GM_WIDTH)
    return (u * s) @ w_out


def forgetting_attention_mixer(h, w_qkvf, b_f, w_o):
    bsz, seq, _ = h.shape
    proj = h @ w_qkvf
    q, k, v, f_logit = jnp.split(proj, [D_MODEL, 2 * D_MODEL, 3 * D_MODEL], axis=-1)
    q = q.reshape(bsz, seq, N_HEADS, HEAD_DIM)
    k = k.reshape(bsz, seq, N_HEADS, HEAD_DIM)
    v = v.reshape(bsz, seq, N_HEADS, HEAD_DIM)
    log_f = jax.nn.log_sigmoid((f_logit + b_f).astype(jnp.float32))
    cum = jnp.cumsum(log_f, axis=1).transpose(0, 2, 1)
    scale = HEAD_DIM ** -0.5
    neg = jnp.finfo(jnp.float32).min
    outs = []
    for blk in range(seq // Q_BLOCK):
        q0 = blk * Q_BLOCK
        q1 = q0 + Q_BLOCK
        qb = q[:, q0:q1]
        kb = k[:, :q1]
        vb = v[:, :q1]
        logits = jnp.einsum("bqhd,bkhd->bhqk", qb, kb,
                            preferred_element_type=jnp.float32) * scale
        logits = logits + cum[:, :, q0:q1, None] - cum[:, :, None, :q1]
        mask = jnp.arange(q0, q1)[:, None] >= jnp.arange(q1)[None, :]
        logits = jnp.where(mask, logits, neg)
        p = jax.nn.softmax(logits, axis=-1)
        outs.append(jnp.einsum("bhqk,bkhd->bqhd", p.astype(vb.dtype), vb))
    o = jnp.concatenate(outs, axis=1).reshape(bsz, seq, D_MODEL)
    return o @ w_o


def conv_gated_ffn(h, w_gate, w_up, conv_w, conv_b, w_down):
    seq = h.shape[1]
    a = h @ w_gate
    a_pad = jnp.pad(a, ((0, 0), (CONV_WIDTH - 1, 0), (0, 0)))
    a = conv_b + a_pad[:, 0:seq] * conv_w[0]
    for i in range(1, CONV_WIDTH):
        a = a + a_pad[:, i:i + seq] * conv_w[i]
    return (jax.nn.silu(a) * (h @ w_up)) @ w_down


def setup_inputs(seed: int = 0) -> dict:
    key = jax.random.key(seed)
    ks = jax.random.split(key, 20)

    def normal(k, shape, scale):
        return jax.random.normal(k, shape, jnp.float32) * scale

    D, E, G, C, H, F = D_MODEL, GM_WIDTH, GM_GROUPS, CHUNK, N_HEADS, FFN_DIM
    return {
        "x": normal(ks[0], (BATCH, SEQ, D), 1.0),
        "mix_norm_g": 1.0 + normal(ks[1], (DEPTH, D), 0.02),
        "ffn_norm_g": 1.0 + normal(ks[2], (DEPTH, D), 0.02),
        "gm_w_in": normal(ks[3], (N_A_LAYERS, D, 2 * E), D ** -0.5),
        "gm_ln_g": 1.0 + normal(ks[4], (N_A_LAYERS, E), 0.02),
        "gm_ln_b": normal(ks[5], (N_A_LAYERS, E), 0.02),
        "gm_w_s": normal(ks[6], (N_A_LAYERS, G, C, C), C ** -0.5),
        "gm_b_s": 1.0 + normal(ks[7], (N_A_LAYERS, G, C), 0.02),
        "gm_w_out": normal(ks[8], (N_A_LAYERS, E, D), E ** -0.5),
        "fox_w_qkvf": normal(ks[9], (N_B_LAYERS, D, 3 * D + H), D ** -0.5),
        "fox_b_f": FORGET_BIAS_INIT + normal(ks[10], (N_B_LAYERS, H), 0.5),
        "fox_w_o": normal(ks[11], (N_B_LAYERS, D, D), D ** -0.5),
        "ffn_w_gate": normal(ks[12], (DEPTH, D, F), D ** -0.5),
        "ffn_w_up": normal(ks[13], (DEPTH, D, F), D ** -0.5),
        "ffn_conv_w": normal(ks[14], (DEPTH, CONV_WIDTH, F), CONV_WIDTH ** -0.5),
        "ffn_conv_b": normal(ks[15], (DEPTH, F), 0.01),
        "ffn_w_down": normal(ks[16], (DEPTH, F, D), F ** -0.5),
        "final_norm_g": 1.0 + normal(ks[17], (D,), 0.02),
    }


def reference(x, mix_norm_g, ffn_norm_g, gm_w_in, gm_ln_g, gm_ln_b, gm_w_s, gm_b_s,
              gm_w_out, fox_w_qkvf, fox_b_f, fox_w_o, ffn_w_gate, ffn_w_up,
              ffn_conv_w, ffn_conv_b, ffn_w_down, final_norm_g):
    h = x
    for i in range(DEPTH):
        hn = rmsnorm(h, mix_norm_g[i])
        j = i // N_MIXERS
        if i % N_MIXERS == 0:
            mix = chunked_spatial_gating_mixer(hn, gm_w_in[j], gm_ln_g[j], gm_ln_b[j],
                                               gm_w_s[j], gm_b_s[j], gm_w_out[j])
        else:
            mix = forgetting_attention_mixer(hn, fox_w_qkvf[j], fox_b_f[j], fox_w_o[j])
        h = h + mix
        hn = rmsnorm(h, ffn_norm_g[i])
        h = h + conv_gated_ffn(hn, ffn_w_gate[i], ffn_w_up[i], ffn_conv_w[i],
                               ffn_conv_b[i], ffn_w_down[i])
    return rmsnorm(h, final_norm_g)
```

```python
from contextlib import ExitStack
import numpy as np
import ml_dtypes
import concourse.bass as bass
import concourse.mybir as mybir
from concourse.bass_utils import run_bass_kernel_spmd

F32 = mybir.dt.float32
BF16 = mybir.dt.bfloat16
AF = mybir.ActivationFunctionType
ALU = mybir.AluOpType
AX = mybir.AxisListType

D = 1024
E = 2048
FF = 2048
NH = 16
DH = 64
RMS_EPS = 1e-6
LN_EPS = 1e-5
NEG = -30000.0
USE_PT2 = True
USE_BCAST = True


class Sem:
    def __init__(self, h, name):
        self.h = h
        self.name = name
        self.n = 0


class Ctx:
    CE = ("pe", "act", "dve", "pool")
    ENG = ("pe", "act", "dve", "pool", "sp")

    def __init__(self, nc, es):
        self.nc = nc
        self.es = es
        self.esem = {e: Sem(es.enter_context(nc.semaphore("e_" + e)), "e_" + e) for e in self.CE}
        self.tickn = {e: 0 for e in self.CE}
        self.base = {e: 0 for e in self.CE}
        self.semcount = {e: 0 for e in self.CE}
        self.ops = {e: [] for e in self.ENG}
        self.waited = {e: {} for e in self.ENG}
        self.needed = {e: set() for e in self.CE}
        self.dsems = []
        self.dbase = {}

    def dsem(self, name, barrier=True):
        s = Sem(self.es.enter_context(self.nc.semaphore(name)), name)
        if barrier:
            self.dsems.append(s)
        self.dbase[name] = 0
        return s

    def op(self, eng, fn, waits=(), dma_sem=None, notick=False):
        wl = []
        for w in waits:
            if w is None:
                continue
            kind, a, v = w
            if kind == "e":
                if v <= self.base[a]:
                    continue
                key = a
            else:
                if v <= self.dbase[a.name]:
                    continue
                key = a.name
            if self.waited[eng].get(key, 0) >= v:
                continue
            self.waited[eng][key] = v
            wl.append(w)
            if kind == "e":
                self.needed[a].add(v)
        t = None
        if dma_sem is not None:
            dma_sem.n += 16
            t = ("d", dma_sem, dma_sem.n)
        elif eng in self.esem and not notick and fn is not None:
            self.tickn[eng] += 1
            t = ("e", eng, self.tickn[eng])
        self.ops[eng].append((fn, wl, t))
        return t

    def last(self, eng):
        return ("e", eng, self.tickn[eng])

    def flush(self):
        allw = [self.last(e) for e in self.CE] + [("d", s, s.n) for s in self.dsems]
        for e in self.ENG:
            self.op(e, None, allw)
        rank = {}
        for e in self.CE:
            ids = sorted(self.needed[e])
            rank[e] = {tid: self.semcount[e] + i + 1 for i, tid in enumerate(ids)}
            self.semcount[e] += len(ids)
        ops = self.ops
        esem = self.esem

        def mk(key):
            def body(e):
                for fn, wl, t in ops[key]:
                    for kind, a, v in wl:
                        if kind == "e":
                            e.wait_ge(esem[a].h, rank[a][v])
                        else:
                            e.wait_ge(a.h, v)
                    if fn is None:
                        continue
                    ins = fn(e)
                    if t is not None:
                        if t[0] == "d":
                            ins.then_inc(t[1].h, 16)
                        elif t[2] in rank[t[1]]:
                            ins.then_inc(esem[t[1]].h, 1)
            return body

        with self.nc.Block() as block:
            block.tensor(mk("pe"))
            block.scalar(mk("act"))
            block.vector(mk("dve"))
            block.gpsimd(mk("pool"))
            block.sync(mk("sp"))
        self.ops = {e: [] for e in self.ENG}
        self.waited = {e: {} for e in self.ENG}
        self.needed = {e: set() for e in self.CE}
        for e in self.CE:
            self.base[e] = self.tickn[e]
        for s in self.dsems:
            self.dbase[s.name] = s.n


class Banks:
    def __init__(self, tiles):
        self.tiles = tiles
        self.free = [[] for _ in tiles]
        self.i = 0

    def get(self):
        b = self.i % len(self.tiles)
        self.i += 1
        w = self.free[b]
        self.free[b] = []
        return b, w

    def rel(self, b, tick):
        self.free[b].append(tick)


def build(S, dbg=False):
    nc = bass.Bass("TRN2", target_bir_lowering=False)
    T = min(512, S)
    NSUB = T // 128
    NST = S // T
    NBLK = S // 128
    okind = "ExternalOutput" if dbg else "Internal"

    def din(name, shape, dt=F32):
        return nc.dram_tensor(name, list(shape), dt, kind="ExternalInput").ap()

    x = din("x", [S, D])
    mix_norm_g = din("mix_norm_g", [2, D])
    ffn_norm_g = din("ffn_norm_g", [2, D])
    gm_w_in = din("gm_w_in", [D, 2 * E])
    gm_ln_g = din("gm_ln_g", [E])
    gm_ln_b = din("gm_ln_b", [E])
    gm_w_s = din("gm_w_s", [8, 128, 128])
    gm_b_s = din("gm_b_s", [8, 128])
    gm_w_out = din("gm_w_out", [E, D])
    fox_w_qkvf = din("fox_w_qkvf", [D, 3 * D + NH])
    fox_b_f = din("fox_b_f", [NH])
    fox_w_o = din("fox_w_o", [D, D])
    ffn_w_gate = din("ffn_w_gate", [2, D, FF])
    ffn_w_up = din("ffn_w_up", [2, D, FF])
    ffn_conv_w = din("ffn_conv_w", [2, 3, FF])
    ffn_conv_b = din("ffn_conv_b", [2, FF])
    ffn_w_down = din("ffn_w_down", [2, FF, D])
    final_norm_g = din("final_norm_g", [D])
    c_ident_bf = din("c_ident_bf", [128, 128], BF16)
    c_ident_f = din("c_ident_f", [128, 128])
    c_tri_f = din("c_tri_f", [128, 128])
    c_maskneg = din("c_maskneg", [128, 128], BF16)
    y = nc.dram_tensor("y", [S, D], F32, kind="ExternalOutput").ap()

    wscr = nc.dram_tensor("wscr", [44, 128, 8, 512], BF16).ap()
    h1s = nc.dram_tensor("h1s", [S, D], F32, kind=okind).ap()
    qTs = nc.dram_tensor("qTs", [D, S], BF16, kind=okind).ap()
    kTs = nc.dram_tensor("kTs", [D, S], BF16, kind=okind).ap()
    vs = nc.dram_tensor("vs", [S, D], BF16, kind=okind).ap()
    aTs = nc.dram_tensor("aTs", [D, S], BF16, kind=okind).ap()

    chunks = []
    for n in range(8):
        chunks.append(gm_w_in[:, n * 512:(n + 1) * 512])
    for n in range(2):
        for kh in range(2):
            chunks.append(gm_w_out[kh * 1024:(kh + 1) * 1024, n * 512:(n + 1) * 512])
    for l in range(2):
        base = len(chunks)
        for n in range(4):
            chunks.append(ffn_w_gate[l, :, n * 512:(n + 1) * 512])
        for n in range(4):
            chunks.append(ffn_w_up[l, :, n * 512:(n + 1) * 512])
        for n in range(2):
            for kh in range(2):
                chunks.append(ffn_w_down[l, kh * 1024:(kh + 1) * 1024, n * 512:(n + 1) * 512])
        if l == 0:
            for n in range(6):
                chunks.append(fox_w_qkvf[:, n * 512:(n + 1) * 512])
            for n in range(2):
                chunks.append(fox_w_o[:, n * 512:(n + 1) * 512])
    assert len(chunks) == 44
    CI_WIN, CI_WOUT, CI_G0, CI_U0, CI_D0, CI_QKV, CI_WO, CI_G1, CI_U1, CI_D1 = 0, 8, 12, 16, 20, 24, 30, 32, 36, 40

    es = ExitStack()
    with es:
        cx = Ctx(nc, es)

        def sb(name, shape, dt, stack=es):
            return stack.enter_context(nc.sbuf_tensor(name, list(shape), dt))

        dsem_dbg = cx.dsem("dbgsem")

        def dump(name, ap, shape, waits):
            if not dbg:
                return None
            dt_ = nc.dram_tensor("dbg_" + name, list(shape), F32, kind="ExternalOutput").ap()
            return cx.op("pool", lambda e: e.dma_start(out=dt_, in_=ap), waits=waits, dma_sem=dsem_dbg)

        psbig = es.enter_context(nc.psum_tensor("psbig", [128, 7, 512], F32))
        ps = [psbig[:, i, :] for i in range(7)]
        pT = es.enter_context(nc.psum_tensor("pT", [128, 1024], BF16))

        ident_bf = sb("ident_bf", [128, 128], BF16)
        ident_f = sb("ident_f", [128, 128], F32)
        tri_f = sb("tri_f", [128, 128], F32)
        ones_f = sb("ones_f", [128, 128], F32)
        maskneg = sb("maskneg", [128, 128], BF16)
        COLS = sb("COLS", [128, 192], F32)
        WcT_bf = sb("WcT_bf", [128, 8, 128], BF16)
        BiasT = sb("BiasT", [128, 16, 128], F32)
        bf_bc = sb("bf_bc", [128, NH], F32)
        gfin_bc = sb("gfin_bc", [128, D], F32)
        wf_bf = sb("wf_bf", [128, 8, NH], BF16)
        CUM = sb("CUM", [128, NBLK, NH], F32)
        CARC = sb("CARC", [128, NBLK + 1, NH], F32)
        NSTAT = 3072
        STAT = sb("STAT", [128, NSTAT], F32)
        CCAR = sb("CCAR", [128, 2, 16, 2], F32)
        junk = sb("junk", [128, 2048], BF16)

        stat_i = [0]

        def stat(n=1):
            i = stat_i[0]
            stat_i[0] += n
            assert stat_i[0] <= NSTAT, "STAT overflow"
            return STAT[:, i:i + n]

        def col_mixg(l, k): return COLS[:, l * 8 + k: l * 8 + k + 1]
        def col_ffng(l, k): return COLS[:, 16 + l * 8 + k: 16 + l * 8 + k + 1]
        def col_lng(k): return COLS[:, 32 + k:33 + k]
        def col_lnb(k): return COLS[:, 48 + k:49 + k]
        def col_cb(l, k): return COLS[:, 64 + l * 16 + k: 65 + l * 16 + k]
        def col_cw(l, i, k): return COLS[:, 96 + (l * 3 + i) * 16 + k: 97 + (l * 3 + i) * 16 + k]

        cast_sems = [cx.dsem("cast%d" % i, barrier=False) for i in range(44)]
        ld0 = cx.dsem("ld0")
        ld_pool = cx.dsem("ld_pool")

        cast_tick = [None] * 44
        cast_order = [4, 5, 6, 7, 0, 1, 2, 3] + list(range(8, 44))
        for i in cast_order:
            src = chunks[i]
            cast_tick[i] = cx.op("pool", lambda e, i=i, src=src: e.dma_start(
                out=wscr[i], in_=src.rearrange("(k p) n -> p k n", p=128)), dma_sem=cast_sems[i])

        with ExitStack() as p0:
            R1 = sb("R1", [128, 128], F32, p0)
            R2 = sb("R2", [64, 128], F32, p0)
            WS = sb("WS", [128, 8, 128], F32, p0)
            WcT_f = sb("WcT_f", [128, 8, 128], F32, p0)
            BSb = sb("BSb", [128, 1024], F32, p0)
            loads = [
                (ident_bf[:], c_ident_bf), (ident_f[:], c_ident_f), (tri_f[:], c_tri_f), (maskneg[:], c_maskneg),
                (R1[0:16, :], mix_norm_g.rearrange("l (k p) -> (l k) p", p=128)),
                (R1[16:32, :], ffn_norm_g.rearrange("l (k p) -> (l k) p", p=128)),
                (R1[32:48, :], gm_ln_g.rearrange("(k p) -> k p", p=128)),
                (R1[48:64, :], gm_ln_b.rearrange("(k p) -> k p", p=128)),
                (R1[64:96, :], ffn_conv_b.rearrange("l (k p) -> (l k) p", p=128)),
                (R1[96:128, :], ffn_conv_w.rearrange("l i (k p) -> (l i k) p", p=128)[0:32]),
                (R2[0:64, :], ffn_conv_w.rearrange("l i (k p) -> (l i k) p", p=128)[32:96]),
                (WS[:], gm_w_s.rearrange("g t s -> t g s")),
                (BSb[:], gm_b_s.rearrange("g t -> (g t)").partition_broadcast(128)),
                (bf_bc[:], fox_b_f.partition_broadcast(128)),
                (gfin_bc[:], final_norm_g.partition_broadcast(128)),
            ]
            for o, i_ in loads:
                cx.op("sp", lambda e, o=o, i_=i_: e.dma_start(out=o, in_=i_), dma_sem=ld0)
            ld0_all = ("d", ld0, ld0.n)
            t_wf = cx.op("pool", lambda e: e.dma_start(
                out=wf_bf[:], in_=fox_w_qkvf[:, 3 * D:3 * D + NH].rearrange("(k p) n -> p k n", p=128)), dma_sem=ld_pool)
            t_m1 = cx.op("dve", lambda e: e.memset(ones_f[:], 1.0))
            t_m2 = cx.op("dve", lambda e: e.memset(STAT[:], 0.0))
            t_m3 = cx.op("dve", lambda e: e.memset(CCAR[:], 0.0))
            t_m4 = cx.op("dve", lambda e: e.memset(CARC[:, 0, :], 0.0))
            cx.op("pe", lambda e: e.matmul(ps[0][:, 0:128], lhsT=R1[:, :], rhs=ident_f[:, :], start=True, stop=True),
                  waits=[ld0_all], notick=True)
            t_pe = cx.op("pe", lambda e: e.matmul(ps[0][:, 128:192], lhsT=R2[0:64, :], rhs=ident_f[0:64, 0:64], start=True, stop=True))
            t_cols = cx.op("dve", lambda e: e.tensor_copy(out=COLS[:], in_=ps[0][:, 0:192]), waits=[t_pe])
            for g in range(8):
                b = 1 + (g % 2) * 2
                t_a = cx.op("pe", lambda e, g=g, b=b: e.matmul(ps[b][:, 0:128], lhsT=WS[:, g, :], rhs=ident_f[:, :], start=True, stop=True),
                            waits=[ld0_all, cx.last("dve")])
                t_b = cx.op("dve", lambda e, g=g, b=b: e.tensor_tensor(out=WcT_f[:, g, :], in0=ps[b][:, 0:128], in1=tri_f[:, :], op=ALU.mult), waits=[t_a])
                t_c = cx.op("dve", lambda e, g=g: e.tensor_copy(out=WcT_bf[:, g, :], in_=WcT_f[:, g, :]), waits=[t_b])
                t_d = cx.op("pe", lambda e, g=g, b=b: e.matmul(ps[b + 1][:, 0:128], lhsT=ones_f[:, :], rhs=WcT_f[:, g, :], start=True, stop=True),
                            waits=[t_b, t_m1])
                for half in range(2):
                    k = 2 * g + half
                    cx.op("dve", lambda e, g=g, b=b, k=k: e.scalar_tensor_tensor(
                        out=BiasT[:, k, :], in0=ps[b + 1][:, 0:128], scalar=col_lnb(k), in1=BSb[:, g * 128:(g + 1) * 128],
                        op0=ALU.mult, op1=ALU.add), waits=[t_d, t_cols])
            dump("COLS", COLS[:], [128, 192], [cx.last("dve")])
            dump("WcT", WcT_bf[:], [128, 8, 128], [cx.last("dve")])
            dump("BiasT", BiasT[:], [128, 16, 128], [cx.last("dve")])
            cx.flush()

        class WRing:
            def __init__(self, tiles, sems):
                self.tiles = tiles
                self.sems = sems
                self.free = [None] * len(tiles)
                self.i = 0

            def load(self, ci):
                slot = self.i % len(self.tiles)
                self.i += 1
                tile = self.tiles[slot]
                assert self.free[slot] != "PENDING", "weight ring slot reused before release"
                t = cx.op("sp", lambda e, tile=tile, ci=ci: e.dma_start(out=tile[:], in_=wscr[ci]),
                          waits=[self.free[slot], cast_tick[ci]], dma_sem=self.sems[slot])
                self.free[slot] = "PENDING"
                return slot, tile, t

            def rel(self, slot, tick):
                self.free[slot] = tick

        wsems = [cx.dsem("wslot%d" % i) for i in range(4)]

        if USE_PT2:
            pT2 = psbig[:, 6, :].bitcast(BF16)
            pTs = [pT[:, :], pT2]
        else:
            pTs = [pT[:, :], pT[:, :]]

        def norm_group(hbuf, g0, hnT, waits_per_j, ns):
            s1 = {}

            def stage1(j, ew_=None):
                hsl = hbuf[:, j, :]
                ss = stat(); ms = stat(); rr = stat(); rstd = stat()
                xi = ns["xi"] % len(ns["xs"])
                ns["xi"] += 1
                xsb = ns["xs"][xi]
                ew = list(waits_per_j[j]) if ew_ is None else list(ew_)
                t1 = cx.op("act", lambda e: e.activation(out=junk[:, 0:1024], in_=hsl, func=AF.Square, accum_out=ss), waits=ew)
                t2 = cx.op("dve", lambda e: e.tensor_scalar(out=ms, in0=ss, scalar1=1.0 / D, scalar2=RMS_EPS, op0=ALU.mult, op1=ALU.add), waits=[t1])
                t3 = cx.op("dve", lambda e: e.reciprocal(out=rr, in_=ms), waits=[t2])
                t4 = cx.op("act", lambda e: e.activation(out=rstd, in_=rr, func=AF.Sqrt), waits=[t3])
                t5 = cx.op("dve", lambda e: e.tensor_scalar(out=xsb[:], in0=hsl, scalar1=rstd, scalar2=None, op0=ALU.mult),
                           waits=[t4] + ns["xs_free"][xi] + ew)
                s1[j] = (xi, xsb, t5)

            def stage2(j, extra=()):
                xi, xsb, t5 = s1[j]
                pi = (ns["pi"] % 2) if USE_PT2 else 0
                ns["pi"] += 1
                pTb = pTs[pi]
                tp = None
                for k in range(8):
                    tp = cx.op("pe", lambda e, k=k: e.transpose(pTb[:, k * 128:(k + 1) * 128], xsb[:, k * 128:(k + 1) * 128], ident_bf[:]),
                               waits=[t5] + ns["pT_free"][pi], notick=(k < 7))
                ns["xs_free"][xi] = [tp]
                if USE_BCAST:
                    tl = cx.op("dve", lambda e: e.tensor_tensor(
                        out=hnT[:, :, j * 128:(j + 1) * 128], in0=pTb.rearrange("p (k t) -> p k t", k=8),
                        in1=COLS[:, g0:g0 + 8].unsqueeze(2).broadcast_to([128, 8, 128]), op=ALU.mult), waits=[tp] + list(extra))
                else:
                    for k in range(8):
                        tl = cx.op("dve", lambda e, k=k: e.tensor_scalar(out=hnT[:, k, j * 128:(j + 1) * 128], in0=pTb[:, k * 128:(k + 1) * 128],
                                                                        scalar1=COLS[:, g0 + k:g0 + k + 1], scalar2=None, op0=ALU.mult), waits=[tp])
                ns["pT_free"][pi] = [tl]
                return tl

            if waits_per_j is None:
                return stage1, stage2
            ticks = [None] * NSUB
            stage1(0)
            for j in range(NSUB):
                if j + 1 < NSUB:
                    stage1(j + 1)
                ticks[j] = stage2(j)
            return ticks

        def norm_cb(s1_, s2_, out):
            def cb(j, r):
                s1_(j, r)
                if j >= 1:
                    out[j - 1] = s2_(j - 1)
            return cb

        def new_norm_state(stack, tag):
            return {"xs": [sb("xs%d%s" % (i, tag), [128, D], BF16, stack) for i in range(4)], "xs_free": [[] for _ in range(4)],
                    "pT_free": [[], []], "xi": 0, "pi": 0}

        def ffn(l, ci_g, ci_u, ci_d, hbuf, hnT, hn_ticks, wr, bank, st, on_res=None):
            actT, Abuf, tbuf, sgbuf = st["actT"], st["A"], st["tb"], st["sg"]
            act_ticks = []
            for n in range(4):
                gs, gt, gtk = wr.load(ci_g + n)
                us, ut, utk = wr.load(ci_u + n)
                tl = None
                for f in range(4):
                    fc = n * 4 + f
                    bg, wg = bank.get()
                    tg = None
                    for k in range(8):
                        tg = cx.op("pe", lambda e, k=k, f=f, bg=bg, gt=gt: e.matmul(ps[bg][:, 0:T], lhsT=gt[:, k, f * 128:(f + 1) * 128], rhs=hnT[:, k, :],
                                                                                 start=(k == 0), stop=(k == 7)),
                                   waits=[gtk] + wg + hn_ticks + st["actT_free"], notick=(k < 7))
                    bu, wu = bank.get()
                    tu = None
                    for k in range(8):
                        tu = cx.op("pe", lambda e, k=k, f=f, bu=bu, ut=ut: e.matmul(ps[bu][:, 0:T], lhsT=ut[:, k, f * 128:(f + 1) * 128], rhs=hnT[:, k, :],
                                                                                 start=(k == 0), stop=(k == 7)),
                                   waits=[utk] + wu, notick=(k < 7))
                    tl = tu
                    ai = fc % len(Abuf)
                    A = Abuf[ai]
                    tb = tbuf[fc % len(tbuf)]
                    sg = sgbuf[fc % len(sgbuf)]
                    ta = cx.op("act", lambda e, A=A, bg=bg: e.activation(out=A[:, 2:T + 2], in_=ps[bg][:, 0:T], func=AF.Identity),
                               waits=[tg] + st["A_free"][ai])
                    bank.rel(bg, ta)
                    tc0 = cx.op("dve", lambda e, A=A, fc=fc: e.tensor_copy(out=A[:, 0:2], in_=CCAR[:, l, fc, :]), waits=st["A_free"][ai])
                    t1 = cx.op("dve", lambda e, A=A, tb=tb, fc=fc: e.tensor_scalar(out=tb[:, 0:T], in0=A[:, 2:T + 2], scalar1=col_cw(l, 2, fc), scalar2=col_cb(l, fc),
                                                                               op0=ALU.mult, op1=ALU.add), waits=[ta] + st["tb_free"][fc % len(tbuf)])
                    t2 = cx.op("dve", lambda e, A=A, tb=tb, fc=fc: e.scalar_tensor_tensor(out=tb[:, 0:T], in0=A[:, 1:T + 1], scalar=col_cw(l, 1, fc), in1=tb[:, 0:T],
                                                                                      op0=ALU.mult, op1=ALU.add), waits=[t1, tc0])
                    t3 = cx.op("dve", lambda e, A=A, tb=tb, fc=fc: e.scalar_tensor_tensor(out=tb[:, 0:T], in0=A[:, 0:T], scalar=col_cw(l, 0, fc), in1=tb[:, 0:T],
                                                                                      op0=ALU.mult, op1=ALU.add), waits=[t2])
                    tc1 = cx.op("dve", lambda e, A=A, fc=fc: e.tensor_copy(out=CCAR[:, l, fc, :], in_=A[:, T:T + 2]), waits=[t2])
                    st["A_free"][ai] = [t3, tc1]
                    ts = cx.op("act", lambda e, tb=tb, sg=sg: e.activation(out=sg[:, 0:T], in_=tb[:, 0:T], func=AF.Silu),
                               waits=[t3] + st["sg_free"][fc % len(sgbuf)])
                    st["tb_free"][fc % len(tbuf)] = [ts]
                    tm = cx.op("dve", lambda e, sg=sg, bu=bu, fc=fc: e.tensor_tensor(out=actT[:, fc, :], in0=ps[bu][:, 0:T], in1=sg[:, 0:T], op=ALU.mult),
                               waits=[ts, tu])
                    bank.rel(bu, tm)
                    st["sg_free"][fc % len(sgbuf)] = [tm]
                    act_ticks.append(tm)
                wr.rel(gs, tl)
                wr.rel(us, tl)
            if l == 0 and not st.get("dumped"):
                st["dumped"] = True
                dump("hnT2", hnT[:], [128, 8, T], hn_ticks)
                dump("actT", actT[:], [128, 16, T], act_ticks)
            res_ticks = [[] for _ in range(NSUB)]
            tlast_pe = None
            for n in range(2):
                bks = [bank.get() for _ in range(NSUB)]
                for kh in range(2):
                    dsl, dt_, dtk = wr.load(ci_d + n * 2 + kh)
                    tl = None
                    for j in range(NSUB):
                        bj, wj = bks[j]
                        for k in range(8):
                            tl = cx.op("pe", lambda e, k=k, j=j, bj=bj, dt_=dt_, kh=kh: e.matmul(
                                ps[bj][:, :], lhsT=actT[:, kh * 8 + k, j * 128:(j + 1) * 128], rhs=dt_[:, k, :],
                                start=(kh == 0 and k == 0), stop=(kh == 1 and k == 7)),
                                waits=[dtk] + wj + act_ticks[:(kh + 1) * 8][-8:], notick=not (k == 7))
                            if k == 7 and kh == 1:
                                tr = cx.op("dve", lambda e, j=j, bj=bj, n=n: e.tensor_tensor(
                                    out=hbuf[:, j, n * 512:(n + 1) * 512], in0=ps[bj][:, :], in1=hbuf[:, j, n * 512:(n + 1) * 512], op=ALU.add),
                                    waits=[tl])
                                bank.rel(bj, tr)
                                res_ticks[j].append(tr)
                                if n == 1 and on_res is not None:
                                    on_res(j, res_ticks[j])
                    wr.rel(dsl, tl)
                    tlast_pe = tl
            st["actT_free"] = [tlast_pe]
            return res_ticks

        def new_ffn_state(stack, tag):
            stt = {
                "actT": sb("actT" + tag, [128, 16, T], BF16, stack),
                "A": [sb("A%d%s" % (i, tag), [128, T + 2], F32, stack) for i in range(3)],
                "tb": [sb("tb%d%s" % (i, tag), [128, T], F32, stack) for i in range(2)],
                "sg": [sb("sg%d%s" % (i, tag), [128, T], F32, stack) for i in range(2)],
            }
            stt["A_free"] = [[] for _ in range(3)]
            stt["tb_free"] = [[] for _ in range(2)]
            stt["sg_free"] = [[] for _ in range(2)]
            stt["actT_free"] = []
            return stt

        st_h1 = cx.dsem("st_h1")
        st_q = cx.dsem("st_q")
        st_k = cx.dsem("st_k")
        st_v = cx.dsem("st_v")
        ld_x = cx.dsem("ld_x")
        with ExitStack() as p1:
            wtiles = [sb("wr%d" % i, [128, 8, 512], BF16, p1) for i in range(4)]
            wr = WRing(wtiles, wsems)
            hbufs = [sb("hbuf%d" % i, [128, NSUB, D], F32, p1) for i in range(2)]
            ns = new_norm_state(p1, "a")
            hnT = sb("hnT", [128, 8, T], BF16, p1)
            uT = sb("uT", [128, 16, T], BF16, p1)
            vb = sb("vb", [128, NSUB, E], BF16, p1)
            gT = sb("gT", [128, 16, T], BF16, p1)
            tmpg = [sb("tmpg%d" % i, [128, 512], F32, p1) for i in range(2)]
            lpb = sb("lpb", [128, NSUB, 2, NH], F32, p1)
            fst = new_ffn_state(p1, "a")
            bank = Banks(ps[:6] if USE_PT2 else ps)
            tmpg_free = [[], []]
            tmpg_i = [0]
            h_frees = [[], []]
            tx_next = None
            uT_free = []
            vb_free = []
            hnT_free = []
            carc_t = [t_m4]
            gT_free = []

            for stile in range(NST):
                t0 = stile * T
                hbuf = hbufs[stile % 2]

                def load_x(st_):
                    hb_ = hbufs[st_ % 2]
                    return cx.op("sp", lambda e, hb_=hb_, st_=st_: e.dma_start(out=hb_[:], in_=x[st_ * T:(st_ + 1) * T, :].rearrange("(j p) d -> p j d", p=128)),
                                 waits=h_frees[st_ % 2], dma_sem=ld_x)
                tx = load_x(0) if stile == 0 else tx_next
                if stile + 1 < NST:
                    tx_next = load_x(stile + 1)
                hn_ticks = norm_group(hbuf, 0, hnT, [[tx] + hnT_free for _ in range(NSUB)], ns)
                if stile == 0:
                    dump("hnT", hnT[:], [128, 8, T], hn_ticks)
                vsum = [stat(4) for _ in range(NSUB)]
                v_ticks = [[] for _ in range(NSUB)]
                tl = None
                for n in range(4, 8):
                    sl, wt, wtk = wr.load(CI_WIN + n)
                    for j in range(NSUB):
                        b, wb_ = bank.get()
                        for k in range(8):
                            tl = cx.op("pe", lambda e, k=k, j=j, b=b, wt=wt: e.matmul(ps[b][:, :], lhsT=hnT[:, k, j * 128:(j + 1) * 128], rhs=wt[:, k, :],
                                                                                   start=(k == 0), stop=(k == 7)),
                                       waits=[wtk, hn_ticks[j]] + wb_, notick=(k < 7))
                        tg_ = cx.op("act", lambda e, j=j, b=b, n=n: e.activation(out=vb[:, j, (n - 4) * 512:(n - 3) * 512], in_=ps[b][:, :], func=AF.Gelu_apprx_tanh,
                                                                             accum_out=vsum[j][:, n - 4:n - 3]), waits=[tl] + vb_free)
                        bank.rel(b, tg_)
                        v_ticks[j].append(tg_)
                    wr.rel(sl, tl)
                vh_ticks = []
                for j in range(NSUB):
                    ssq = stat()
                    tot = stat()
                    mean = stat()
                    msq = stat()
                    var = stat()
                    rr = stat()
                    rstd = stat()
                    nmr = stat()
                    ta = cx.op("act", lambda e, j=j, ssq=ssq: e.activation(out=junk[:, :], in_=vb[:, j, :], func=AF.Square, accum_out=ssq),
                               waits=v_ticks[j])
                    tb_ = cx.op("dve", lambda e, j=j, tot=tot: e.tensor_reduce(out=tot, in_=vsum[j], axis=AX.X, op=ALU.add), waits=v_ticks[j])
                    tc = cx.op("dve", lambda e, tot=tot, mean=mean: e.tensor_scalar(out=mean, in0=tot, scalar1=1.0 / E, scalar2=None, op0=ALU.mult), waits=[tb_])
                    td = cx.op("dve", lambda e, mean=mean, msq=msq: e.tensor_tensor(out=msq, in0=mean, in1=mean, op=ALU.mult), waits=[tc])
                    te = cx.op("dve", lambda e, ssq=ssq, msq=msq, var=var: e.scalar_tensor_tensor(out=var, in0=ssq, scalar=1.0 / E, in1=msq, op0=ALU.mult, op1=ALU.subtract),
                               waits=[ta, td])
                    tf = cx.op("dve", lambda e, var=var: e.tensor_scalar(out=var, in0=var, scalar1=LN_EPS, scalar2=None, op0=ALU.add), waits=[te])
                    tg2 = cx.op("dve", lambda e, var=var, rr=rr: e.reciprocal(out=rr, in_=var), waits=[tf])
                    th = cx.op("act", lambda e, rr=rr, rstd=rstd: e.activation(out=rstd, in_=rr, func=AF.Sqrt), waits=[tg2])
                    ti = cx.op("dve", lambda e, mean=mean, rstd=rstd, nmr=nmr: e.scalar_tensor_tensor(out=nmr, in0=mean, scalar=-1.0, in1=rstd, op0=ALU.mult, op1=ALU.mult),
                               waits=[th, tc])
                    tj = cx.op("dve", lambda e, j=j, rstd=rstd, nmr=nmr: e.tensor_scalar(out=vb[:, j, :], in0=vb[:, j, :], scalar1=rstd, scalar2=nmr, op0=ALU.mult, op1=ALU.add),
                               waits=[ti, ta])
                    vh_ticks.append(tj)
                u_ticks = []
                g_tick = {}

                def spatial_group(q4):
                    for j in range(NSUB):
                        b, wb_ = bank.get()
                        tl_ = None
                        for c4 in range(4):
                            cidx = q4 * 4 + c4
                            tl_ = cx.op("pe", lambda e, j=j, cidx=cidx, c4=c4, b=b: e.matmul(
                                ps[b][:, c4 * 128:(c4 + 1) * 128], lhsT=vb[:, j, cidx * 128:(cidx + 1) * 128], rhs=WcT_bf[:, cidx // 2, :], start=True, stop=True),
                                waits=[vh_ticks[j]] + wb_, notick=(c4 < 3))
                        ti_ = tmpg_i[0] % 2
                        tmpg_i[0] += 1
                        tmp = tmpg[ti_]
                        tt = None
                        for c4 in range(4):
                            cidx = q4 * 4 + c4
                            tt = cx.op("dve", lambda e, cidx=cidx, c4=c4, b=b, tmp=tmp: e.scalar_tensor_tensor(
                                out=tmp[:, c4 * 128:(c4 + 1) * 128], in0=ps[b][:, c4 * 128:(c4 + 1) * 128], scalar=col_lng(cidx), in1=BiasT[:, cidx, :],
                                op0=ALU.mult, op1=ALU.add), waits=[tl_] + tmpg_free[ti_])
                        bank.rel(b, tt)
                        tm = cx.op("dve", lambda e, q4=q4, j=j, tmp=tmp: e.tensor_tensor(
                            out=gT[:, q4 * 4:(q4 + 1) * 4, j * 128:(j + 1) * 128],
                            in0=tmp[:, :].rearrange("p (c t) -> p c t", c=4), in1=uT[:, q4 * 4:(q4 + 1) * 4, j * 128:(j + 1) * 128], op=ALU.mult),
                            waits=[tt] + u_ticks[q4 * 4:(q4 + 1) * 4] + gT_free)
                        tmpg_free[ti_] = [tm]
                        g_tick[(j, q4)] = tm

                for n in range(4):
                    sl, wt, wtk = wr.load(CI_WIN + n)
                    for f in range(4):
                        fc = n * 4 + f
                        b, wb_ = bank.get()
                        for k in range(8):
                            tl = cx.op("pe", lambda e, k=k, f=f, b=b, wt=wt: e.matmul(ps[b][:, 0:T], lhsT=wt[:, k, f * 128:(f + 1) * 128], rhs=hnT[:, k, :],
                                                                                   start=(k == 0), stop=(k == 7)),
                                       waits=[wtk] + hn_ticks + wb_, notick=(k < 7))
                        tg_ = cx.op("act", lambda e, fc=fc, b=b: e.activation(out=uT[:, fc, :], in_=ps[b][:, 0:T], func=AF.Gelu_apprx_tanh),
                                    waits=[tl] + uT_free)
                        bank.rel(b, tg_)
                        u_ticks.append(tg_)
                    wr.rel(sl, tl)
                    if n >= 1:
                        spatial_group(n - 1)
                hnT_free = [tl]
                spatial_group(3)
                g_ticks = [g_tick[(j, q4)] for j in range(NSUB) for q4 in range(4)]
                if stile == 0:
                    dump("vhat", vb[:], [128, NSUB, E], vh_ticks)
                    dump("uT", uT[:], [128, 16, T], u_ticks)
                res0 = [[] for _ in range(NSUB)]
                n_s1, n_s2 = norm_group(hbuf, 16, hnT, None, ns)
                hn_f = [None] * NSUB
                n_cb = norm_cb(n_s1, n_s2, hn_f)
                for n in range(2):
                    bks = [bank.get() for _ in range(NSUB)]
                    for kh in range(2):
                        sl, wt, wtk = wr.load(CI_WOUT + n * 2 + kh)
                        for j in range(NSUB):
                            bj, wj = bks[j]
                            for k in range(8):
                                tl = cx.op("pe", lambda e, k=k, j=j, bj=bj, wt=wt, kh=kh: e.matmul(
                                    ps[bj][:, :], lhsT=gT[:, kh * 8 + k, j * 128:(j + 1) * 128], rhs=wt[:, k, :],
                                    start=(kh == 0 and k == 0), stop=(kh == 1 and k == 7)),
                                    waits=[wtk] + wj + g_ticks[j * 4:(j + 1) * 4], notick=(k < 7))
                            if kh == 1:
                                tr = cx.op("dve", lambda e, j=j, bj=bj, n=n, hbuf=hbuf: e.tensor_tensor(
                                    out=hbuf[:, j, n * 512:(n + 1) * 512], in0=ps[bj][:, :], in1=hbuf[:, j, n * 512:(n + 1) * 512], op=ALU.add), waits=[tl])
                                bank.rel(bj, tr)
                                res0[j].append(tr)
                                if n == 1:
                                    n_cb(j, res0[j])
                        wr.rel(sl, tl)
                gT_free = [tl]
                if stile == 0:
                    dump("gT", gT[:], [128, 16, T], g_ticks)
                    dump("hmix", hbufs[0][:], [128, NSUB, D], [t for r_ in res0 for t in r_])
                hn_f[NSUB - 1] = n_s2(NSUB - 1)
                hn_ticks = list(hn_f)
                m_s1, m_s2 = norm_group(hbuf, 8, hnT, None, ns)
                hn_m = [None] * NSUB
                res1 = ffn(0, CI_G0, CI_U0, CI_D0, hbuf, hnT, hn_ticks, wr, bank, fst, on_res=norm_cb(m_s1, m_s2, hn_m))
                hnT_free = [cx.last("pe")]
                allres = [t for r in res1 for t in r]
                th1 = cx.op("pool", lambda e, t0=t0, hbuf=hbuf: e.dma_start(out=h1s[t0:t0 + T, :].rearrange("(j p) d -> p j d", p=128), in_=hbuf[:]),
                            waits=allres, dma_sem=st_h1)
                hn_m[NSUB - 1] = m_s2(NSUB - 1)
                hn_ticks = list(hn_m)
                h_frees[stile % 2] = [th1] + hn_ticks
                qk_ticks = {0: [], 1: []}
                for n in range(4):
                    sl, wt, wtk = wr.load(CI_QKV + n)
                    for f in range(4):
                        fc = (n % 2) * 4 + f
                        b, wb_ = bank.get()
                        for k in range(8):
                            tl = cx.op("pe", lambda e, k=k, f=f, b=b, wt=wt: e.matmul(ps[b][:, 0:T], lhsT=wt[:, k, f * 128:(f + 1) * 128], rhs=hnT[:, k, :],
                                                                                   start=(k == 0), stop=(k == 7)),
                                       waits=[wtk] + hn_ticks + wb_, notick=(k < 7))
                        dst = uT[:, (n // 2) * 8 + fc, :]
                        tcp = cx.op("act", lambda e, dst=dst, b=b: e.activation(out=dst, in_=ps[b][:, 0:T], func=AF.Identity), waits=[tl] + g_ticks)
                        bank.rel(b, tcp)
                        qk_ticks[n // 2].append(tcp)
                    wr.rel(sl, tl)
                tq = cx.op("pool", lambda e, t0=t0: e.dma_start(out=qTs[:, t0:t0 + T].rearrange("(k p) t -> p k t", p=128), in_=uT[:, 0:8, :]),
                           waits=qk_ticks[0], dma_sem=st_q)
                tk = cx.op("pool", lambda e, t0=t0: e.dma_start(out=kTs[:, t0:t0 + T].rearrange("(k p) t -> p k t", p=128), in_=uT[:, 8:16, :]),
                           waits=qk_ticks[1], dma_sem=st_k)
                uT_free = [tq, tk]
                v_st_ticks = []
                for n in range(2):
                    sl, wt, wtk = wr.load(CI_QKV + 4 + n)
                    for j in range(NSUB):
                        b, wb_ = bank.get()
                        for k in range(8):
                            tl = cx.op("pe", lambda e, k=k, j=j, b=b, wt=wt: e.matmul(ps[b][:, :], lhsT=hnT[:, k, j * 128:(j + 1) * 128], rhs=wt[:, k, :],
                                                                                   start=(k == 0), stop=(k == 7)),
                                       waits=[wtk, hn_ticks[j]] + wb_, notick=(k < 7))
                        tcp = cx.op("act", lambda e, j=j, n=n, b=b: e.activation(out=vb[:, j, n * 512:(n + 1) * 512], in_=ps[b][:, :], func=AF.Identity),
                                    waits=[tl, cx.last("pe")])
                        bank.rel(b, tcp)
                        v_st_ticks.append(tcp)
                    wr.rel(sl, tl)
                tv = cx.op("pool", lambda e, t0=t0: e.dma_start(out=vs[t0:t0 + T, :].rearrange("(j p) d -> p j d", p=128), in_=vb[:, :, 0:D]),
                           waits=v_st_ticks, dma_sem=st_v)
                vb_free = [tv]
                for j in range(NSUB):
                    blk = stile * NSUB + j
                    b, wb_ = bank.get()
                    for k in range(8):
                        tl = cx.op("pe", lambda e, k=k, j=j, b=b: e.matmul(ps[b][:, 0:NH], lhsT=hnT[:, k, j * 128:(j + 1) * 128], rhs=wf_bf[:, k, :],
                                                                        start=(k == 0), stop=(k == 7)),
                                   waits=[hn_ticks[j], t_wf] + wb_, notick=(k < 7))
                    xf = lpb[:, j, 0, :]
                    lp = lpb[:, j, 1, :]
                    t1 = cx.op("dve", lambda e, b=b, xf=xf: e.tensor_tensor(out=xf, in0=ps[b][:, 0:NH], in1=bf_bc[:, :], op=ALU.add), waits=[tl, cx.last("pe")])
                    t2 = cx.op("act", lambda e, xf=xf: e.activation(out=xf, in_=xf, func=AF.Exp, scale=-1.0), waits=[t1])
                    t3 = cx.op("act", lambda e, xf=xf, lp=lp: e.activation(out=lp, in_=xf, func=AF.Ln, bias=1.0), waits=[t2])
                    t4 = None
                    cx.op("pe", lambda e, b=b, lp=lp: e.matmul(ps[b][:, 32:32 + NH], lhsT=tri_f[:, :], rhs=lp, start=True, stop=True), waits=[t3, t1], notick=True)
                    t4 = cx.op("pe", lambda e, b=b, lp=lp: e.matmul(ps[b][:, 64:64 + NH], lhsT=ones_f[:, :], rhs=lp, start=True, stop=True))
                    t5 = cx.op("dve", lambda e, b=b, blk=blk: e.tensor_tensor(out=CUM[:, blk, :], in0=CARC[:, blk, :], in1=ps[b][:, 32:32 + NH], op=ALU.subtract),
                               waits=[t4] + carc_t)
                    t6 = cx.op("dve", lambda e, b=b, blk=blk: e.tensor_tensor(out=CARC[:, blk + 1, :], in0=CARC[:, blk, :], in1=ps[b][:, 64:64 + NH], op=ALU.subtract),
                               waits=[t4] + carc_t)
                    carc_t = [t6]
                    bank.rel(b, t6)
                hnT_free = [cx.last("pe")]
            cx.flush()

        ld_k = [cx.dsem("ld_k%d" % i) for i in range(2)]
        ld_q = [cx.dsem("ld_q%d" % i) for i in range(2)]
        ld_v = [cx.dsem("ld_v%d" % i) for i in range(2)]
        st_a = [cx.dsem("st_a%d" % i) for i in range(4)]
        TQ = min(1024, S)
        NQ = S // TQ
        BW = min(512, TQ)
        NB = TQ // BW
        KBQ = TQ // 128
        NSB = 3
        with ExitStack() as p2:
            KT = [sb("KT%d" % i, [128, S], BF16, p2) for i in range(2)]
            QZ = [[sb("QZ%d_%d" % (i, hh), [128, S], BF16, p2) for hh in range(2)] for i in range(2)]
            VA = [sb("VA%d" % i, [128, NBLK, 128], BF16, p2) for i in range(2)]
            NPT = 4
            PT = [sb("PT%d" % i, [128, TQ], BF16, p2) for i in range(NPT)]
            biasH = [sb("biasH%d" % i, [128, NQ, NBLK], F32, p2) for i in range(2)]
            Osb = [sb("Osb%d" % i, [64, 512], F32, p2) for i in range(4)]
            Lsb = [sb("Lsb%d" % i, [64, 512], F32, p2) for i in range(4)]
            ast = [sb("ast%d" % i, [64, 512], BF16, p2) for i in range(4)]
            tmo = []
            tmv = []
            for i in range(2):
                tmv.append(cx.op("dve", lambda e, i=i: e.memset(VA[i][:, :, DH:128], 1.0)))
                cx.op("dve", lambda e, i=i: e.memset(QZ[i][0][64:128, :], 0.0))
                tmo.append(cx.op("dve", lambda e, i=i: e.memset(QZ[i][1][0:64, :], 0.0)))
            SB = [psbig[:, 2 * b_:2 * b_ + NB, :] for b_ in range(NSB)]
            SBf = [SB[b_].rearrange("p a b -> p (a b)") for b_ in range(NSB)]
            OBs = [psbig[:, 6, :], pT[:, :].bitcast(F32)]
            sfree = [[] for _ in range(NSB)]
            s_i = [0]
            pt_free = [[] for _ in range(NPT)]
            pt_i = [0]
            biasH_free = [[], []]
            osb_free = [[] for _ in range(4)]
            osb_i = [0]
            ofree = [[] for _ in range(NB)]
            ast_free = [[] for _ in range(4)]
            ast_i = [0]
            buf_free = [[], []]
            va_free = [[], []]
            pvq = []
            PV_LAG = 2
            last_pe_box = [None]
            for hp in range(NH // 2):
                bi = hp % 2
                tkl = cx.op("sp", lambda e, hp=hp, bi=bi: e.dma_start(out=KT[bi][:], in_=kTs[hp * 128:(hp + 1) * 128, :]), waits=buf_free[bi], dma_sem=ld_k[bi])
                cx.op("sp", lambda e, hp=hp, bi=bi: e.dma_start(out=QZ[bi][0][0:64, :], in_=qTs[hp * 128:hp * 128 + 64, :]),
                      waits=buf_free[bi] + [tmo[bi]], dma_sem=ld_q[bi])
                tql = cx.op("sp", lambda e, hp=hp, bi=bi: e.dma_start(out=QZ[bi][1][64:128, :], in_=qTs[hp * 128 + 64:hp * 128 + 128, :]),
                            waits=buf_free[bi] + [tmo[bi]], dma_sem=ld_q[bi])
                tvls = []
                for hh_ in range(2):
                    h_ = hp * 2 + hh_
                    tvls.append(cx.op("sp", lambda e, h_=h_: e.dma_start(
                        out=VA[h_ % 2][:, :, 0:DH], in_=vs[:, h_ * 64:(h_ + 1) * 64].rearrange("(b p) d -> p b d", p=128)),
                        waits=va_free[h_ % 2] + [tmv[h_ % 2]], dma_sem=ld_v[h_ % 2]))
                last_pe = None
                for hh in range(2):
                    h = hp * 2 + hh
                    vbuf = VA[h % 2]
                    tvl = tvls[hh]
                    bH = biasH[h % 2]
                    tbias = None
                    for qp in range(NQ):
                        nkb = (qp + 1) * KBQ
                        cref_blk = qp * KBQ + KBQ // 2
                        tbias = cx.op("dve", lambda e, bH=bH, h=h, nkb=nkb, qp=qp, cref_blk=cref_blk: e.tensor_scalar(
                            out=bH[:, qp, 0:nkb], in0=CUM[:, 0:nkb, h], scalar1=-1.0, scalar2=CARC[:, cref_blk, h:h + 1], op0=ALU.mult, op1=ALU.add),
                            waits=biasH_free[h % 2])
                    for qp in range(NQ):
                        q0 = qp * TQ
                        nkb = (qp + 1) * KBQ
                        bt = bH[:, qp, :]
                        lastj = [max(j for j in range(nkb) if max(j - qp * KBQ, 0) * 128 < (nb + 1) * BW) for nb in range(NB)]
                        exp_ticks = []
                        pv_last = [None] * NB
                        for j in range(nkb):
                            jj = j - qp * KBQ
                            c0 = max(jj, 0) * 128
                            sbi = s_i[0] % NSB
                            s_i[0] += 1
                            tqk = None
                            for nb in range(NB):
                                lo = max(c0, nb * BW)
                                hi = (nb + 1) * BW
                                if lo >= hi:
                                    continue
                                tqk = cx.op("pe", lambda e, sbi=sbi, nb=nb, lo=lo, hi=hi, j=j, bi=bi, hh=hh, q0=q0: e.matmul(
                                    SB[sbi][:, nb, lo - nb * BW:BW], lhsT=KT[bi][:, j * 128:(j + 1) * 128], rhs=QZ[bi][hh][:, q0 + lo:q0 + hi], start=True, stop=True),
                                    waits=[tkl, tql] + sfree[sbi], notick=(jj >= 0 or nb < NB - 1))
                            if jj >= 0:
                                nbm = c0 // BW
                                off = c0 - nbm * BW
                                tqk = cx.op("pe", lambda e, sbi=sbi, nbm=nbm, off=off: e.matmul(
                                    SB[sbi][:, nbm, off:off + 128], lhsT=ident_bf[:, :], rhs=maskneg[:, :], start=False, stop=True))
                            pi = pt_i[0] % NPT
                            pt_i[0] += 1
                            te = cx.op("act", lambda e, pi=pi, sbi=sbi, c0=c0, bt=bt, j=j: e.activation(
                                out=PT[pi][:, c0:TQ], in_=SBf[sbi][:, c0:TQ], func=AF.Exp, scale=DH ** -0.5, bias=bt[:, j:j + 1]),
                                waits=[tqk, tbias] + pt_free[pi])
                            sfree[sbi] = [te]
                            exp_ticks.append(te)

                            def do_pv(pi_=pi, te_=te, c0_=c0, j_=j, vbuf=vbuf, tvl=tvl, lastj=lastj, pv_last=pv_last):
                                tpv = None
                                for nb in range(NB):
                                    lo = max(c0_, nb * BW)
                                    hi = (nb + 1) * BW
                                    if lo >= hi:
                                        continue
                                    tpv = cx.op("pe", lambda e, nb=nb, lo=lo, hi=hi, st_=(j_ == lastj[nb]): e.matmul(
                                        OBs[nb][:, lo - nb * BW:BW], lhsT=vbuf[:, j_, :], rhs=PT[pi_][:, lo:hi], start=(j_ == 0), stop=st_),
                                        waits=[te_, tvl] + (ofree[nb] if j_ == 0 else []))
                                    pv_last[nb] = tpv
                                pt_free[pi_] = [tpv]
                                last_pe_box[0] = tpv
                            pvq.append(do_pv)
                            while len(pvq) > PV_LAG:
                                pvq.pop(0)()
                        if qp == NQ - 1:
                            biasH_free[h % 2] = [exp_ticks[-1]]

                        def do_norm(h=h, q0=q0, pv_last=pv_last):
                            for nb in range(NB):
                                oi = osb_i[0] % 4
                                osb_i[0] += 1
                                tc1 = cx.op("dve", lambda e, nb=nb, oi=oi: e.tensor_copy(out=Osb[oi][:, 0:BW], in_=OBs[nb][0:64, 0:BW]),
                                            waits=[pv_last[nb]] + osb_free[oi])
                                tc2 = cx.op("dve", lambda e, nb=nb, oi=oi: e.tensor_copy(out=Lsb[oi][:, 0:BW], in_=OBs[nb][64:128, 0:BW]),
                                            waits=[pv_last[nb]] + osb_free[oi])
                                ofree[nb] = [tc1, tc2]
                                trc = cx.op("dve", lambda e, oi=oi: e.reciprocal(out=Lsb[oi][:, 0:BW], in_=Lsb[oi][:, 0:BW]), waits=[tc2])
                                ai = ast_i[0] % 4
                                ast_i[0] += 1
                                tmu = cx.op("dve", lambda e, ai=ai, oi=oi: e.tensor_tensor(out=ast[ai][:, 0:BW], in0=Osb[oi][:, 0:BW], in1=Lsb[oi][:, 0:BW], op=ALU.mult),
                                            waits=[trc, tc1] + ast_free[ai])
                                osb_free[oi] = [tmu]
                                tst = cx.op("pool", lambda e, ai=ai, nb=nb: e.dma_start(out=aTs[h * 64:(h + 1) * 64, q0 + nb * BW:q0 + (nb + 1) * BW], in_=ast[ai][:, 0:BW]),
                                            waits=[tmu], dma_sem=st_a[ai])
                                ast_free[ai] = [tst]
                        pvq.append(do_norm)
                    while pvq:
                        pvq.pop(0)()
                    last_pe = last_pe_box[0]
                    va_free[h % 2] = [last_pe]
                buf_free[bi] = [last_pe]
            cx.flush()

        ld_h = cx.dsem("ld_h")
        ld_a = cx.dsem("ld_a")
        st_y = cx.dsem("st_y")
        with ExitStack() as p3:
            wtiles = [sb("wq%d" % i, [128, 8, 512], BF16, p3) for i in range(4)]
            wr = WRing(wtiles, wsems)
            hbufs = [sb("hbuf3%d" % i, [128, NSUB, D], F32, p3) for i in range(2)]
            aT = sb("aT", [128, 8, T], BF16, p3)
            ns = new_norm_state(p3, "b")
            hnT = sb("hnT3", [128, 8, T], BF16, p3)
            fst = new_ffn_state(p3, "b")
            bank = Banks(ps[:6] if USE_PT2 else ps)
            h_frees = [[], []]
            aT_free = []
            hnT_free = []
            for stile in range(NST):
                t0 = stile * T
                hbuf = hbufs[stile % 2]
                th = cx.op("sp", lambda e, t0=t0, hbuf=hbuf: e.dma_start(out=hbuf[:], in_=h1s[t0:t0 + T, :].rearrange("(j p) d -> p j d", p=128)),
                           waits=h_frees[stile % 2], dma_sem=ld_h)
                ta_ = cx.op("sp", lambda e, t0=t0: e.dma_start(out=aT[:], in_=aTs[:, t0:t0 + T].rearrange("(k p) t -> p k t", p=128)),
                            waits=aT_free, dma_sem=ld_a)
                res0 = [[] for _ in range(NSUB)]
                tl = None
                n_s1, n_s2 = norm_group(hbuf, 24, hnT, None, ns)
                hn_f = [None] * NSUB
                n_cb = norm_cb(n_s1, n_s2, hn_f)
                for n in range(2):
                    sl, wt, wtk = wr.load(CI_WO + n)
                    for j in range(NSUB):
                        b, wb_ = bank.get()
                        for k in range(8):
                            tl = cx.op("pe", lambda e, k=k, j=j, b=b, wt=wt: e.matmul(ps[b][:, :], lhsT=aT[:, k, j * 128:(j + 1) * 128], rhs=wt[:, k, :],
                                                                                   start=(k == 0), stop=(k == 7)),
                                       waits=[wtk, ta_] + wb_, notick=(k < 7))
                        tr = cx.op("dve", lambda e, j=j, b=b, n=n, hbuf=hbuf: e.tensor_tensor(
                            out=hbuf[:, j, n * 512:(n + 1) * 512], in0=ps[b][:, :], in1=hbuf[:, j, n * 512:(n + 1) * 512], op=ALU.add), waits=[tl, th])
                        bank.rel(b, tr)
                        res0[j].append(tr)
                        if n == 1:
                            n_cb(j, res0[j])
                    wr.rel(sl, tl)
                aT_free = [tl]
                hn_f[NSUB - 1] = n_s2(NSUB - 1)
                hn_ticks = list(hn_f)
                outs = []

                def final_norm(j, rj, hbuf=hbuf, outs=outs):
                    res1 = {j: rj}
                    ss = stat()
                    ms = stat()
                    rr = stat()
                    rstd = stat()
                    hsl = hbuf[:, j, :]
                    t1 = cx.op("act", lambda e, hsl=hsl, ss=ss: e.activation(out=junk[:, 0:1024], in_=hsl, func=AF.Square, accum_out=ss),
                               waits=res1[j])
                    t2 = cx.op("dve", lambda e, ss=ss, ms=ms: e.tensor_scalar(out=ms, in0=ss, scalar1=1.0 / D, scalar2=RMS_EPS, op0=ALU.mult, op1=ALU.add), waits=[t1])
                    t3 = cx.op("dve", lambda e, ms=ms, rr=rr: e.reciprocal(out=rr, in_=ms), waits=[t2])
                    t4 = cx.op("act", lambda e, rr=rr, rstd=rstd: e.activation(out=rstd, in_=rr, func=AF.Sqrt), waits=[t3])
                    t5 = cx.op("dve", lambda e, hsl=hsl, rstd=rstd: e.scalar_tensor_tensor(out=hsl, in0=hsl, scalar=rstd, in1=gfin_bc[:, :], op0=ALU.mult, op1=ALU.mult),
                               waits=[t4, t1] + res1[j])
                    outs.append(t5)
                res1 = ffn(1, CI_G1, CI_U1, CI_D1, hbuf, hnT, hn_ticks, wr, bank, fst, on_res=final_norm)
                hnT_free = [cx.last("pe")]
                ty = cx.op("pool", lambda e, t0=t0, hbuf=hbuf: e.dma_start(out=y[t0:t0 + T, :].rearrange("(j p) d -> p j d", p=128), in_=hbuf[:]),
                           waits=list(outs), dma_sem=st_y)
                h_frees[stile % 2] = [ty]
            cx.flush()
    return nc


def _consts():
    bf = ml_dtypes.bfloat16
    ident = np.eye(128, dtype=np.float32)
    s = np.arange(128)[:, None]
    t = np.arange(128)[None, :]
    tri = (s <= t).astype(np.float32)
    maskneg = np.where(t >= s, 0.0, NEG).astype(np.float32)
    return {
        "c_ident_bf": ident.astype(bf), "c_ident_f": ident, "c_tri_f": tri, "c_maskneg": maskneg.astype(bf),
    }


def make_in_maps(inputs, S, n_cores):
    c = _consts()
    f = lambda a: np.ascontiguousarray(np.asarray(a, dtype=np.float32))
    shared = {
        "mix_norm_g": f(inputs["mix_norm_g"]), "ffn_norm_g": f(inputs["ffn_norm_g"]),
        "gm_w_in": f(inputs["gm_w_in"][0]), "gm_ln_g": f(inputs["gm_ln_g"][0]), "gm_ln_b": f(inputs["gm_ln_b"][0]),
        "gm_w_s": f(inputs["gm_w_s"][0]), "gm_b_s": f(inputs["gm_b_s"][0]), "gm_w_out": f(inputs["gm_w_out"][0]),
        "fox_w_qkvf": f(inputs["fox_w_qkvf"][0]), "fox_b_f": f(inputs["fox_b_f"][0]), "fox_w_o": f(inputs["fox_w_o"][0]),
        "ffn_w_gate": f(inputs["ffn_w_gate"]), "ffn_w_up": f(inputs["ffn_w_up"]), "ffn_conv_w": f(inputs["ffn_conv_w"]),
        "ffn_conv_b": f(inputs["ffn_conv_b"]), "ffn_w_down": f(inputs["ffn_w_down"]), "final_norm_g": f(inputs["final_norm_g"]),
    }
    shared.update(c)
    xs = np.asarray(inputs["x"], dtype=np.float32)
    maps = []
    for i in range(n_cores):
        m = dict(shared)
        m["x"] = np.ascontiguousarray(xs[i, :S])
        maps.append(m)
    return maps


_NC_CACHE = {}


def kernel(**inputs):
    xs = np.asarray(inputs["x"])
    B, S, _ = xs.shape
    if S not in _NC_CACHE:
        _NC_CACHE[S] = build(S)
    nc = _NC_CACHE[S]
    maps = make_in_maps(inputs, S, B)
    res = run_bass_kernel_spmd(nc, maps, core_ids=list(range(B)))
    out = np.stack([np.asarray(r["y"], dtype=np.float32) for r in res.results], axis=0)
    return out
```

```python
from contextlib import ExitStack
import numpy as np
import ml_dtypes
import concourse.bass as bass
import concourse.mybir as mybir
from concourse.bass_utils import run_bass_kernel_spmd

F32 = mybir.dt.float32
BF16 = mybir.dt.bfloat16
AF = mybir.ActivationFunctionType
ALU = mybir.AluOpType
AX = mybir.AxisListType

D = 1024
E = 2048
FF = 2048
NH = 16
DH = 64
RMS_EPS = 1e-6
LN_EPS = 1e-5
NEG = -30000.0
USE_PT2 = True
USE_BCAST = True


class Sem:
    def __init__(self, h, name):
        self.h = h
        self.name = name
        self.n = 0


class Ctx:
    CE = ("pe", "act", "dve", "pool")
    ENG = ("pe", "act", "dve", "pool", "sp")

    def __init__(self, nc, es):
        self.nc = nc
        self.es = es
        self.esem = {e: Sem(es.enter_context(nc.semaphore("e_" + e)), "e_" + e) for e in self.CE}
        self.tickn = {e: 0 for e in self.CE}
        self.base = {e: 0 for e in self.CE}
        self.semcount = {e: 0 for e in self.CE}
        self.ops = {e: [] for e in self.ENG}
        self.waited = {e: {} for e in self.ENG}
        self.needed = {e: set() for e in self.CE}
        self.dsems = []
        self.dbase = {}

    def dsem(self, name, barrier=True):
        s = Sem(self.es.enter_context(self.nc.semaphore(name)), name)
        if barrier:
            self.dsems.append(s)
        self.dbase[name] = 0
        return s

    def op(self, eng, fn, waits=(), dma_sem=None, notick=False):
        wl = []
        for w in waits:
            if w is None:
                continue
            kind, a, v = w
            if kind == "e":
                if v <= self.base[a]:
                    continue
                key = a
            else:
                if v <= self.dbase[a.name]:
                    continue
                key = a.name
            if self.waited[eng].get(key, 0) >= v:
                continue
            self.waited[eng][key] = v
            wl.append(w)
            if kind == "e":
                self.needed[a].add(v)
        t = None
        if dma_sem is not None:
            dma_sem.n += 16
            t = ("d", dma_sem, dma_sem.n)
        elif eng in self.esem and not notick and fn is not None:
            self.tickn[eng] += 1
            t = ("e", eng, self.tickn[eng])
        self.ops[eng].append((fn, wl, t))
        return t

    def last(self, eng):
        return ("e", eng, self.tickn[eng])

    def flush(self):
        allw = [self.last(e) for e in self.CE] + [("d", s, s.n) for s in self.dsems]
        for e in self.ENG:
            self.op(e, None, allw)
        rank = {}
        for e in self.CE:
            ids = sorted(self.needed[e])
            rank[e] = {tid: self.semcount[e] + i + 1 for i, tid in enumerate(ids)}
            self.semcount[e] += len(ids)
        ops = self.ops
        esem = self.esem

        def mk(key):
            def body(e):
                for fn, wl, t in ops[key]:
                    for kind, a, v in wl:
                        if kind == "e":
                            e.wait_ge(esem[a].h, rank[a][v])
                        else:
                            e.wait_ge(a.h, v)
                    if fn is None:
                        continue
                    ins = fn(e)
                    if t is not None:
                        if t[0] == "d":
                            ins.then_inc(t[1].h, 16)
                        elif t[2] in rank[t[1]]:
                            ins.then_inc(esem[t[1]].h, 1)
            return body

        with self.nc.Block() as block:
            block.tensor(mk("pe"))
            block.scalar(mk("act"))
            block.vector(mk("dve"))
            block.gpsimd(mk("pool"))
            block.sync(mk("sp"))
        self.ops = {e: [] for e in self.ENG}
        self.waited = {e: {} for e in self.ENG}
        self.needed = {e: set() for e in self.CE}
        for e in self.CE:
            self.base[e] = self.tickn[e]
        for s in self.dsems:
            self.dbase[s.name] = s.n


class Banks:
    def __init__(self, tiles):
        self.tiles = tiles
        self.free = [[] for _ in tiles]
        self.i = 0

    def get(self):
        b = self.i % len(self.tiles)
        self.i += 1
        w = self.free[b]
        self.free[b] = []
        return b, w

    def rel(self, b, tick):
        self.free[b].append(tick)


def build(S, dbg=False):
    nc = bass.Bass("TRN2", target_bir_lowering=False)
    T = min(512, S)
    NSUB = T // 128
    NST = S // T
    NBLK = S // 128
    okind = "ExternalOutput" if dbg else "Internal"

    def din(name, shape, dt=F32):
        return nc.dram_tensor(name, list(shape), dt, kind="ExternalInput").ap()

    x = din("x", [S, D])
    mix_norm_g = din("mix_norm_g", [2, D])
    ffn_norm_g = din("ffn_norm_g", [2, D])
    gm_w_in = din("gm_w_in", [D, 2 * E])
    gm_ln_g = din("gm_ln_g", [E])
    gm_ln_b = din("gm_ln_b", [E])
    gm_w_s = din("gm_w_s", [8, 128, 128])
    gm_b_s = din("gm_b_s", [8, 128])
    gm_w_out = din("gm_w_out", [E, D])
    fox_w_qkvf = din("fox_w_qkvf", [D, 3 * D + NH])
    fox_b_f = din("fox_b_f", [NH])
    fox_w_o = din("fox_w_o", [D, D])
    ffn_w_gate = din("ffn_w_gate", [2, D, FF])
    ffn_w_up = din("ffn_w_up", [2, D, FF])
    ffn_conv_w = din("ffn_conv_w", [2, 3, FF])
    ffn_conv_b = din("ffn_conv_b", [2, FF])
    ffn_w_down = din("ffn_w_down", [2, FF, D])
    final_norm_g = din("final_norm_g", [D])
    c_ident_bf = din("c_ident_bf", [128, 128], BF16)
    c_ident_f = din("c_ident_f", [128, 128])
    c_tri_f = din("c_tri_f", [128, 128])
    c_maskneg = din("c_maskneg", [128, 128], BF16)
    y = nc.dram_tensor("y", [S, D], F32, kind="ExternalOutput").ap()

    wscr = nc.dram_tensor("wscr", [44, 128, 8, 512], BF16).ap()
    h1s = nc.dram_tensor("h1s", [S, D], F32, kind=okind).ap()
    qTs = nc.dram_tensor("qTs", [D, S], BF16, kind=okind).ap()
    kTs = nc.dram_tensor("kTs", [D, S], BF16, kind=okind).ap()
    vs = nc.dram_tensor("vs", [S, D], BF16, kind=okind).ap()
    aTs = nc.dram_tensor("aTs", [D, S], BF16, kind=okind).ap()

    chunks = []
    for n in range(8):
        chunks.append(gm_w_in[:, n * 512:(n + 1) * 512])
    for n in range(2):
        for kh in range(2):
            chunks.append(gm_w_out[kh * 1024:(kh + 1) * 1024, n * 512:(n + 1) * 512])
    for l in range(2):
        base = len(chunks)
        for n in range(4):
            chunks.append(ffn_w_gate[l, :, n * 512:(n + 1) * 512])
        for n in range(4):
            chunks.append(ffn_w_up[l, :, n * 512:(n + 1) * 512])
        for n in range(2):
            for kh in range(2):
                chunks.append(ffn_w_down[l, kh * 1024:(kh + 1) * 1024, n * 512:(n + 1) * 512])
        if l == 0:
            for n in range(6):
                chunks.append(fox_w_qkvf[:, n * 512:(n + 1) * 512])
            for n in range(2):
                chunks.append(fox_w_o[:, n * 512:(n + 1) * 512])
    assert len(chunks) == 44
    CI_WIN, CI_WOUT, CI_G0, CI_U0, CI_D0, CI_QKV, CI_WO, CI_G1, CI_U1, CI_D1 = 0, 8, 12, 16, 20, 24, 30, 32, 36, 40

    es = ExitStack()
    with es:
        cx = Ctx(nc, es)

        def sb(name, shape, dt, stack=es):
            return stack.enter_context(nc.sbuf_tensor(name, list(shape), dt))

        dsem_dbg = cx.dsem("dbgsem")

        def dump(name, ap, shape, waits):
            if not dbg:
                return None
            dt_ = nc.dram_tensor("dbg_" + name, list(shape), F32, kind="ExternalOutput").ap()
            return cx.op("pool", lambda e: e.dma_start(out=dt_, in_=ap), waits=waits, dma_sem=dsem_dbg)

        psbig = es.enter_context(nc.psum_tensor("psbig", [128, 7, 512], F32))
        ps = [psbig[:, i, :] for i in range(7)]
        pT = es.enter_context(nc.psum_tensor("pT", [128, 1024], BF16))

        ident_bf = sb("ident_bf", [128, 128], BF16)
        ident_f = sb("ident_f", [128, 128], F32)
        tri_f = sb("tri_f", [128, 128], F32)
        ones_f = sb("ones_f", [128, 128], F32)
        maskneg = sb("maskneg", [128, 128], BF16)
        COLS = sb("COLS", [128, 192], F32)
        WcT_bf = sb("WcT_bf", [128, 8, 128], BF16)
        BiasT = sb("BiasT", [128, 16, 128], F32)
        bf_bc = sb("bf_bc", [128, NH], F32)
        gfin_bc = sb("gfin_bc", [128, D], F32)
        wf_bf = sb("wf_bf", [128, 8, NH], BF16)
        CUM = sb("CUM", [128, NBLK, NH], F32)
        CARC = sb("CARC", [128, NBLK + 1, NH], F32)
        NSTAT = 3072
        STAT = sb("STAT", [128, NSTAT], F32)
        CCAR = sb("CCAR", [128, 2, 16, 2], F32)
        junk = sb("junk", [128, 2048], BF16)

        stat_i = [0]

        def stat(n=1):
            i = stat_i[0]
            stat_i[0] += n
            assert stat_i[0] <= NSTAT, "STAT overflow"
            return STAT[:, i:i + n]

        def col_mixg(l, k): return COLS[:, l * 8 + k: l * 8 + k + 1]
        def col_ffng(l, k): return COLS[:, 16 + l * 8 + k: 16 + l * 8 + k + 1]
        def col_lng(k): return COLS[:, 32 + k:33 + k]
        def col_lnb(k): return COLS[:, 48 + k:49 + k]
        def col_cb(l, k): return COLS[:, 64 + l * 16 + k: 65 + l * 16 + k]
        def col_cw(l, i, k): return COLS[:, 96 + (l * 3 + i) * 16 + k: 97 + (l * 3 + i) * 16 + k]

        cast_sems = [cx.dsem("cast%d" % i, barrier=False) for i in range(44)]
        ld0 = cx.dsem("ld0")
        ld_pool = cx.dsem("ld_pool")

        t_wf = cx.op("pool", lambda e: e.dma_start(
            out=wf_bf[:], in_=fox_w_qkvf[:, 3 * D:3 * D + NH].rearrange("(k p) n -> p k n", p=128)), dma_sem=ld_pool)
        cast_tick = [None] * 44
        cast_order = [4, 5, 6, 7, 0, 1, 2, 3] + list(range(8, 44))
        for i in cast_order:
            src = chunks[i]
            cast_tick[i] = cx.op("pool", lambda e, i=i, src=src: e.dma_start(
                out=wscr[i], in_=src.rearrange("(k p) n -> p k n", p=128)), dma_sem=cast_sems[i])

        with ExitStack() as p0:
            R1 = sb("R1", [128, 128], F32, p0)
            R2 = sb("R2", [64, 128], F32, p0)
            WS = sb("WS", [128, 8, 128], F32, p0)
            WcT_f = sb("WcT_f", [128, 8, 128], F32, p0)
            BSb = sb("BSb", [128, 1024], F32, p0)
            loads = [
                (ident_bf[:], c_ident_bf), (ident_f[:], c_ident_f), (tri_f[:], c_tri_f), (maskneg[:], c_maskneg),
                (R1[0:16, :], mix_norm_g.rearrange("l (k p) -> (l k) p", p=128)),
                (R1[16:32, :], ffn_norm_g.rearrange("l (k p) -> (l k) p", p=128)),
                (R1[32:48, :], gm_ln_g.rearrange("(k p) -> k p", p=128)),
                (R1[48:64, :], gm_ln_b.rearrange("(k p) -> k p", p=128)),
                (R1[64:96, :], ffn_conv_b.rearrange("l (k p) -> (l k) p", p=128)),
                (R1[96:128, :], ffn_conv_w.rearrange("l i (k p) -> (l i k) p", p=128)[0:32]),
                (R2[0:64, :], ffn_conv_w.rearrange("l i (k p) -> (l i k) p", p=128)[32:96]),
                (WS[:], gm_w_s.rearrange("g t s -> t g s")),
                (BSb[:], gm_b_s.rearrange("g t -> (g t)").partition_broadcast(128)),
                (bf_bc[:], fox_b_f.partition_broadcast(128)),
                (gfin_bc[:], final_norm_g.partition_broadcast(128)),
            ]
            for o, i_ in loads:
                cx.op("sp", lambda e, o=o, i_=i_: e.dma_start(out=o, in_=i_), dma_sem=ld0)
            ld0_all = ("d", ld0, ld0.n)
            t_m1 = cx.op("dve", lambda e: e.memset(ones_f[:], 1.0))
            t_m2 = cx.op("dve", lambda e: e.memset(STAT[:], 0.0))
            t_m3 = cx.op("dve", lambda e: e.memset(CCAR[:], 0.0))
            t_m4 = cx.op("dve", lambda e: e.memset(CARC[:, 0, :], 0.0))
            cx.op("pe", lambda e: e.matmul(ps[0][:, 0:128], lhsT=R1[:, :], rhs=ident_f[:, :], start=True, stop=True),
                  waits=[ld0_all], notick=True)
            t_pe = cx.op("pe", lambda e: e.matmul(ps[0][:, 128:192], lhsT=R2[0:64, :], rhs=ident_f[0:64, 0:64], start=True, stop=True))
            t_cols = cx.op("dve", lambda e: e.tensor_copy(out=COLS[:], in_=ps[0][:, 0:192]), waits=[t_pe])
            for g in range(8):
                b = 1 + (g % 2) * 2
                t_a = cx.op("pe", lambda e, g=g, b=b: e.matmul(ps[b][:, 0:128], lhsT=WS[:, g, :], rhs=ident_f[:, :], start=True, stop=True),
                            waits=[ld0_all, cx.last("dve")])
                t_b = cx.op("dve", lambda e, g=g, b=b: e.tensor_tensor(out=WcT_f[:, g, :], in0=ps[b][:, 0:128], in1=tri_f[:, :], op=ALU.mult), waits=[t_a])
                t_c = cx.op("dve", lambda e, g=g: e.tensor_copy(out=WcT_bf[:, g, :], in_=WcT_f[:, g, :]), waits=[t_b])
                t_d = cx.op("pe", lambda e, g=g, b=b: e.matmul(ps[b + 1][:, 0:128], lhsT=ones_f[:, :], rhs=WcT_f[:, g, :], start=True, stop=True),
                            waits=[t_b, t_m1])
                for half in range(2):
                    k = 2 * g + half
                    cx.op("dve", lambda e, g=g, b=b, k=k: e.scalar_tensor_tensor(
                        out=BiasT[:, k, :], in0=ps[b + 1][:, 0:128], scalar=col_lnb(k), in1=BSb[:, g * 128:(g + 1) * 128],
                        op0=ALU.mult, op1=ALU.add), waits=[t_d, t_cols])
            dump("COLS", COLS[:], [128, 192], [cx.last("dve")])
            dump("WcT", WcT_bf[:], [128, 8, 128], [cx.last("dve")])
            dump("BiasT", BiasT[:], [128, 16, 128], [cx.last("dve")])
            cx.flush()

        class WRing:
            def __init__(self, tiles, sems):
                self.tiles = tiles
                self.sems = sems
                self.free = [None] * len(tiles)
                self.i = 0

            def load(self, ci):
                slot = self.i % len(self.tiles)
                self.i += 1
                tile = self.tiles[slot]
                assert self.free[slot] != "PENDING", "weight ring slot reused before release"
                t = cx.op("sp", lambda e, tile=tile, ci=ci: e.dma_start(out=tile[:], in_=wscr[ci]),
                          waits=[self.free[slot], cast_tick[ci]], dma_sem=self.sems[slot])
                self.free[slot] = "PENDING"
                return slot, tile, t

            def rel(self, slot, tick):
                self.free[slot] = tick

        wsems = [cx.dsem("wslot%d" % i) for i in range(4)]

        if USE_PT2:
            pT2 = psbig[:, 6, :].bitcast(BF16)
            pTs = [pT[:, :], pT2]
        else:
            pTs = [pT[:, :], pT[:, :]]

        def norm_group(hbuf, g0, hnT, waits_per_j, ns):
            s1 = {}

            def stage1(j, ew_=None):
                hsl = hbuf[:, j, :]
                ss = stat(); ms = stat(); rr = stat(); rstd = stat()
                xi = ns["xi"] % len(ns["xs"])
                ns["xi"] += 1
                xsb = ns["xs"][xi]
                ew = list(waits_per_j[j]) if ew_ is None else list(ew_)
                t1 = cx.op("act", lambda e: e.activation(out=junk[:, 0:1024], in_=hsl, func=AF.Square, accum_out=ss), waits=ew)
                t2 = cx.op("dve", lambda e: e.tensor_scalar(out=ms, in0=ss, scalar1=1.0 / D, scalar2=RMS_EPS, op0=ALU.mult, op1=ALU.add), waits=[t1])
                t3 = cx.op("dve", lambda e: e.reciprocal(out=rr, in_=ms), waits=[t2])
                t4 = cx.op("act", lambda e: e.activation(out=rstd, in_=rr, func=AF.Sqrt), waits=[t3])
                t5 = cx.op("dve", lambda e: e.tensor_scalar(out=xsb[:], in0=hsl, scalar1=rstd, scalar2=None, op0=ALU.mult),
                           waits=[t4] + ns["xs_free"][xi] + ew)
                s1[j] = (xi, xsb, t5)

            def stage2(j, extra=()):
                xi, xsb, t5 = s1[j]
                pi = (ns["pi"] % 2) if USE_PT2 else 0
                ns["pi"] += 1
                pTb = pTs[pi]
                tp = None
                for k in range(8):
                    tp = cx.op("pe", lambda e, k=k: e.transpose(pTb[:, k * 128:(k + 1) * 128], xsb[:, k * 128:(k + 1) * 128], ident_bf[:]),
                               waits=[t5] + ns["pT_free"][pi], notick=(k < 7))
                ns["xs_free"][xi] = [tp]
                if USE_BCAST:
                    tl = cx.op("dve", lambda e: e.tensor_tensor(
                        out=hnT[:, :, j * 128:(j + 1) * 128], in0=pTb.rearrange("p (k t) -> p k t", k=8),
                        in1=COLS[:, g0:g0 + 8].unsqueeze(2).broadcast_to([128, 8, 128]), op=ALU.mult), waits=[tp] + list(extra))
                else:
                    for k in range(8):
                        tl = cx.op("dve", lambda e, k=k: e.tensor_scalar(out=hnT[:, k, j * 128:(j + 1) * 128], in0=pTb[:, k * 128:(k + 1) * 128],
                                                                        scalar1=COLS[:, g0 + k:g0 + k + 1], scalar2=None, op0=ALU.mult), waits=[tp])
                ns["pT_free"][pi] = [tl]
                return tl

            if waits_per_j is None:
                return stage1, stage2
            ticks = [None] * NSUB
            stage1(0)
            for j in range(NSUB):
                if j + 1 < NSUB:
                    stage1(j + 1)
                ticks[j] = stage2(j)
            return ticks

        def norm_cb(s1_, s2_, out):
            def cb(j, r):
                s1_(j, r)
                if j >= 1:
                    out[j - 1] = s2_(j - 1)
            return cb

        def new_norm_state(stack, tag):
            return {"xs": [sb("xs%d%s" % (i, tag), [128, D], BF16, stack) for i in range(4)], "xs_free": [[] for _ in range(4)],
                    "pT_free": [[], []], "xi": 0, "pi": 0}

        def ffn(l, ci_g, ci_u, ci_d, hbuf, hnT, hn_ticks, wr, bank, st, on_res=None):
            actT, Abuf, tbuf, sgbuf = st["actT"], st["A"], st["tb"], st["sg"]
            act_ticks = []
            for n in range(4):
                gs, gt, gtk = wr.load(ci_g + n)
                us, ut, utk = wr.load(ci_u + n)
                tl = None
                for f in range(4):
                    fc = n * 4 + f
                    bg, wg = bank.get()
                    tg = None
                    for k in range(8):
                        tg = cx.op("pe", lambda e, k=k, f=f, bg=bg, gt=gt: e.matmul(ps[bg][:, 0:T], lhsT=gt[:, k, f * 128:(f + 1) * 128], rhs=hnT[:, k, :],
                                                                                 start=(k == 0), stop=(k == 7)),
                                   waits=[gtk] + wg + hn_ticks + st["actT_free"], notick=(k < 7))
                    bu, wu = bank.get()
                    tu = None
                    for k in range(8):
                        tu = cx.op("pe", lambda e, k=k, f=f, bu=bu, ut=ut: e.matmul(ps[bu][:, 0:T], lhsT=ut[:, k, f * 128:(f + 1) * 128], rhs=hnT[:, k, :],
                                                                                 start=(k == 0), stop=(k == 7)),
                                   waits=[utk] + wu, notick=(k < 7))
                    tl = tu
                    ai = fc % len(Abuf)
                    A = Abuf[ai]
                    tb = tbuf[fc % len(tbuf)]
                    sg = sgbuf[fc % len(sgbuf)]
                    ta = cx.op("act", lambda e, A=A, bg=bg: e.activation(out=A[:, 2:T + 2], in_=ps[bg][:, 0:T], func=AF.Identity),
                               waits=[tg] + st["A_free"][ai])
                    bank.rel(bg, ta)
                    tc0 = cx.op("dve", lambda e, A=A, fc=fc: e.tensor_copy(out=A[:, 0:2], in_=CCAR[:, l, fc, :]), waits=st["A_free"][ai])
                    t1 = cx.op("dve", lambda e, A=A, tb=tb, fc=fc: e.tensor_scalar(out=tb[:, 0:T], in0=A[:, 2:T + 2], scalar1=col_cw(l, 2, fc), scalar2=col_cb(l, fc),
                                                                               op0=ALU.mult, op1=ALU.add), waits=[ta] + st["tb_free"][fc % len(tbuf)])
                    t2 = cx.op("dve", lambda e, A=A, tb=tb, fc=fc: e.scalar_tensor_tensor(out=tb[:, 0:T], in0=A[:, 1:T + 1], scalar=col_cw(l, 1, fc), in1=tb[:, 0:T],
                                                                                      op0=ALU.mult, op1=ALU.add), waits=[t1, tc0])
                    t3 = cx.op("dve", lambda e, A=A, tb=tb, fc=fc: e.scalar_tensor_tensor(out=tb[:, 0:T], in0=A[:, 0:T], scalar=col_cw(l, 0, fc), in1=tb[:, 0:T],
                                                                                      op0=ALU.mult, op1=ALU.add), waits=[t2])
                    tc1 = cx.op("dve", lambda e, A=A, fc=fc: e.tensor_copy(out=CCAR[:, l, fc, :], in_=A[:, T:T + 2]), waits=[t2])
                    st["A_free"][ai] = [t3, tc1]
                    ts = cx.op("act", lambda e, tb=tb, sg=sg: e.activation(out=sg[:, 0:T], in_=tb[:, 0:T], func=AF.Silu),
                               waits=[t3] + st["sg_free"][fc % len(sgbuf)])
                    st["tb_free"][fc % len(tbuf)] = [ts]
                    tm = cx.op("dve", lambda e, sg=sg, bu=bu, fc=fc: e.tensor_tensor(out=actT[:, fc, :], in0=ps[bu][:, 0:T], in1=sg[:, 0:T], op=ALU.mult),
                               waits=[ts, tu])
                    bank.rel(bu, tm)
                    st["sg_free"][fc % len(sgbuf)] = [tm]
                    act_ticks.append(tm)
                wr.rel(gs, tl)
                wr.rel(us, tl)
            if l == 0 and not st.get("dumped"):
                st["dumped"] = True
                dump("hnT2", hnT[:], [128, 8, T], hn_ticks)
                dump("actT", actT[:], [128, 16, T], act_ticks)
            res_ticks = [[] for _ in range(NSUB)]
            tlast_pe = None
            for n in range(2):
                bks = [bank.get() for _ in range(NSUB)]
                for kh in range(2):
                    dsl, dt_, dtk = wr.load(ci_d + n * 2 + kh)
                    tl = None
                    for j in range(NSUB):
                        bj, wj = bks[j]
                        for k in range(8):
                            tl = cx.op("pe", lambda e, k=k, j=j, bj=bj, dt_=dt_, kh=kh: e.matmul(
                                ps[bj][:, :], lhsT=actT[:, kh * 8 + k, j * 128:(j + 1) * 128], rhs=dt_[:, k, :],
                                start=(kh == 0 and k == 0), stop=(kh == 1 and k == 7)),
                                waits=[dtk] + wj + act_ticks[:(kh + 1) * 8][-8:], notick=not (k == 7))
                            if k == 7 and kh == 1:
                                tr = cx.op("dve", lambda e, j=j, bj=bj, n=n: e.tensor_tensor(
                                    out=hbuf[:, j, n * 512:(n + 1) * 512], in0=ps[bj][:, :], in1=hbuf[:, j, n * 512:(n + 1) * 512], op=ALU.add),
                                    waits=[tl])
                                bank.rel(bj, tr)
                                res_ticks[j].append(tr)
                                if n == 1 and on_res is not None:
                                    on_res(j, res_ticks[j])
                    wr.rel(dsl, tl)
                    tlast_pe = tl
            st["actT_free"] = [tlast_pe]
            return res_ticks

        def new_ffn_state(stack, tag):
            stt = {
                "actT": sb("actT" + tag, [128, 16, T], BF16, stack),
                "A": [sb("A%d%s" % (i, tag), [128, T + 2], F32, stack) for i in range(3)],
                "tb": [sb("tb%d%s" % (i, tag), [128, T], F32, stack) for i in range(2)],
                "sg": [sb("sg%d%s" % (i, tag), [128, T], F32, stack) for i in range(2)],
            }
            stt["A_free"] = [[] for _ in range(3)]
            stt["tb_free"] = [[] for _ in range(2)]
            stt["sg_free"] = [[] for _ in range(2)]
            stt["actT_free"] = []
            return stt

        st_h1 = cx.dsem("st_h1")
        st_q = cx.dsem("st_q")
        st_k = cx.dsem("st_k")
        st_v = cx.dsem("st_v")
        ld_x = cx.dsem("ld_x")
        with ExitStack() as p1:
            wtiles = [sb("wr%d" % i, [128, 8, 512], BF16, p1) for i in range(4)]
            wr = WRing(wtiles, wsems)
            hbufs = [sb("hbuf%d" % i, [128, NSUB, D], F32, p1) for i in range(2)]
            ns = new_norm_state(p1, "a")
            hnT = sb("hnT", [128, 8, T], BF16, p1)
            uT = sb("uT", [128, 16, T], BF16, p1)
            vb = sb("vb", [128, NSUB, E], BF16, p1)
            gT = sb("gT", [128, 16, T], BF16, p1)
            tmpg = [sb("tmpg%d" % i, [128, 512], F32, p1) for i in range(2)]
            lpb = sb("lpb", [128, NSUB, 2, NH], F32, p1)
            fst = new_ffn_state(p1, "a")
            bank = Banks(ps[:6] if USE_PT2 else ps)
            tmpg_free = [[], []]
            tmpg_i = [0]
            h_frees = [[], []]
            tx_next = None
            uT_free = []
            vb_free = []
            hnT_free = []
            carc_t = [t_m4]
            gT_free = []

            for stile in range(NST):
                t0 = stile * T
                hbuf = hbufs[stile % 2]

                def load_x(st_):
                    hb_ = hbufs[st_ % 2]
                    return cx.op("sp", lambda e, hb_=hb_, st_=st_: e.dma_start(out=hb_[:], in_=x[st_ * T:(st_ + 1) * T, :].rearrange("(j p) d -> p j d", p=128)),
                                 waits=h_frees[st_ % 2], dma_sem=ld_x)
                tx = load_x(0) if stile == 0 else tx_next
                if stile + 1 < NST:
                    tx_next = load_x(stile + 1)
                hn_ticks = norm_group(hbuf, 0, hnT, [[tx] + hnT_free for _ in range(NSUB)], ns)
                if stile == 0:
                    dump("hnT", hnT[:], [128, 8, T], hn_ticks)
                vsum = [stat(4) for _ in range(NSUB)]
                v_ticks = [[] for _ in range(NSUB)]
                tl = None
                for n in range(4, 8):
                    sl, wt, wtk = wr.load(CI_WIN + n)
                    for j in range(NSUB):
                        b, wb_ = bank.get()
                        for k in range(8):
                            tl = cx.op("pe", lambda e, k=k, j=j, b=b, wt=wt: e.matmul(ps[b][:, :], lhsT=hnT[:, k, j * 128:(j + 1) * 128], rhs=wt[:, k, :],
                                                                                   start=(k == 0), stop=(k == 7)),
                                       waits=[wtk, hn_ticks[j]] + wb_, notick=(k < 7))
                        tg_ = cx.op("act", lambda e, j=j, b=b, n=n: e.activation(out=vb[:, j, (n - 4) * 512:(n - 3) * 512], in_=ps[b][:, :], func=AF.Gelu_apprx_tanh,
                                                                             accum_out=vsum[j][:, n - 4:n - 3]), waits=[tl] + vb_free)
                        bank.rel(b, tg_)
                        v_ticks[j].append(tg_)
                    wr.rel(sl, tl)
                vh_ticks = []
                for j in range(NSUB):
                    ssq = stat()
                    tot = stat()
                    mean = stat()
                    msq = stat()
                    var = stat()
                    rr = stat()
                    rstd = stat()
                    nmr = stat()
                    ta = cx.op("act", lambda e, j=j, ssq=ssq: e.activation(out=junk[:, :], in_=vb[:, j, :], func=AF.Square, accum_out=ssq),
                               waits=v_ticks[j])
                    tb_ = cx.op("dve", lambda e, j=j, tot=tot: e.tensor_reduce(out=tot, in_=vsum[j], axis=AX.X, op=ALU.add), waits=v_ticks[j])
                    tc = cx.op("dve", lambda e, tot=tot, mean=mean: e.tensor_scalar(out=mean, in0=tot, scalar1=1.0 / E, scalar2=None, op0=ALU.mult), waits=[tb_])
                    td = cx.op("dve", lambda e, mean=mean, msq=msq: e.tensor_tensor(out=msq, in0=mean, in1=mean, op=ALU.mult), waits=[tc])
                    te = cx.op("dve", lambda e, ssq=ssq, msq=msq, var=var: e.scalar_tensor_tensor(out=var, in0=ssq, scalar=1.0 / E, in1=msq, op0=ALU.mult, op1=ALU.subtract),
                               waits=[ta, td])
                    tf = cx.op("dve", lambda e, var=var: e.tensor_scalar(out=var, in0=var, scalar1=LN_EPS, scalar2=None, op0=ALU.add), waits=[te])
                    tg2 = cx.op("dve", lambda e, var=var, rr=rr: e.reciprocal(out=rr, in_=var), waits=[tf])
                    th = cx.op("act", lambda e, rr=rr, rstd=rstd: e.activation(out=rstd, in_=rr, func=AF.Sqrt), waits=[tg2])
                    ti = cx.op("dve", lambda e, mean=mean, rstd=rstd, nmr=nmr: e.scalar_tensor_tensor(out=nmr, in0=mean, scalar=-1.0, in1=rstd, op0=ALU.mult, op1=ALU.mult),
                               waits=[th, tc])
                    tj = cx.op("dve", lambda e, j=j, rstd=rstd, nmr=nmr: e.tensor_scalar(out=vb[:, j, :], in0=vb[:, j, :], scalar1=rstd, scalar2=nmr, op0=ALU.mult, op1=ALU.add),
                               waits=[ti, ta])
                    vh_ticks.append(tj)
                u_ticks = []
                g_tick = {}

                def spatial_group(q4):
                    for j in range(NSUB):
                        b, wb_ = bank.get()
                        tl_ = None
                        for c4 in range(4):
                            cidx = q4 * 4 + c4
                            tl_ = cx.op("pe", lambda e, j=j, cidx=cidx, c4=c4, b=b: e.matmul(
                                ps[b][:, c4 * 128:(c4 + 1) * 128], lhsT=vb[:, j, cidx * 128:(cidx + 1) * 128], rhs=WcT_bf[:, cidx // 2, :], start=True, stop=True),
                                waits=[vh_ticks[j]] + wb_, notick=(c4 < 3))
                        ti_ = tmpg_i[0] % 2
                        tmpg_i[0] += 1
                        tmp = tmpg[ti_]
                        tt = None
                        for c4 in range(4):
                            cidx = q4 * 4 + c4
                            tt = cx.op("dve", lambda e, cidx=cidx, c4=c4, b=b, tmp=tmp: e.scalar_tensor_tensor(
                                out=tmp[:, c4 * 128:(c4 + 1) * 128], in0=ps[b][:, c4 * 128:(c4 + 1) * 128], scalar=col_lng(cidx), in1=BiasT[:, cidx, :],
                                op0=ALU.mult, op1=ALU.add), waits=[tl_] + tmpg_free[ti_])
                        bank.rel(b, tt)
                        tm = cx.op("dve", lambda e, q4=q4, j=j, tmp=tmp: e.tensor_tensor(
                            out=gT[:, q4 * 4:(q4 + 1) * 4, j * 128:(j + 1) * 128],
                            in0=tmp[:, :].rearrange("p (c t) -> p c t", c=4), in1=uT[:, q4 * 4:(q4 + 1) * 4, j * 128:(j + 1) * 128], op=ALU.mult),
                            waits=[tt] + u_ticks[q4 * 4:(q4 + 1) * 4] + gT_free)
                        tmpg_free[ti_] = [tm]
                        g_tick[(j, q4)] = tm

                for n in range(4):
                    sl, wt, wtk = wr.load(CI_WIN + n)
                    for f in range(4):
                        fc = n * 4 + f
                        b, wb_ = bank.get()
                        for k in range(8):
                            tl = cx.op("pe", lambda e, k=k, f=f, b=b, wt=wt: e.matmul(ps[b][:, 0:T], lhsT=wt[:, k, f * 128:(f + 1) * 128], rhs=hnT[:, k, :],
                                                                                   start=(k == 0), stop=(k == 7)),
                                       waits=[wtk] + hn_ticks + wb_, notick=(k < 7))
                        tg_ = cx.op("act", lambda e, fc=fc, b=b: e.activation(out=uT[:, fc, :], in_=ps[b][:, 0:T], func=AF.Gelu_apprx_tanh),
                                    waits=[tl] + uT_free)
                        bank.rel(b, tg_)
                        u_ticks.append(tg_)
                    wr.rel(sl, tl)
                    if n >= 1:
                        spatial_group(n - 1)
                hnT_free = [tl]
                spatial_group(3)
                g_ticks = [g_tick[(j, q4)] for j in range(NSUB) for q4 in range(4)]
                if stile == 0:
                    dump("vhat", vb[:], [128, NSUB, E], vh_ticks)
                    dump("uT", uT[:], [128, 16, T], u_ticks)
                res0 = [[] for _ in range(NSUB)]
                n_s1, n_s2 = norm_group(hbuf, 16, hnT, None, ns)
                hn_f = [None] * NSUB
                n_cb = norm_cb(n_s1, n_s2, hn_f)
                for n in range(2):
                    bks = [bank.get() for _ in range(NSUB)]
                    for kh in range(2):
                        sl, wt, wtk = wr.load(CI_WOUT + n * 2 + kh)
                        for j in range(NSUB):
                            bj, wj = bks[j]
                            for k in range(8):
                                tl = cx.op("pe", lambda e, k=k, j=j, bj=bj, wt=wt, kh=kh: e.matmul(
                                    ps[bj][:, :], lhsT=gT[:, kh * 8 + k, j * 128:(j + 1) * 128], rhs=wt[:, k, :],
                                    start=(kh == 0 and k == 0), stop=(kh == 1 and k == 7)),
                                    waits=[wtk] + wj + g_ticks[j * 4:(j + 1) * 4], notick=(k < 7))
                            if kh == 1:
                                tr = cx.op("dve", lambda e, j=j, bj=bj, n=n, hbuf=hbuf: e.tensor_tensor(
                                    out=hbuf[:, j, n * 512:(n + 1) * 512], in0=ps[bj][:, :], in1=hbuf[:, j, n * 512:(n + 1) * 512], op=ALU.add), waits=[tl])
                                bank.rel(bj, tr)
                                res0[j].append(tr)
                                if n == 1:
                                    n_cb(j, res0[j])
                        wr.rel(sl, tl)
                gT_free = [tl]
                if stile == 0:
                    dump("gT", gT[:], [128, 16, T], g_ticks)
                    dump("hmix", hbufs[0][:], [128, NSUB, D], [t for r_ in res0 for t in r_])
                hn_f[NSUB - 1] = n_s2(NSUB - 1)
                hn_ticks = list(hn_f)
                m_s1, m_s2 = norm_group(hbuf, 8, hnT, None, ns)
                hn_m = [None] * NSUB
                res1 = ffn(0, CI_G0, CI_U0, CI_D0, hbuf, hnT, hn_ticks, wr, bank, fst, on_res=norm_cb(m_s1, m_s2, hn_m))
                hnT_free = [cx.last("pe")]
                allres = [t for r in res1 for t in r]
                th1 = cx.op("pool", lambda e, t0=t0, hbuf=hbuf: e.dma_start(out=h1s[t0:t0 + T, :].rearrange("(j p) d -> p j d", p=128), in_=hbuf[:]),
                            waits=allres, dma_sem=st_h1)
                hn_m[NSUB - 1] = m_s2(NSUB - 1)
                hn_ticks = list(hn_m)
                h_frees[stile % 2] = [th1] + hn_ticks
                qk_ticks = {0: [], 1: []}
                for n in range(4):
                    sl, wt, wtk = wr.load(CI_QKV + n)
                    for f in range(4):
                        fc = (n % 2) * 4 + f
                        b, wb_ = bank.get()
                        for k in range(8):
                            tl = cx.op("pe", lambda e, k=k, f=f, b=b, wt=wt: e.matmul(ps[b][:, 0:T], lhsT=wt[:, k, f * 128:(f + 1) * 128], rhs=hnT[:, k, :],
                                                                                   start=(k == 0), stop=(k == 7)),
                                       waits=[wtk] + hn_ticks + wb_, notick=(k < 7))
                        dst = uT[:, (n // 2) * 8 + fc, :]
                        tcp = cx.op("act", lambda e, dst=dst, b=b: e.activation(out=dst, in_=ps[b][:, 0:T], func=AF.Identity), waits=[tl] + g_ticks)
                        bank.rel(b, tcp)
                        qk_ticks[n // 2].append(tcp)
                    wr.rel(sl, tl)
                tq = cx.op("pool", lambda e, t0=t0: e.dma_start(out=qTs[:, t0:t0 + T].rearrange("(k p) t -> p k t", p=128), in_=uT[:, 0:8, :]),
                           waits=qk_ticks[0], dma_sem=st_q)
                tk = cx.op("pool", lambda e, t0=t0: e.dma_start(out=kTs[:, t0:t0 + T].rearrange("(k p) t -> p k t", p=128), in_=uT[:, 8:16, :]),
                           waits=qk_ticks[1], dma_sem=st_k)
                uT_free = [tq, tk]
                v_st_ticks = []
                for n in range(2):
                    sl, wt, wtk = wr.load(CI_QKV + 4 + n)
                    for j in range(NSUB):
                        b, wb_ = bank.get()
                        for k in range(8):
                            tl = cx.op("pe", lambda e, k=k, j=j, b=b, wt=wt: e.matmul(ps[b][:, :], lhsT=hnT[:, k, j * 128:(j + 1) * 128], rhs=wt[:, k, :],
                                                                                   start=(k == 0), stop=(k == 7)),
                                       waits=[wtk, hn_ticks[j]] + wb_, notick=(k < 7))
                        tcp = cx.op("act", lambda e, j=j, n=n, b=b: e.activation(out=vb[:, j, n * 512:(n + 1) * 512], in_=ps[b][:, :], func=AF.Identity),
                                    waits=[tl, cx.last("pe")])
                        bank.rel(b, tcp)
                        v_st_ticks.append(tcp)
                    wr.rel(sl, tl)
                tv = cx.op("pool", lambda e, t0=t0: e.dma_start(out=vs[t0:t0 + T, :].rearrange("(j p) d -> p j d", p=128), in_=vb[:, :, 0:D]),
                           waits=v_st_ticks, dma_sem=st_v)
                vb_free = [tv]
                for j in range(NSUB):
                    blk = stile * NSUB + j
                    b, wb_ = bank.get()
                    for k in range(8):
                        tl = cx.op("pe", lambda e, k=k, j=j, b=b: e.matmul(ps[b][:, 0:NH], lhsT=hnT[:, k, j * 128:(j + 1) * 128], rhs=wf_bf[:, k, :],
                                                                        start=(k == 0), stop=(k == 7)),
                                   waits=[hn_ticks[j], t_wf] + wb_, notick=(k < 7))
                    xf = lpb[:, j, 0, :]
                    lp = lpb[:, j, 1, :]
                    t1 = cx.op("dve", lambda e, b=b, xf=xf: e.tensor_tensor(out=xf, in0=ps[b][:, 0:NH], in1=bf_bc[:, :], op=ALU.add), waits=[tl, cx.last("pe")])
                    t2 = cx.op("act", lambda e, xf=xf: e.activation(out=xf, in_=xf, func=AF.Exp, scale=-1.0), waits=[t1])
                    t3 = cx.op("act", lambda e, xf=xf, lp=lp: e.activation(out=lp, in_=xf, func=AF.Ln, bias=1.0), waits=[t2])
                    t4 = None
                    cx.op("pe", lambda e, b=b, lp=lp: e.matmul(ps[b][:, 32:32 + NH], lhsT=tri_f[:, :], rhs=lp, start=True, stop=True), waits=[t3, t1], notick=True)
                    t4 = cx.op("pe", lambda e, b=b, lp=lp: e.matmul(ps[b][:, 64:64 + NH], lhsT=ones_f[:, :], rhs=lp, start=True, stop=True))
                    t5 = cx.op("dve", lambda e, b=b, blk=blk: e.tensor_tensor(out=CUM[:, blk, :], in0=CARC[:, blk, :], in1=ps[b][:, 32:32 + NH], op=ALU.subtract),
                               waits=[t4] + carc_t)
                    t6 = cx.op("dve", lambda e, b=b, blk=blk: e.tensor_tensor(out=CARC[:, blk + 1, :], in0=CARC[:, blk, :], in1=ps[b][:, 64:64 + NH], op=ALU.subtract),
                               waits=[t4] + carc_t)
                    carc_t = [t6]
                    bank.rel(b, t6)
                hnT_free = [cx.last("pe")]
            cx.flush()

        ld_k = [cx.dsem("ld_k%d" % i) for i in range(2)]
        ld_q = [cx.dsem("ld_q%d" % i) for i in range(2)]
        ld_v = [cx.dsem("ld_v%d" % i) for i in range(2)]
        st_a = [cx.dsem("st_a%d" % i) for i in range(4)]
        TQ = min(1024, S)
        NQ = S // TQ
        BW = min(512, TQ)
        NB = TQ // BW
        KBQ = TQ // 128
        NSB = 3
        with ExitStack() as p2:
            KT = [sb("KT%d" % i, [128, S], BF16, p2) for i in range(2)]
            QZ = [[sb("QZ%d_%d" % (i, hh), [128, S], BF16, p2) for hh in range(2)] for i in range(2)]
            VA = [sb("VA%d" % i, [128, NBLK, 128], BF16, p2) for i in range(2)]
            NPT = 4
            PT = [sb("PT%d" % i, [128, TQ], BF16, p2) for i in range(NPT)]
            biasH = [sb("biasH%d" % i, [128, NQ, NBLK], F32, p2) for i in range(2)]
            Osb = [sb("Osb%d" % i, [64, 512], F32, p2) for i in range(4)]
            Lsb = [sb("Lsb%d" % i, [64, 512], F32, p2) for i in range(4)]
            ast = [sb("ast%d" % i, [64, 512], BF16, p2) for i in range(4)]
            tmo = []
            tmv = []
            for i in range(2):
                tmv.append(cx.op("dve", lambda e, i=i: e.memset(VA[i][:, :, DH:128], 1.0)))
                cx.op("dve", lambda e, i=i: e.memset(QZ[i][0][64:128, :], 0.0))
                tmo.append(cx.op("dve", lambda e, i=i: e.memset(QZ[i][1][0:64, :], 0.0)))
            SB = [psbig[:, 2 * b_:2 * b_ + NB, :] for b_ in range(NSB)]
            SBf = [SB[b_].rearrange("p a b -> p (a b)") for b_ in range(NSB)]
            OBs = [psbig[:, 6, :], pT[:, :].bitcast(F32)]
            sfree = [[] for _ in range(NSB)]
            s_i = [0]
            pt_free = [[] for _ in range(NPT)]
            pt_i = [0]
            biasH_free = [[], []]
            osb_free = [[] for _ in range(4)]
            osb_i = [0]
            ofree = [[] for _ in range(NB)]
            ast_free = [[] for _ in range(4)]
            ast_i = [0]
            buf_free = [[], []]
            va_free = [[], []]
            pvq = []
            PV_LAG = 2
            last_pe_box = [None]
            for hp in range(NH // 2):
                bi = hp % 2
                tkl = cx.op("sp", lambda e, hp=hp, bi=bi: e.dma_start(out=KT[bi][:], in_=kTs[hp * 128:(hp + 1) * 128, :]), waits=buf_free[bi], dma_sem=ld_k[bi])
                cx.op("sp", lambda e, hp=hp, bi=bi: e.dma_start(out=QZ[bi][0][0:64, :], in_=qTs[hp * 128:hp * 128 + 64, :]),
                      waits=buf_free[bi] + [tmo[bi]], dma_sem=ld_q[bi])
                tql = cx.op("sp", lambda e, hp=hp, bi=bi: e.dma_start(out=QZ[bi][1][64:128, :], in_=qTs[hp * 128 + 64:hp * 128 + 128, :]),
                            waits=buf_free[bi] + [tmo[bi]], dma_sem=ld_q[bi])
                tvls = []
                for hh_ in range(2):
                    h_ = hp * 2 + hh_
                    tvls.append(cx.op("sp", lambda e, h_=h_: e.dma_start(
                        out=VA[h_ % 2][:, :, 0:DH], in_=vs[:, h_ * 64:(h_ + 1) * 64].rearrange("(b p) d -> p b d", p=128)),
                        waits=va_free[h_ % 2] + [tmv[h_ % 2]], dma_sem=ld_v[h_ % 2]))
                last_pe = None
                for hh in range(2):
                    h = hp * 2 + hh
                    vbuf = VA[h % 2]
                    tvl = tvls[hh]
                    bH = biasH[h % 2]
                    tbias = None
                    for qp in range(NQ):
                        nkb = (qp + 1) * KBQ
                        cref_blk = qp * KBQ + KBQ // 2
                        tbias = cx.op("dve", lambda e, bH=bH, h=h, nkb=nkb, qp=qp, cref_blk=cref_blk: e.tensor_scalar(
                            out=bH[:, qp, 0:nkb], in0=CUM[:, 0:nkb, h], scalar1=-1.0, scalar2=CARC[:, cref_blk, h:h + 1], op0=ALU.mult, op1=ALU.add),
                            waits=biasH_free[h % 2])
                    for qp in range(NQ):
                        q0 = qp * TQ
                        nkb = (qp + 1) * KBQ
                        bt = bH[:, qp, :]
                        lastj = [max(j for j in range(nkb) if max(j - qp * KBQ, 0) * 128 < (nb + 1) * BW) for nb in range(NB)]
                        exp_ticks = []
                        pv_last = [None] * NB
                        for j in range(nkb):
                            jj = j - qp * KBQ
                            c0 = max(jj, 0) * 128
                            sbi = s_i[0] % NSB
                            s_i[0] += 1
                            tqk = None
                            for nb in range(NB):
                                lo = max(c0, nb * BW)
                                hi = (nb + 1) * BW
                                if lo >= hi:
                                    continue
                                tqk = cx.op("pe", lambda e, sbi=sbi, nb=nb, lo=lo, hi=hi, j=j, bi=bi, hh=hh, q0=q0: e.matmul(
                                    SB[sbi][:, nb, lo - nb * BW:BW], lhsT=KT[bi][:, j * 128:(j + 1) * 128], rhs=QZ[bi][hh][:, q0 + lo:q0 + hi], start=True, stop=True),
                                    waits=[tkl, tql] + sfree[sbi], notick=(jj >= 0 or nb < NB - 1))
                            if jj >= 0:
                                nbm = c0 // BW
                                off = c0 - nbm * BW
                                tqk = cx.op("pe", lambda e, sbi=sbi, nbm=nbm, off=off: e.matmul(
                                    SB[sbi][:, nbm, off:off + 128], lhsT=ident_bf[:, :], rhs=maskneg[:, :], start=False, stop=True))
                            pi = pt_i[0] % NPT
                            pt_i[0] += 1
                            te = cx.op("act", lambda e, pi=pi, sbi=sbi, c0=c0, bt=bt, j=j: e.activation(
                                out=PT[pi][:, c0:TQ], in_=SBf[sbi][:, c0:TQ], func=AF.Exp, scale=DH ** -0.5, bias=bt[:, j:j + 1]),
                                waits=[tqk, tbias] + pt_free[pi])
                            sfree[sbi] = [te]
                            exp_ticks.append(te)

                            def do_pv(pi_=pi, te_=te, c0_=c0, j_=j, vbuf=vbuf, tvl=tvl, lastj=lastj, pv_last=pv_last):
                                tpv = None
                                for nb in range(NB):
                                    lo = max(c0_, nb * BW)
                                    hi = (nb + 1) * BW
                                    if lo >= hi:
                                        continue
                                    tpv = cx.op("pe", lambda e, nb=nb, lo=lo, hi=hi, st_=(j_ == lastj[nb]): e.matmul(
                                        OBs[nb][:, lo - nb * BW:BW], lhsT=vbuf[:, j_, :], rhs=PT[pi_][:, lo:hi], start=(j_ == 0), stop=st_),
                                        waits=[te_, tvl] + (ofree[nb] if j_ == 0 else []))
                                    pv_last[nb] = tpv
                                pt_free[pi_] = [tpv]
                                last_pe_box[0] = tpv
                            pvq.append(do_pv)
                            while len(pvq) > PV_LAG:
                                pvq.pop(0)()
                        if qp == NQ - 1:
                            biasH_free[h % 2] = [exp_ticks[-1]]

                        def do_norm(h=h, q0=q0, pv_last=pv_last):
                            for nb in range(NB):
                                oi = osb_i[0] % 4
                                osb_i[0] += 1
                                tc1 = cx.op("dve", lambda e, nb=nb, oi=oi: e.tensor_copy(out=Osb[oi][:, 0:BW], in_=OBs[nb][0:64, 0:BW]),
                                            waits=[pv_last[nb]] + osb_free[oi])
                                tc2 = cx.op("dve", lambda e, nb=nb, oi=oi: e.tensor_copy(out=Lsb[oi][:, 0:BW], in_=OBs[nb][64:128, 0:BW]),
                                            waits=[pv_last[nb]] + osb_free[oi])
                                ofree[nb] = [tc1, tc2]
                                trc = cx.op("dve", lambda e, oi=oi: e.reciprocal(out=Lsb[oi][:, 0:BW], in_=Lsb[oi][:, 0:BW]), waits=[tc2])
                                ai = ast_i[0] % 4
                                ast_i[0] += 1
                                tmu = cx.op("dve", lambda e, ai=ai, oi=oi: e.tensor_tensor(out=ast[ai][:, 0:BW], in0=Osb[oi][:, 0:BW], in1=Lsb[oi][:, 0:BW], op=ALU.mult),
                                            waits=[trc, tc1] + ast_free[ai])
                                osb_free[oi] = [tmu]
                                tst = cx.op("pool", lambda e, ai=ai, nb=nb: e.dma_start(out=aTs[h * 64:(h + 1) * 64, q0 + nb * BW:q0 + (nb + 1) * BW], in_=ast[ai][:, 0:BW]),
                                            waits=[tmu], dma_sem=st_a[ai])
                                ast_free[ai] = [tst]
                        pvq.append(do_norm)
                    while pvq:
                        pvq.pop(0)()
                    last_pe = last_pe_box[0]
                    va_free[h % 2] = [last_pe]
                buf_free[bi] = [last_pe]
            cx.flush()

        ld_h = cx.dsem("ld_h")
        ld_a = cx.dsem("ld_a")
        st_y = cx.dsem("st_y")
        with ExitStack() as p3:
            wtiles = [sb("wq%d" % i, [128, 8, 512], BF16, p3) for i in range(4)]
            wr = WRing(wtiles, wsems)
            hbufs = [sb("hbuf3%d" % i, [128, NSUB, D], F32, p3) for i in range(2)]
            aT = sb("aT", [128, 8, T], BF16, p3)
            ns = new_norm_state(p3, "b")
            hnT = sb("hnT3", [128, 8, T], BF16, p3)
            fst = new_ffn_state(p3, "b")
            bank = Banks(ps[:6] if USE_PT2 else ps)
            h_frees = [[], []]
            aT_free = []
            hnT_free = []
            for stile in range(NST):
                t0 = stile * T
                hbuf = hbufs[stile % 2]
                th = cx.op("sp", lambda e, t0=t0, hbuf=hbuf: e.dma_start(out=hbuf[:], in_=h1s[t0:t0 + T, :].rearrange("(j p) d -> p j d", p=128)),
                           waits=h_frees[stile % 2], dma_sem=ld_h)
                ta_ = cx.op("sp", lambda e, t0=t0: e.dma_start(out=aT[:], in_=aTs[:, t0:t0 + T].rearrange("(k p) t -> p k t", p=128)),
                            waits=aT_free, dma_sem=ld_a)
                res0 = [[] for _ in range(NSUB)]
                tl = None
                n_s1, n_s2 = norm_group(hbuf, 24, hnT, None, ns)
                hn_f = [None] * NSUB
                n_cb = norm_cb(n_s1, n_s2, hn_f)
                for n in range(2):
                    sl, wt, wtk = wr.load(CI_WO + n)
                    for j in range(NSUB):
                        b, wb_ = bank.get()
                        for k in range(8):
                            tl = cx.op("pe", lambda e, k=k, j=j, b=b, wt=wt: e.matmul(ps[b][:, :], lhsT=aT[:, k, j * 128:(j + 1) * 128], rhs=wt[:, k, :],
                                                                                   start=(k == 0), stop=(k == 7)),
                                       waits=[wtk, ta_] + wb_, notick=(k < 7))
                        tr = cx.op("dve", lambda e, j=j, b=b, n=n, hbuf=hbuf: e.tensor_tensor(
                            out=hbuf[:, j, n * 512:(n + 1) * 512], in0=ps[b][:, :], in1=hbuf[:, j, n * 512:(n + 1) * 512], op=ALU.add), waits=[tl, th])
                        bank.rel(b, tr)
                        res0[j].append(tr)
                        if n == 1:
                            n_cb(j, res0[j])
                    wr.rel(sl, tl)
                aT_free = [tl]
                hn_f[NSUB - 1] = n_s2(NSUB - 1)
                hn_ticks = list(hn_f)
                outs = []

                def final_norm(j, rj, hbuf=hbuf, outs=outs):
                    res1 = {j: rj}
                    ss = stat()
                    ms = stat()
                    rr = stat()
                    rstd = stat()
                    hsl = hbuf[:, j, :]
                    t1 = cx.op("act", lambda e, hsl=hsl, ss=ss: e.activation(out=junk[:, 0:1024], in_=hsl, func=AF.Square, accum_out=ss),
                               waits=res1[j])
                    t2 = cx.op("dve", lambda e, ss=ss, ms=ms: e.tensor_scalar(out=ms, in0=ss, scalar1=1.0 / D, scalar2=RMS_EPS, op0=ALU.mult, op1=ALU.add), waits=[t1])
                    t3 = cx.op("dve", lambda e, ms=ms, rr=rr: e.reciprocal(out=rr, in_=ms), waits=[t2])
                    t4 = cx.op("act", lambda e, rr=rr, rstd=rstd: e.activation(out=rstd, in_=rr, func=AF.Sqrt), waits=[t3])
                    t5 = cx.op("dve", lambda e, hsl=hsl, rstd=rstd: e.scalar_tensor_tensor(out=hsl, in0=hsl, scalar=rstd, in1=gfin_bc[:, :], op0=ALU.mult, op1=ALU.mult),
                               waits=[t4, t1] + res1[j])
                    outs.append(t5)
                res1 = ffn(1, CI_G1, CI_U1, CI_D1, hbuf, hnT, hn_ticks, wr, bank, fst, on_res=final_norm)
                hnT_free = [cx.last("pe")]
                ty = cx.op("pool", lambda e, t0=t0, hbuf=hbuf: e.dma_start(out=y[t0:t0 + T, :].rearrange("(j p) d -> p j d", p=128), in_=hbuf[:]),
                           waits=list(outs), dma_sem=st_y)
                h_frees[stile % 2] = [ty]
            cx.flush()
    return nc


def _consts():
    bf = ml_dtypes.bfloat16
    ident = np.eye(128, dtype=np.float32)
    s = np.arange(128)[:, None]
    t = np.arange(128)[None, :]
    tri = (s <= t).astype(np.float32)
    maskneg = np.where(t >= s, 0.0, NEG).astype(np.float32)
    return {
        "c_ident_bf": ident.astype(bf), "c_ident_f": ident, "c_tri_f": tri, "c_maskneg": maskneg.astype(bf),
    }


def make_in_maps(inputs, S, n_cores):
    c = _consts()
    f = lambda a: np.ascontiguousarray(np.asarray(a, dtype=np.float32))
    shared = {
        "mix_norm_g": f(inputs["mix_norm_g"]), "ffn_norm_g": f(inputs["ffn_norm_g"]),
        "gm_w_in": f(inputs["gm_w_in"][0]), "gm_ln_g": f(inputs["gm_ln_g"][0]), "gm_ln_b": f(inputs["gm_ln_b"][0]),
        "gm_w_s": f(inputs["gm_w_s"][0]), "gm_b_s": f(inputs["gm_b_s"][0]), "gm_w_out": f(inputs["gm_w_out"][0]),
        "fox_w_qkvf": f(inputs["fox_w_qkvf"][0]), "fox_b_f": f(inputs["fox_b_f"][0]), "fox_w_o": f(inputs["fox_w_o"][0]),
        "ffn_w_gate": f(inputs["ffn_w_gate"]), "ffn_w_up": f(inputs["ffn_w_up"]), "ffn_conv_w": f(inputs["ffn_conv_w"]),
        "ffn_conv_b": f(inputs["ffn_conv_b"]), "ffn_w_down": f(inputs["ffn_w_down"]), "final_norm_g": f(inputs["final_norm_g"]),
    }
    shared.update(c)
    xs = np.asarray(inputs["x"], dtype=np.float32)
    maps = []
    for i in range(n_cores):
        m = dict(shared)
        m["x"] = np.ascontiguousarray(xs[i, :S])
        maps.append(m)
    return maps


_NC_CACHE = {}


def kernel(**inputs):
    xs = np.asarray(inputs["x"])
    B, S, _ = xs.shape
    if S not in _NC_CACHE:
        _NC_CACHE[S] = build(S)
    nc = _NC_CACHE[S]
    maps = make_in_maps(inputs, S, B)
    res = run_bass_kernel_spmd(nc, maps, core_ids=list(range(B)))
    out = np.stack([np.asarray(r["y"], dtype=np.float32) for r in res.results], axis=0)
    return out
```

```python
from contextlib import ExitStack
import numpy as np
import ml_dtypes
import concourse.bass as bass
import concourse.mybir as mybir
from concourse.bass_utils import run_bass_kernel_spmd

F32 = mybir.dt.float32
BF16 = mybir.dt.bfloat16
AF = mybir.ActivationFunctionType
ALU = mybir.AluOpType
AX = mybir.AxisListType

D = 1024
E = 2048
FF = 2048
NH = 16
DH = 64
RMS_EPS = 1e-6
LN_EPS = 1e-5
NEG = -30000.0
USE_PT2 = True
USE_BCAST = True


class Sem:
    def __init__(self, h, name):
        self.h = h
        self.name = name
        self.n = 0


class Ctx:
    CE = ("pe", "act", "dve", "pool")
    ENG = ("pe", "act", "dve", "pool", "sp")

    def __init__(self, nc, es):
        self.nc = nc
        self.es = es
        self.esem = {e: Sem(es.enter_context(nc.semaphore("e_" + e)), "e_" + e) for e in self.CE}
        self.tickn = {e: 0 for e in self.CE}
        self.base = {e: 0 for e in self.CE}
        self.semcount = {e: 0 for e in self.CE}
        self.ops = {e: [] for e in self.ENG}
        self.waited = {e: {} for e in self.ENG}
        self.needed = {e: set() for e in self.CE}
        self.dsems = []
        self.dbase = {}

    def dsem(self, name, barrier=True):
        s = Sem(self.es.enter_context(self.nc.semaphore(name)), name)
        if barrier:
            self.dsems.append(s)
        self.dbase[name] = 0
        return s

    def op(self, eng, fn, waits=(), dma_sem=None, notick=False):
        wl = []
        for w in waits:
            if w is None:
                continue
            kind, a, v = w
            if kind == "e":
                if v <= self.base[a]:
                    continue
                key = a
            else:
                if v <= self.dbase[a.name]:
                    continue
                key = a.name
            if self.waited[eng].get(key, 0) >= v:
                continue
            self.waited[eng][key] = v
            wl.append(w)
            if kind == "e":
                self.needed[a].add(v)
        t = None
        if dma_sem is not None:
            dma_sem.n += 16
            t = ("d", dma_sem, dma_sem.n)
        elif eng in self.esem and not notick and fn is not None:
            self.tickn[eng] += 1
            t = ("e", eng, self.tickn[eng])
        self.ops[eng].append((fn, wl, t))
        return t

    def last(self, eng):
        return ("e", eng, self.tickn[eng])

    def flush(self):
        allw = [self.last(e) for e in self.CE] + [("d", s, s.n) for s in self.dsems]
        for e in self.ENG:
            self.op(e, None, allw)
        rank = {}
        for e in self.CE:
            ids = sorted(self.needed[e])
            rank[e] = {tid: self.semcount[e] + i + 1 for i, tid in enumerate(ids)}
            self.semcount[e] += len(ids)
        ops = self.ops
        esem = self.esem

        def mk(key):
            def body(e):
                for fn, wl, t in ops[key]:
                    for kind, a, v in wl:
                        if kind == "e":
                            e.wait_ge(esem[a].h, rank[a][v])
                        else:
                            e.wait_ge(a.h, v)
                    if fn is None:
                        continue
                    ins = fn(e)
                    if t is not None:
                        if t[0] == "d":
                            ins.then_inc(t[1].h, 16)
                        elif t[2] in rank[t[1]]:
                            ins.then_inc(esem[t[1]].h, 1)
            return body

        with self.nc.Block() as block:
            block.tensor(mk("pe"))
            block.scalar(mk("act"))
            block.vector(mk("dve"))
            block.gpsimd(mk("pool"))
            block.sync(mk("sp"))
        self.ops = {e: [] for e in self.ENG}
        self.waited = {e: {} for e in self.ENG}
        self.needed = {e: set() for e in self.CE}
        for e in self.CE:
            self.base[e] = self.tickn[e]
        for s in self.dsems:
            self.dbase[s.name] = s.n


class Banks:
    def __init__(self, tiles):
        self.tiles = tiles
        self.free = [[] for _ in tiles]
        self.i = 0

    def get(self):
        b = self.i % len(self.tiles)
        self.i += 1
        w = self.free[b]
        self.free[b] = []
        return b, w

    def rel(self, b, tick):
        self.free[b].append(tick)


def build(S, dbg=False):
    nc = bass.Bass("TRN2", target_bir_lowering=False)
    T = min(512, S)
    NSUB = T // 128
    NST = S // T
    NBLK = S // 128
    okind = "ExternalOutput" if dbg else "Internal"

    def din(name, shape, dt=F32):
        return nc.dram_tensor(name, list(shape), dt, kind="ExternalInput").ap()

    x = din("x", [S, D])
    mix_norm_g = din("mix_norm_g", [2, D])
    ffn_norm_g = din("ffn_norm_g", [2, D])
    gm_w_in = din("gm_w_in", [D, 2 * E])
    gm_ln_g = din("gm_ln_g", [E])
    gm_ln_b = din("gm_ln_b", [E])
    gm_w_s = din("gm_w_s", [8, 128, 128])
    gm_b_s = din("gm_b_s", [8, 128])
    gm_w_out = din("gm_w_out", [E, D])
    fox_w_qkvf = din("fox_w_qkvf", [D, 3 * D + NH])
    fox_b_f = din("fox_b_f", [NH])
    fox_w_o = din("fox_w_o", [D, D])
    ffn_w_gate = din("ffn_w_gate", [2, D, FF])
    ffn_w_up = din("ffn_w_up", [2, D, FF])
    ffn_conv_w = din("ffn_conv_w", [2, 3, FF])
    ffn_conv_b = din("ffn_conv_b", [2, FF])
    ffn_w_down = din("ffn_w_down", [2, FF, D])
    final_norm_g = din("final_norm_g", [D])
    c_ident_bf = din("c_ident_bf", [128, 128], BF16)
    c_ident_f = din("c_ident_f", [128, 128])
    c_tri_f = din("c_tri_f", [128, 128])
    c_maskneg = din("c_maskneg", [128, 128], BF16)
    y = nc.dram_tensor("y", [S, D], F32, kind="ExternalOutput").ap()

    wscr = nc.dram_tensor("wscr", [44, 128, 8, 512], BF16).ap()
    h1s = nc.dram_tensor("h1s", [S, D], F32, kind=okind).ap()
    qTs = nc.dram_tensor("qTs", [D, S], BF16, kind=okind).ap()
    kTs = nc.dram_tensor("kTs", [D, S], BF16, kind=okind).ap()
    vs = nc.dram_tensor("vs", [S, D], BF16, kind=okind).ap()
    aTs = nc.dram_tensor("aTs", [D, S], BF16, kind=okind).ap()

    chunks = []
    for n in range(8):
        chunks.append(gm_w_in[:, n * 512:(n + 1) * 512])
    for n in range(2):
        for kh in range(2):
            chunks.append(gm_w_out[kh * 1024:(kh + 1) * 1024, n * 512:(n + 1) * 512])
    for l in range(2):
        base = len(chunks)
        for n in range(4):
            chunks.append(ffn_w_gate[l, :, n * 512:(n + 1) * 512])
        for n in range(4):
            chunks.append(ffn_w_up[l, :, n * 512:(n + 1) * 512])
        for n in range(2):
            for kh in range(2):
                chunks.append(ffn_w_down[l, kh * 1024:(kh + 1) * 1024, n * 512:(n + 1) * 512])
        if l == 0:
            for n in range(6):
                chunks.append(fox_w_qkvf[:, n * 512:(n + 1) * 512])
            for n in range(2):
                chunks.append(fox_w_o[:, n * 512:(n + 1) * 512])
    assert len(chunks) == 44
    CI_WIN, CI_WOUT, CI_G0, CI_U0, CI_D0, CI_QKV, CI_WO, CI_G1, CI_U1, CI_D1 = 0, 8, 12, 16, 20, 24, 30, 32, 36, 40

    es = ExitStack()
    with es:
        cx = Ctx(nc, es)

        def sb(name, shape, dt, stack=es):
            return stack.enter_context(nc.sbuf_tensor(name, list(shape), dt))

        dsem_dbg = cx.dsem("dbgsem")

        def dump(name, ap, shape, waits):
            if not dbg:
                return None
            dt_ = nc.dram_tensor("dbg_" + name, list(shape), F32, kind="ExternalOutput").ap()
            return cx.op("pool", lambda e: e.dma_start(out=dt_, in_=ap), waits=waits, dma_sem=dsem_dbg)

        psbig = es.enter_context(nc.psum_tensor("psbig", [128, 7, 512], F32))
        ps = [psbig[:, i, :] for i in range(7)]
        pT = es.enter_context(nc.psum_tensor("pT", [128, 1024], BF16))

        ident_bf = sb("ident_bf", [128, 128], BF16)
        ident_f = sb("ident_f", [128, 128], F32)
        tri_f = sb("tri_f", [128, 128], F32)
        ones_f = sb("ones_f", [128, 128], F32)
        maskneg = sb("maskneg", [128, 128], BF16)
        COLS = sb("COLS", [128, 192], F32)
        WcT_bf = sb("WcT_bf", [128, 8, 128], BF16)
        BiasT = sb("BiasT", [128, 16, 128], F32)
        bf_bc = sb("bf_bc", [128, NH], F32)
        gfin_bc = sb("gfin_bc", [128, D], F32)
        wf_bf = sb("wf_bf", [128, 8, NH], BF16)
        CUM = sb("CUM", [128, NBLK, NH], F32)
        CARC = sb("CARC", [128, NBLK + 1, NH], F32)
        NSTAT = 3072
        STAT = sb("STAT", [128, NSTAT], F32)
        CCAR = sb("CCAR", [128, 2, 16, 2], F32)
        junk = sb("junk", [128, 2048], BF16)

        stat_i = [0]

        def stat(n=1):
            i = stat_i[0]
            stat_i[0] += n
            assert stat_i[0] <= NSTAT, "STAT overflow"
            return STAT[:, i:i + n]

        def col_mixg(l, k): return COLS[:, l * 8 + k: l * 8 + k + 1]
        def col_ffng(l, k): return COLS[:, 16 + l * 8 + k: 16 + l * 8 + k + 1]
        def col_lng(k): return COLS[:, 32 + k:33 + k]
        def col_lnb(k): return COLS[:, 48 + k:49 + k]
        def col_cb(l, k): return COLS[:, 64 + l * 16 + k: 65 + l * 16 + k]
        def col_cw(l, i, k): return COLS[:, 96 + (l * 3 + i) * 16 + k: 97 + (l * 3 + i) * 16 + k]

        cast_sems = [cx.dsem("cast%d" % i, barrier=False) for i in range(44)]
        ld0 = cx.dsem("ld0")
        ld_pool = cx.dsem("ld_pool")

        t_wf = cx.op("pool", lambda e: e.dma_start(
            out=wf_bf[:], in_=fox_w_qkvf[:, 3 * D:3 * D + NH].rearrange("(k p) n -> p k n", p=128)), dma_sem=ld_pool)
        cast_tick = [None] * 44

        with ExitStack() as p0:
            R1 = sb("R1", [128, 128], F32, p0)
            R2 = sb("R2", [64, 128], F32, p0)
            WS = sb("WS", [128, 8, 128], F32, p0)
            WcT_f = sb("WcT_f", [128, 8, 128], F32, p0)
            BSb = sb("BSb", [128, 1024], F32, p0)
            loads = [
                (ident_bf[:], c_ident_bf), (ident_f[:], c_ident_f), (tri_f[:], c_tri_f), (maskneg[:], c_maskneg),
                (R1[0:16, :], mix_norm_g.rearrange("l (k p) -> (l k) p", p=128)),
                (R1[16:32, :], ffn_norm_g.rearrange("l (k p) -> (l k) p", p=128)),
                (R1[32:48, :], gm_ln_g.rearrange("(k p) -> k p", p=128)),
                (R1[48:64, :], gm_ln_b.rearrange("(k p) -> k p", p=128)),
                (R1[64:96, :], ffn_conv_b.rearrange("l (k p) -> (l k) p", p=128)),
                (R1[96:128, :], ffn_conv_w.rearrange("l i (k p) -> (l i k) p", p=128)[0:32]),
                (R2[0:64, :], ffn_conv_w.rearrange("l i (k p) -> (l i k) p", p=128)[32:96]),
                (WS[:], gm_w_s.rearrange("g t s -> t g s")),
                (BSb[:], gm_b_s.rearrange("g t -> (g t)").partition_broadcast(128)),
                (bf_bc[:], fox_b_f.partition_broadcast(128)),
                (gfin_bc[:], final_norm_g.partition_broadcast(128)),
            ]
            for o, i_ in loads:
                cx.op("sp", lambda e, o=o, i_=i_: e.dma_start(out=o, in_=i_), dma_sem=ld0)
            ld0_all = ("d", ld0, ld0.n)
            t_m1 = cx.op("dve", lambda e: e.memset(ones_f[:], 1.0))
            t_m2 = cx.op("dve", lambda e: e.memset(STAT[:], 0.0))
            t_m3 = cx.op("dve", lambda e: e.memset(CCAR[:], 0.0))
            t_m4 = cx.op("dve", lambda e: e.memset(CARC[:, 0, :], 0.0))
            cx.op("pe", lambda e: e.matmul(ps[0][:, 0:128], lhsT=R1[:, :], rhs=ident_f[:, :], start=True, stop=True),
                  waits=[ld0_all], notick=True)
            t_pe = cx.op("pe", lambda e: e.matmul(ps[0][:, 128:192], lhsT=R2[0:64, :], rhs=ident_f[0:64, 0:64], start=True, stop=True))
            t_cols = cx.op("dve", lambda e: e.tensor_copy(out=COLS[:], in_=ps[0][:, 0:192]), waits=[t_pe])
            for g in range(8):
                b = 1 + (g % 2) * 2
                t_a = cx.op("pe", lambda e, g=g, b=b: e.matmul(ps[b][:, 0:128], lhsT=WS[:, g, :], rhs=ident_f[:, :], start=True, stop=True),
                            waits=[ld0_all, cx.last("dve")])
                t_b = cx.op("dve", lambda e, g=g, b=b: e.tensor_tensor(out=WcT_f[:, g, :], in0=ps[b][:, 0:128], in1=tri_f[:, :], op=ALU.mult), waits=[t_a])
                t_c = cx.op("dve", lambda e, g=g: e.tensor_copy(out=WcT_bf[:, g, :], in_=WcT_f[:, g, :]), waits=[t_b])
                t_d = cx.op("pe", lambda e, g=g, b=b: e.matmul(ps[b + 1][:, 0:128], lhsT=ones_f[:, :], rhs=WcT_f[:, g, :], start=True, stop=True),
                            waits=[t_b, t_m1])
                for half in range(2):
                    k = 2 * g + half
                    cx.op("dve", lambda e, g=g, b=b, k=k: e.scalar_tensor_tensor(
                        out=BiasT[:, k, :], in0=ps[b + 1][:, 0:128], scalar=col_lnb(k), in1=BSb[:, g * 128:(g + 1) * 128],
                        op0=ALU.mult, op1=ALU.add), waits=[t_d, t_cols])
            dump("COLS", COLS[:], [128, 192], [cx.last("dve")])
            dump("WcT", WcT_bf[:], [128, 8, 128], [cx.last("dve")])
            dump("BiasT", BiasT[:], [128, 16, 128], [cx.last("dve")])
            cx.flush()

        cast_order = [4, 5, 6, 7, 0, 1, 2, 3] + list(range(8, 44))
        for i in cast_order:
            src = chunks[i]
            cast_tick[i] = cx.op("pool", lambda e, i=i, src=src: e.dma_start(
                out=wscr[i], in_=src.rearrange("(k p) n -> p k n", p=128)), dma_sem=cast_sems[i])

        class WRing:
            def __init__(self, tiles, sems):
                self.tiles = tiles
                self.sems = sems
                self.free = [None] * len(tiles)
                self.i = 0

            def load(self, ci):
                slot = self.i % len(self.tiles)
                self.i += 1
                tile = self.tiles[slot]
                assert self.free[slot] != "PENDING", "weight ring slot reused before release"
                t = cx.op("sp", lambda e, tile=tile, ci=ci: e.dma_start(out=tile[:], in_=wscr[ci]),
                          waits=[self.free[slot], cast_tick[ci]], dma_sem=self.sems[slot])
                self.free[slot] = "PENDING"
                return slot, tile, t

            def rel(self, slot, tick):
                self.free[slot] = tick

        wsems = [cx.dsem("wslot%d" % i) for i in range(4)]

        if USE_PT2:
            pT2 = psbig[:, 6, :].bitcast(BF16)
            pTs = [pT[:, :], pT2]
        else:
            pTs = [pT[:, :], pT[:, :]]

        def norm_group(hbuf, g0, hnT, waits_per_j, ns):
            s1 = {}

            def stage1(j, ew_=None):
                hsl = hbuf[:, j, :]
                ss = stat(); ms = stat(); rr = stat(); rstd = stat()
                xi = ns["xi"] % len(ns["xs"])
                ns["xi"] += 1
                xsb = ns["xs"][xi]
                ew = list(waits_per_j[j]) if ew_ is None else list(ew_)
                t1 = cx.op("act", lambda e: e.activation(out=junk[:, 0:1024], in_=hsl, func=AF.Square, accum_out=ss), waits=ew)
                t2 = cx.op("dve", lambda e: e.tensor_scalar(out=ms, in0=ss, scalar1=1.0 / D, scalar2=RMS_EPS, op0=ALU.mult, op1=ALU.add), waits=[t1])
                t3 = cx.op("dve", lambda e: e.reciprocal(out=rr, in_=ms), waits=[t2])
                t4 = cx.op("act", lambda e: e.activation(out=rstd, in_=rr, func=AF.Sqrt), waits=[t3])
                t5 = cx.op("dve", lambda e: e.tensor_scalar(out=xsb[:], in0=hsl, scalar1=rstd, scalar2=None, op0=ALU.mult),
                           waits=[t4] + ns["xs_free"][xi] + ew)
                s1[j] = (xi, xsb, t5)

            def stage2(j, extra=()):
                xi, xsb, t5 = s1[j]
                pi = (ns["pi"] % 2) if USE_PT2 else 0
                ns["pi"] += 1
                pTb = pTs[pi]
                tp = None
                for k in range(8):
                    tp = cx.op("pe", lambda e, k=k: e.transpose(pTb[:, k * 128:(k + 1) * 128], xsb[:, k * 128:(k + 1) * 128], ident_bf[:]),
                               waits=[t5] + ns["pT_free"][pi], notick=(k < 7))
                ns["xs_free"][xi] = [tp]
                if USE_BCAST:
                    tl = cx.op("dve", lambda e: e.tensor_tensor(
                        out=hnT[:, :, j * 128:(j + 1) * 128], in0=pTb.rearrange("p (k t) -> p k t", k=8),
                        in1=COLS[:, g0:g0 + 8].unsqueeze(2).broadcast_to([128, 8, 128]), op=ALU.mult), waits=[tp] + list(extra))
                else:
                    for k in range(8):
                        tl = cx.op("dve", lambda e, k=k: e.tensor_scalar(out=hnT[:, k, j * 128:(j + 1) * 128], in0=pTb[:, k * 128:(k + 1) * 128],
                                                                        scalar1=COLS[:, g0 + k:g0 + k + 1], scalar2=None, op0=ALU.mult), waits=[tp])
                ns["pT_free"][pi] = [tl]
                return tl

            if waits_per_j is None:
                return stage1, stage2
            ticks = [None] * NSUB
            stage1(0)
            for j in range(NSUB):
                if j + 1 < NSUB:
                    stage1(j + 1)
                ticks[j] = stage2(j)
            return ticks

        def norm_cb(s1_, s2_, out):
            def cb(j, r):
                s1_(j, r)
                if j >= 1:
                    out[j - 1] = s2_(j - 1)
            return cb

        def new_norm_state(stack, tag):
            return {"xs": [sb("xs%d%s" % (i, tag), [128, D], BF16, stack) for i in range(4)], "xs_free": [[] for _ in range(4)],
                    "pT_free": [[], []], "xi": 0, "pi": 0}

        def ffn(l, ci_g, ci_u, ci_d, hbuf, hnT, hn_ticks, wr, bank, st, on_res=None):
            actT, Abuf, tbuf, sgbuf = st["actT"], st["A"], st["tb"], st["sg"]
            act_ticks = []
            for n in range(4):
                gs, gt, gtk = wr.load(ci_g + n)
                us, ut, utk = wr.load(ci_u + n)
                tl = None
                for f in range(4):
                    fc = n * 4 + f
                    bg, wg = bank.get()
                    tg = None
                    for k in range(8):
                        tg = cx.op("pe", lambda e, k=k, f=f, bg=bg, gt=gt: e.matmul(ps[bg][:, 0:T], lhsT=gt[:, k, f * 128:(f + 1) * 128], rhs=hnT[:, k, :],
                                                                                 start=(k == 0), stop=(k == 7)),
                                   waits=[gtk] + wg + hn_ticks + st["actT_free"], notick=(k < 7))
                    bu, wu = bank.get()
                    tu = None
                    for k in range(8):
                        tu = cx.op("pe", lambda e, k=k, f=f, bu=bu, ut=ut: e.matmul(ps[bu][:, 0:T], lhsT=ut[:, k, f * 128:(f + 1) * 128], rhs=hnT[:, k, :],
                                                                                 start=(k == 0), stop=(k == 7)),
                                   waits=[utk] + wu, notick=(k < 7))
                    tl = tu
                    ai = fc % len(Abuf)
                    A = Abuf[ai]
                    tb = tbuf[fc % len(tbuf)]
                    sg = sgbuf[fc % len(sgbuf)]
                    ta = cx.op("act", lambda e, A=A, bg=bg: e.activation(out=A[:, 2:T + 2], in_=ps[bg][:, 0:T], func=AF.Identity),
                               waits=[tg] + st["A_free"][ai])
                    bank.rel(bg, ta)
                    tc0 = cx.op("dve", lambda e, A=A, fc=fc: e.tensor_copy(out=A[:, 0:2], in_=CCAR[:, l, fc, :]), waits=st["A_free"][ai])
                    t1 = cx.op("dve", lambda e, A=A, tb=tb, fc=fc: e.tensor_scalar(out=tb[:, 0:T], in0=A[:, 2:T + 2], scalar1=col_cw(l, 2, fc), scalar2=col_cb(l, fc),
                                                                               op0=ALU.mult, op1=ALU.add), waits=[ta] + st["tb_free"][fc % len(tbuf)])
                    t2 = cx.op("dve", lambda e, A=A, tb=tb, fc=fc: e.scalar_tensor_tensor(out=tb[:, 0:T], in0=A[:, 1:T + 1], scalar=col_cw(l, 1, fc), in1=tb[:, 0:T],
                                                                                      op0=ALU.mult, op1=ALU.add), waits=[t1, tc0])
                    t3 = cx.op("dve", lambda e, A=A, tb=tb, fc=fc: e.scalar_tensor_tensor(out=tb[:, 0:T], in0=A[:, 0:T], scalar=col_cw(l, 0, fc), in1=tb[:, 0:T],
                                                                                      op0=ALU.mult, op1=ALU.add), waits=[t2])
                    tc1 = cx.op("dve", lambda e, A=A, fc=fc: e.tensor_copy(out=CCAR[:, l, fc, :], in_=A[:, T:T + 2]), waits=[t2])
                    st["A_free"][ai] = [t3, tc1]
                    ts = cx.op("act", lambda e, tb=tb, sg=sg: e.activation(out=sg[:, 0:T], in_=tb[:, 0:T], func=AF.Silu),
                               waits=[t3] + st["sg_free"][fc % len(sgbuf)])
                    st["tb_free"][fc % len(tbuf)] = [ts]
                    tm = cx.op("dve", lambda e, sg=sg, bu=bu, fc=fc: e.tensor_tensor(out=actT[:, fc, :], in0=ps[bu][:, 0:T], in1=sg[:, 0:T], op=ALU.mult),
                               waits=[ts, tu])
                    bank.rel(bu, tm)
                    st["sg_free"][fc % len(sgbuf)] = [tm]
                    act_ticks.append(tm)
                wr.rel(gs, tl)
                wr.rel(us, tl)
            if l == 0 and not st.get("dumped"):
                st["dumped"] = True
                dump("hnT2", hnT[:], [128, 8, T], hn_ticks)
                dump("actT", actT[:], [128, 16, T], act_ticks)
            res_ticks = [[] for _ in range(NSUB)]
            tlast_pe = None
            for n in range(2):
                bks = [bank.get() for _ in range(NSUB)]
                for kh in range(2):
                    dsl, dt_, dtk = wr.load(ci_d + n * 2 + kh)
                    tl = None
                    for j in range(NSUB):
                        bj, wj = bks[j]
                        for k in range(8):
                            tl = cx.op("pe", lambda e, k=k, j=j, bj=bj, dt_=dt_, kh=kh: e.matmul(
                                ps[bj][:, :], lhsT=actT[:, kh * 8 + k, j * 128:(j + 1) * 128], rhs=dt_[:, k, :],
                                start=(kh == 0 and k == 0), stop=(kh == 1 and k == 7)),
                                waits=[dtk] + wj + act_ticks[:(kh + 1) * 8][-8:], notick=not (k == 7))
                            if k == 7 and kh == 1:
                                tr = cx.op("dve", lambda e, j=j, bj=bj, n=n: e.tensor_tensor(
                                    out=hbuf[:, j, n * 512:(n + 1) * 512], in0=ps[bj][:, :], in1=hbuf[:, j, n * 512:(n + 1) * 512], op=ALU.add),
                                    waits=[tl])
                                bank.rel(bj, tr)
                                res_ticks[j].append(tr)
                                if n == 1 and on_res is not None:
                                    on_res(j, res_ticks[j])
                    wr.rel(dsl, tl)
                    tlast_pe = tl
            st["actT_free"] = [tlast_pe]
            return res_ticks

        def new_ffn_state(stack, tag):
            stt = {
                "actT": sb("actT" + tag, [128, 16, T], BF16, stack),
                "A": [sb("A%d%s" % (i, tag), [128, T + 2], F32, stack) for i in range(3)],
                "tb": [sb("tb%d%s" % (i, tag), [128, T], F32, stack) for i in range(2)],
                "sg": [sb("sg%d%s" % (i, tag), [128, T], F32, stack) for i in range(2)],
            }
            stt["A_free"] = [[] for _ in range(3)]
            stt["tb_free"] = [[] for _ in range(2)]
            stt["sg_free"] = [[] for _ in range(2)]
            stt["actT_free"] = []
            return stt

        st_h1 = cx.dsem("st_h1")
        st_q = cx.dsem("st_q")
        st_k = cx.dsem("st_k")
        st_v = cx.dsem("st_v")
        ld_x = cx.dsem("ld_x")
        with ExitStack() as p1:
            wtiles = [sb("wr%d" % i, [128, 8, 512], BF16, p1) for i in range(4)]
            wr = WRing(wtiles, wsems)
            hbufs = [sb("hbuf%d" % i, [128, NSUB, D], F32, p1) for i in range(2)]
            ns = new_norm_state(p1, "a")
            hnT = sb("hnT", [128, 8, T], BF16, p1)
            uT = sb("uT", [128, 16, T], BF16, p1)
            vb = sb("vb", [128, NSUB, E], BF16, p1)
            gT = sb("gT", [128, 16, T], BF16, p1)
            tmpg = [sb("tmpg%d" % i, [128, 512], F32, p1) for i in range(2)]
            lpb = sb("lpb", [128, NSUB, 2, NH], F32, p1)
            fst = new_ffn_state(p1, "a")
            bank = Banks(ps[:6] if USE_PT2 else ps)
            tmpg_free = [[], []]
            tmpg_i = [0]
            h_frees = [[], []]
            tx_next = None
            uT_free = []
            vb_free = []
            hnT_free = []
            carc_t = [t_m4]
            gT_free = []

            for stile in range(NST):
                t0 = stile * T
                hbuf = hbufs[stile % 2]

                def load_x(st_):
                    hb_ = hbufs[st_ % 2]
                    return cx.op("sp", lambda e, hb_=hb_, st_=st_: e.dma_start(out=hb_[:], in_=x[st_ * T:(st_ + 1) * T, :].rearrange("(j p) d -> p j d", p=128)),
                                 waits=h_frees[st_ % 2], dma_sem=ld_x)
                tx = load_x(0) if stile == 0 else tx_next
                if stile + 1 < NST:
                    tx_next = load_x(stile + 1)
                hn_ticks = norm_group(hbuf, 0, hnT, [[tx] + hnT_free for _ in range(NSUB)], ns)
                if stile == 0:
                    dump("hnT", hnT[:], [128, 8, T], hn_ticks)
                vsum = [stat(4) for _ in range(NSUB)]
                v_ticks = [[] for _ in range(NSUB)]
                tl = None
                for n in range(4, 8):
                    sl, wt, wtk = wr.load(CI_WIN + n)
                    for j in range(NSUB):
                        b, wb_ = bank.get()
                        for k in range(8):
                            tl = cx.op("pe", lambda e, k=k, j=j, b=b, wt=wt: e.matmul(ps[b][:, :], lhsT=hnT[:, k, j * 128:(j + 1) * 128], rhs=wt[:, k, :],
                                                                                   start=(k == 0), stop=(k == 7)),
                                       waits=[wtk, hn_ticks[j]] + wb_, notick=(k < 7))
                        tg_ = cx.op("act", lambda e, j=j, b=b, n=n: e.activation(out=vb[:, j, (n - 4) * 512:(n - 3) * 512], in_=ps[b][:, :], func=AF.Gelu_apprx_tanh,
                                                                             accum_out=vsum[j][:, n - 4:n - 3]), waits=[tl] + vb_free)
                        bank.rel(b, tg_)
                        v_ticks[j].append(tg_)
                    wr.rel(sl, tl)
                vh_ticks = []
                for j in range(NSUB):
                    ssq = stat()
                    tot = stat()
                    mean = stat()
                    msq = stat()
                    var = stat()
                    rr = stat()
                    rstd = stat()
                    nmr = stat()
                    ta = cx.op("act", lambda e, j=j, ssq=ssq: e.activation(out=junk[:, :], in_=vb[:, j, :], func=AF.Square, accum_out=ssq),
                               waits=v_ticks[j])
                    tb_ = cx.op("dve", lambda e, j=j, tot=tot: e.tensor_reduce(out=tot, in_=vsum[j], axis=AX.X, op=ALU.add), waits=v_ticks[j])
                    tc = cx.op("dve", lambda e, tot=tot, mean=mean: e.tensor_scalar(out=mean, in0=tot, scalar1=1.0 / E, scalar2=None, op0=ALU.mult), waits=[tb_])
                    td = cx.op("dve", lambda e, mean=mean, msq=msq: e.tensor_tensor(out=msq, in0=mean, in1=mean, op=ALU.mult), waits=[tc])
                    te = cx.op("dve", lambda e, ssq=ssq, msq=msq, var=var: e.scalar_tensor_tensor(out=var, in0=ssq, scalar=1.0 / E, in1=msq, op0=ALU.mult, op1=ALU.subtract),
                               waits=[ta, td])
                    tf = cx.op("dve", lambda e, var=var: e.tensor_scalar(out=var, in0=var, scalar1=LN_EPS, scalar2=None, op0=ALU.add), waits=[te])
                    tg2 = cx.op("dve", lambda e, var=var, rr=rr: e.reciprocal(out=rr, in_=var), waits=[tf])
                    th = cx.op("act", lambda e, rr=rr, rstd=rstd: e.activation(out=rstd, in_=rr, func=AF.Sqrt), waits=[tg2])
                    ti = cx.op("dve", lambda e, mean=mean, rstd=rstd, nmr=nmr: e.scalar_tensor_tensor(out=nmr, in0=mean, scalar=-1.0, in1=rstd, op0=ALU.mult, op1=ALU.mult),
                               waits=[th, tc])
                    tj = cx.op("dve", lambda e, j=j, rstd=rstd, nmr=nmr: e.tensor_scalar(out=vb[:, j, :], in0=vb[:, j, :], scalar1=rstd, scalar2=nmr, op0=ALU.mult, op1=ALU.add),
                               waits=[ti, ta])
                    vh_ticks.append(tj)
                u_ticks = []
                g_tick = {}

                def spatial_group(q4):
                    for j in range(NSUB):
                        b, wb_ = bank.get()
                        tl_ = None
                        for c4 in range(4):
                            cidx = q4 * 4 + c4
                            tl_ = cx.op("pe", lambda e, j=j, cidx=cidx, c4=c4, b=b: e.matmul(
                                ps[b][:, c4 * 128:(c4 + 1) * 128], lhsT=vb[:, j, cidx * 128:(cidx + 1) * 128], rhs=WcT_bf[:, cidx // 2, :], start=True, stop=True),
                                waits=[vh_ticks[j]] + wb_, notick=(c4 < 3))
                        ti_ = tmpg_i[0] % 2
                        tmpg_i[0] += 1
                        tmp = tmpg[ti_]
                        tt = None
                        for c4 in range(4):
                            cidx = q4 * 4 + c4
                            tt = cx.op("dve", lambda e, cidx=cidx, c4=c4, b=b, tmp=tmp: e.scalar_tensor_tensor(
                                out=tmp[:, c4 * 128:(c4 + 1) * 128], in0=ps[b][:, c4 * 128:(c4 + 1) * 128], scalar=col_lng(cidx), in1=BiasT[:, cidx, :],
                                op0=ALU.mult, op1=ALU.add), waits=[tl_] + tmpg_free[ti_])
                        bank.rel(b, tt)
                        tm = cx.op("dve", lambda e, q4=q4, j=j, tmp=tmp: e.tensor_tensor(
                            out=gT[:, q4 * 4:(q4 + 1) * 4, j * 128:(j + 1) * 128],
                            in0=tmp[:, :].rearrange("p (c t) -> p c t", c=4), in1=uT[:, q4 * 4:(q4 + 1) * 4, j * 128:(j + 1) * 128], op=ALU.mult),
                            waits=[tt] + u_ticks[q4 * 4:(q4 + 1) * 4] + gT_free)
                        tmpg_free[ti_] = [tm]
                        g_tick[(j, q4)] = tm

                for n in range(4):
                    sl, wt, wtk = wr.load(CI_WIN + n)
                    for f in range(4):
                        fc = n * 4 + f
                        b, wb_ = bank.get()
                        for k in range(8):
                            tl = cx.op("pe", lambda e, k=k, f=f, b=b, wt=wt: e.matmul(ps[b][:, 0:T], lhsT=wt[:, k, f * 128:(f + 1) * 128], rhs=hnT[:, k, :],
                                                                                   start=(k == 0), stop=(k == 7)),
                                       waits=[wtk] + hn_ticks + wb_, notick=(k < 7))
                        tg_ = cx.op("act", lambda e, fc=fc, b=b: e.activation(out=uT[:, fc, :], in_=ps[b][:, 0:T], func=AF.Gelu_apprx_tanh),
                                    waits=[tl] + uT_free)
                        bank.rel(b, tg_)
                        u_ticks.append(tg_)
                    wr.rel(sl, tl)
                    if n >= 1:
                        spatial_group(n - 1)
                hnT_free = [tl]
                spatial_group(3)
                g_ticks = [g_tick[(j, q4)] for j in range(NSUB) for q4 in range(4)]
                if stile == 0:
                    dump("vhat", vb[:], [128, NSUB, E], vh_ticks)
                    dump("uT", uT[:], [128, 16, T], u_ticks)
                res0 = [[] for _ in range(NSUB)]
                n_s1, n_s2 = norm_group(hbuf, 16, hnT, None, ns)
                hn_f = [None] * NSUB
                n_cb = norm_cb(n_s1, n_s2, hn_f)
                for n in range(2):
                    bks = [bank.get() for _ in range(NSUB)]
                    for kh in range(2):
                        sl, wt, wtk = wr.load(CI_WOUT + n * 2 + kh)
                        for j in range(NSUB):
                            bj, wj = bks[j]
                            for k in range(8):
                                tl = cx.op("pe", lambda e, k=k, j=j, bj=bj, wt=wt, kh=kh: e.matmul(
                                    ps[bj][:, :], lhsT=gT[:, kh * 8 + k, j * 128:(j + 1) * 128], rhs=wt[:, k, :],
                                    start=(kh == 0 and k == 0), stop=(kh == 1 and k == 7)),
                                    waits=[wtk] + wj + g_ticks[j * 4:(j + 1) * 4], notick=(k < 7))
                            if kh == 1:
                                tr = cx.op("dve", lambda e, j=j, bj=bj, n=n, hbuf=hbuf: e.tensor_tensor(
                                    out=hbuf[:, j, n * 512:(n + 1) * 512], in0=ps[bj][:, :], in1=hbuf[:, j, n * 512:(n + 1) * 512], op=ALU.add), waits=[tl])
                                bank.rel(bj, tr)
                                res0[j].append(tr)
                                if n == 1:
                                    n_cb(j, res0[j])
                        wr.rel(sl, tl)
                gT_free = [tl]
                if stile == 0:
                    dump("gT", gT[:], [128, 16, T], g_ticks)
                    dump("hmix", hbufs[0][:], [128, NSUB, D], [t for r_ in res0 for t in r_])
                hn_f[NSUB - 1] = n_s2(NSUB - 1)
                hn_ticks = list(hn_f)
                m_s1, m_s2 = norm_group(hbuf, 8, hnT, None, ns)
                hn_m = [None] * NSUB
                res1 = ffn(0, CI_G0, CI_U0, CI_D0, hbuf, hnT, hn_ticks, wr, bank, fst, on_res=norm_cb(m_s1, m_s2, hn_m))
                hnT_free = [cx.last("pe")]
                allres = [t for r in res1 for t in r]
                th1 = cx.op("pool", lambda e, t0=t0, hbuf=hbuf: e.dma_start(out=h1s[t0:t0 + T, :].rearrange("(j p) d -> p j d", p=128), in_=hbuf[:]),
                            waits=allres, dma_sem=st_h1)
                hn_m[NSUB - 1] = m_s2(NSUB - 1)
                hn_ticks = list(hn_m)
                h_frees[stile % 2] = [th1] + hn_ticks
                qk_ticks = {0: [], 1: []}
                for n in range(4):
                    sl, wt, wtk = wr.load(CI_QKV + n)
                    for f in range(4):
                        fc = (n % 2) * 4 + f
                        b, wb_ = bank.get()
                        for k in range(8):
                            tl = cx.op("pe", lambda e, k=k, f=f, b=b, wt=wt: e.matmul(ps[b][:, 0:T], lhsT=wt[:, k, f * 128:(f + 1) * 128], rhs=hnT[:, k, :],
                                                                                   start=(k == 0), stop=(k == 7)),
                                       waits=[wtk] + hn_ticks + wb_, notick=(k < 7))
                        dst = uT[:, (n // 2) * 8 + fc, :]
                        tcp = cx.op("act", lambda e, dst=dst, b=b: e.activation(out=dst, in_=ps[b][:, 0:T], func=AF.Identity), waits=[tl] + g_ticks)
                        bank.rel(b, tcp)
                        qk_ticks[n // 2].append(tcp)
                    wr.rel(sl, tl)
                tq = cx.op("pool", lambda e, t0=t0: e.dma_start(out=qTs[:, t0:t0 + T].rearrange("(k p) t -> p k t", p=128), in_=uT[:, 0:8, :]),
                           waits=qk_ticks[0], dma_sem=st_q)
                tk = cx.op("pool", lambda e, t0=t0: e.dma_start(out=kTs[:, t0:t0 + T].rearrange("(k p) t -> p k t", p=128), in_=uT[:, 8:16, :]),
                           waits=qk_ticks[1], dma_sem=st_k)
                uT_free = [tq, tk]
                v_st_ticks = []
                for n in range(2):
                    sl, wt, wtk = wr.load(CI_QKV + 4 + n)
                    for j in range(NSUB):
                        b, wb_ = bank.get()
                        for k in range(8):
                            tl = cx.op("pe", lambda e, k=k, j=j, b=b, wt=wt: e.matmul(ps[b][:, :], lhsT=hnT[:, k, j * 128:(j + 1) * 128], rhs=wt[:, k, :],
                                                                                   start=(k == 0), stop=(k == 7)),
                                       waits=[wtk, hn_ticks[j]] + wb_, notick=(k < 7))
                        tcp = cx.op("act", lambda e, j=j, n=n, b=b: e.activation(out=vb[:, j, n * 512:(n + 1) * 512], in_=ps[b][:, :], func=AF.Identity),
                                    waits=[tl, cx.last("pe")])
                        bank.rel(b, tcp)
                        v_st_ticks.append(tcp)
                    wr.rel(sl, tl)
                tv = cx.op("pool", lambda e, t0=t0: e.dma_start(out=vs[t0:t0 + T, :].rearrange("(j p) d -> p j d", p=128), in_=vb[:, :, 0:D]),
                           waits=v_st_ticks, dma_sem=st_v)
                vb_free = [tv]
                for j in range(NSUB):
                    blk = stile * NSUB + j
                    b, wb_ = bank.get()
                    for k in range(8):
                        tl = cx.op("pe", lambda e, k=k, j=j, b=b: e.matmul(ps[b][:, 0:NH], lhsT=hnT[:, k, j * 128:(j + 1) * 128], rhs=wf_bf[:, k, :],
                                                                        start=(k == 0), stop=(k == 7)),
                                   waits=[hn_ticks[j], t_wf] + wb_, notick=(k < 7))
                    xf = lpb[:, j, 0, :]
                    lp = lpb[:, j, 1, :]
                    t1 = cx.op("dve", lambda e, b=b, xf=xf: e.tensor_tensor(out=xf, in0=ps[b][:, 0:NH], in1=bf_bc[:, :], op=ALU.add), waits=[tl, cx.last("pe")])
                    t2 = cx.op("act", lambda e, xf=xf: e.activation(out=xf, in_=xf, func=AF.Exp, scale=-1.0), waits=[t1])
                    t3 = cx.op("act", lambda e, xf=xf, lp=lp: e.activation(out=lp, in_=xf, func=AF.Ln, bias=1.0), waits=[t2])
                    t4 = None
                    cx.op("pe", lambda e, b=b, lp=lp: e.matmul(ps[b][:, 32:32 + NH], lhsT=tri_f[:, :], rhs=lp, start=True, stop=True), waits=[t3, t1], notick=True)
                    t4 = cx.op("pe", lambda e, b=b, lp=lp: e.matmul(ps[b][:, 64:64 + NH], lhsT=ones_f[:, :], rhs=lp, start=True, stop=True))
                    t5 = cx.op("dve", lambda e, b=b, blk=blk: e.tensor_tensor(out=CUM[:, blk, :], in0=CARC[:, blk, :], in1=ps[b][:, 32:32 + NH], op=ALU.subtract),
                               waits=[t4] + carc_t)
                    t6 = cx.op("dve", lambda e, b=b, blk=blk: e.tensor_tensor(out=CARC[:, blk + 1, :], in0=CARC[:, blk, :], in1=ps[b][:, 64:64 + NH], op=ALU.subtract),
                               waits=[t4] + carc_t)
                    carc_t = [t6]
                    bank.rel(b, t6)
                hnT_free = [cx.last("pe")]
            cx.flush()

        ld_k = [cx.dsem("ld_k%d" % i) for i in range(2)]
        ld_q = [cx.dsem("ld_q%d" % i) for i in range(2)]
        ld_v = [cx.dsem("ld_v%d" % i) for i in range(2)]
        st_a = [cx.dsem("st_a%d" % i) for i in range(4)]
        TQ = min(1024, S)
        NQ = S // TQ
        BW = min(512, TQ)
        NB = TQ // BW
        KBQ = TQ // 128
        NSB = 3
        with ExitStack() as p2:
            KT = [sb("KT%d" % i, [128, S], BF16, p2) for i in range(2)]
            QZ = [[sb("QZ%d_%d" % (i, hh), [128, S], BF16, p2) for hh in range(2)] for i in range(2)]
            VA = [sb("VA%d" % i, [128, NBLK, 128], BF16, p2) for i in range(2)]
            NPT = 4
            PT = [sb("PT%d" % i, [128, TQ], BF16, p2) for i in range(NPT)]
            biasH = [sb("biasH%d" % i, [128, NQ, NBLK], F32, p2) for i in range(2)]
            Osb = [sb("Osb%d" % i, [64, 512], F32, p2) for i in range(4)]
            Lsb = [sb("Lsb%d" % i, [64, 512], F32, p2) for i in range(4)]
            ast = [sb("ast%d" % i, [64, 512], BF16, p2) for i in range(4)]
            tmo = []
            tmv = []
            for i in range(2):
                tmv.append(cx.op("dve", lambda e, i=i: e.memset(VA[i][:, :, DH:128], 1.0)))
                cx.op("dve", lambda e, i=i: e.memset(QZ[i][0][64:128, :], 0.0))
                tmo.append(cx.op("dve", lambda e, i=i: e.memset(QZ[i][1][0:64, :], 0.0)))
            SB = [psbig[:, 2 * b_:2 * b_ + NB, :] for b_ in range(NSB)]
            SBf = [SB[b_].rearrange("p a b -> p (a b)") for b_ in range(NSB)]
            OBs = [psbig[:, 6, :], pT[:, :].bitcast(F32)]
            sfree = [[] for _ in range(NSB)]
            s_i = [0]
            pt_free = [[] for _ in range(NPT)]
            pt_i = [0]
            biasH_free = [[], []]
            osb_free = [[] for _ in range(4)]
            osb_i = [0]
            ofree = [[] for _ in range(NB)]
            ast_free = [[] for _ in range(4)]
            ast_i = [0]
            buf_free = [[], []]
            va_free = [[], []]
            pvq = []
            PV_LAG = 2
            last_pe_box = [None]
            for hp in range(NH // 2):
                bi = hp % 2
                tkl = cx.op("sp", lambda e, hp=hp, bi=bi: e.dma_start(out=KT[bi][:], in_=kTs[hp * 128:(hp + 1) * 128, :]), waits=buf_free[bi], dma_sem=ld_k[bi])
                cx.op("sp", lambda e, hp=hp, bi=bi: e.dma_start(out=QZ[bi][0][0:64, :], in_=qTs[hp * 128:hp * 128 + 64, :]),
                      waits=buf_free[bi] + [tmo[bi]], dma_sem=ld_q[bi])
                tql = cx.op("sp", lambda e, hp=hp, bi=bi: e.dma_start(out=QZ[bi][1][64:128, :], in_=qTs[hp * 128 + 64:hp * 128 + 128, :]),
                            waits=buf_free[bi] + [tmo[bi]], dma_sem=ld_q[bi])
                tvls = []
                for hh_ in range(2):
                    h_ = hp * 2 + hh_
                    tvls.append(cx.op("sp", lambda e, h_=h_: e.dma_start(
                        out=VA[h_ % 2][:, :, 0:DH], in_=vs[:, h_ * 64:(h_ + 1) * 64].rearrange("(b p) d -> p b d", p=128)),
                        waits=va_free[h_ % 2] + [tmv[h_ % 2]], dma_sem=ld_v[h_ % 2]))
                last_pe = None
                for hh in range(2):
                    h = hp * 2 + hh
                    vbuf = VA[h % 2]
                    tvl = tvls[hh]
                    bH = biasH[h % 2]
                    tbias = None
                    for qp in range(NQ):
                        nkb = (qp + 1) * KBQ
                        cref_blk = qp * KBQ + KBQ // 2
                        tbias = cx.op("dve", lambda e, bH=bH, h=h, nkb=nkb, qp=qp, cref_blk=cref_blk: e.tensor_scalar(
                            out=bH[:, qp, 0:nkb], in0=CUM[:, 0:nkb, h], scalar1=-1.0, scalar2=CARC[:, cref_blk, h:h + 1], op0=ALU.mult, op1=ALU.add),
                            waits=biasH_free[h % 2])
                    for qp in range(NQ):
                        q0 = qp * TQ
                        nkb = (qp + 1) * KBQ
                        bt = bH[:, qp, :]
                        lastj = [max(j for j in range(nkb) if max(j - qp * KBQ, 0) * 128 < (nb + 1) * BW) for nb in range(NB)]
                        exp_ticks = []
                        pv_last = [None] * NB
                        for j in range(nkb):
                            jj = j - qp * KBQ
                            c0 = max(jj, 0) * 128
                            sbi = s_i[0] % NSB
                            s_i[0] += 1
                            tqk = None
                            for nb in range(NB):
                                lo = max(c0, nb * BW)
                                hi = (nb + 1) * BW
                                if lo >= hi:
                                    continue
                                tqk = cx.op("pe", lambda e, sbi=sbi, nb=nb, lo=lo, hi=hi, j=j, bi=bi, hh=hh, q0=q0: e.matmul(
                                    SB[sbi][:, nb, lo - nb * BW:BW], lhsT=KT[bi][:, j * 128:(j + 1) * 128], rhs=QZ[bi][hh][:, q0 + lo:q0 + hi], start=True, stop=True),
                                    waits=[tkl, tql] + sfree[sbi], notick=(jj >= 0 or nb < NB - 1))
                            if jj >= 0:
                                nbm = c0 // BW
                                off = c0 - nbm * BW
                                tqk = cx.op("pe", lambda e, sbi=sbi, nbm=nbm, off=off: e.matmul(
                                    SB[sbi][:, nbm, off:off + 128], lhsT=ident_bf[:, :], rhs=maskneg[:, :], start=False, stop=True))
                            pi = pt_i[0] % NPT
                            pt_i[0] += 1
                            te = cx.op("act", lambda e, pi=pi, sbi=sbi, c0=c0, bt=bt, j=j: e.activation(
                                out=PT[pi][:, c0:TQ], in_=SBf[sbi][:, c0:TQ], func=AF.Exp, scale=DH ** -0.5, bias=bt[:, j:j + 1]),
                                waits=[tqk, tbias] + pt_free[pi])
                            sfree[sbi] = [te]
                            exp_ticks.append(te)

                            def do_pv(pi_=pi, te_=te, c0_=c0, j_=j, vbuf=vbuf, tvl=tvl, lastj=lastj, pv_last=pv_last):
                                tpv = None
                                for nb in range(NB):
                                    lo = max(c0_, nb * BW)
                                    hi = (nb + 1) * BW
                                    if lo >= hi:
                                        continue
                                    tpv = cx.op("pe", lambda e, nb=nb, lo=lo, hi=hi, st_=(j_ == lastj[nb]): e.matmul(
                                        OBs[nb][:, lo - nb * BW:BW], lhsT=vbuf[:, j_, :], rhs=PT[pi_][:, lo:hi], start=(j_ == 0), stop=st_),
                                        waits=[te_, tvl] + (ofree[nb] if j_ == 0 else []))
                                    pv_last[nb] = tpv
                                pt_free[pi_] = [tpv]
                                last_pe_box[0] = tpv
                            pvq.append(do_pv)
                            while len(pvq) > PV_LAG:
                                pvq.pop(0)()
                        if qp == NQ - 1:
                            biasH_free[h % 2] = [exp_ticks[-1]]

                        def do_norm(h=h, q0=q0, pv_last=pv_last):
                            for nb in range(NB):
                                oi = osb_i[0] % 4
                                osb_i[0] += 1
                                tc1 = cx.op("dve", lambda e, nb=nb, oi=oi: e.tensor_copy(out=Osb[oi][:, 0:BW], in_=OBs[nb][0:64, 0:BW]),
                                            waits=[pv_last[nb]] + osb_free[oi])
                                tc2 = cx.op("dve", lambda e, nb=nb, oi=oi: e.tensor_copy(out=Lsb[oi][:, 0:BW], in_=OBs[nb][64:128, 0:BW]),
                                            waits=[pv_last[nb]] + osb_free[oi])
                                ofree[nb] = [tc1, tc2]
                                trc = cx.op("dve", lambda e, oi=oi: e.reciprocal(out=Lsb[oi][:, 0:BW], in_=Lsb[oi][:, 0:BW]), waits=[tc2])
                                ai = ast_i[0] % 4
                                ast_i[0] += 1
                                tmu = cx.op("dve", lambda e, ai=ai, oi=oi: e.tensor_tensor(out=ast[ai][:, 0:BW], in0=Osb[oi][:, 0:BW], in1=Lsb[oi][:, 0:BW], op=ALU.mult),
                                            waits=[trc, tc1] + ast_free[ai])
                                osb_free[oi] = [tmu]
                                tst = cx.op("pool", lambda e, ai=ai, nb=nb: e.dma_start(out=aTs[h * 64:(h + 1) * 64, q0 + nb * BW:q0 + (nb + 1) * BW], in_=ast[ai][:, 0:BW]),
                                            waits=[tmu], dma_sem=st_a[ai])
                                ast_free[ai] = [tst]
                        pvq.append(do_norm)
                    while pvq:
                        pvq.pop(0)()
                    last_pe = last_pe_box[0]
                    va_free[h % 2] = [last_pe]
                buf_free[bi] = [last_pe]
            cx.flush()

        ld_h = cx.dsem("ld_h")
        ld_a = cx.dsem("ld_a")
        st_y = cx.dsem("st_y")
        with ExitStack() as p3:
            wtiles = [sb("wq%d" % i, [128, 8, 512], BF16, p3) for i in range(4)]
            wr = WRing(wtiles, wsems)
            hbufs = [sb("hbuf3%d" % i, [128, NSUB, D], F32, p3) for i in range(2)]
            aT = sb("aT", [128, 8, T], BF16, p3)
            ns = new_norm_state(p3, "b")
            hnT = sb("hnT3", [128, 8, T], BF16, p3)
            fst = new_ffn_state(p3, "b")
            bank = Banks(ps[:6] if USE_PT2 else ps)
            h_frees = [[], []]
            aT_free = []
            hnT_free = []
            for stile in range(NST):
                t0 = stile * T
                hbuf = hbufs[stile % 2]
                th = cx.op("sp", lambda e, t0=t0, hbuf=hbuf: e.dma_start(out=hbuf[:], in_=h1s[t0:t0 + T, :].rearrange("(j p) d -> p j d", p=128)),
                           waits=h_frees[stile % 2], dma_sem=ld_h)
                ta_ = cx.op("sp", lambda e, t0=t0: e.dma_start(out=aT[:], in_=aTs[:, t0:t0 + T].rearrange("(k p) t -> p k t", p=128)),
                            waits=aT_free, dma_sem=ld_a)
                res0 = [[] for _ in range(NSUB)]
                tl = None
                n_s1, n_s2 = norm_group(hbuf, 24, hnT, None, ns)
                hn_f = [None] * NSUB
                n_cb = norm_cb(n_s1, n_s2, hn_f)
                for n in range(2):
                    sl, wt, wtk = wr.load(CI_WO + n)
                    for j in range(NSUB):
                        b, wb_ = bank.get()
                        for k in range(8):
                            tl = cx.op("pe", lambda e, k=k, j=j, b=b, wt=wt: e.matmul(ps[b][:, :], lhsT=aT[:, k, j * 128:(j + 1) * 128], rhs=wt[:, k, :],
                                                                                   start=(k == 0), stop=(k == 7)),
                                       waits=[wtk, ta_] + wb_, notick=(k < 7))
                        tr = cx.op("dve", lambda e, j=j, b=b, n=n, hbuf=hbuf: e.tensor_tensor(
                            out=hbuf[:, j, n * 512:(n + 1) * 512], in0=ps[b][:, :], in1=hbuf[:, j, n * 512:(n + 1) * 512], op=ALU.add), waits=[tl, th])
                        bank.rel(b, tr)
                        res0[j].append(tr)
                        if n == 1:
                            n_cb(j, res0[j])
                    wr.rel(sl, tl)
                aT_free = [tl]
                hn_f[NSUB - 1] = n_s2(NSUB - 1)
                hn_ticks = list(hn_f)
                outs = []

                def final_norm(j, rj, hbuf=hbuf, outs=outs):
                    res1 = {j: rj}
                    ss = stat()
                    ms = stat()
                    rr = stat()
                    rstd = stat()
                    hsl = hbuf[:, j, :]
                    t1 = cx.op("act", lambda e, hsl=hsl, ss=ss: e.activation(out=junk[:, 0:1024], in_=hsl, func=AF.Square, accum_out=ss),
                               waits=res1[j])
                    t2 = cx.op("dve", lambda e, ss=ss, ms=ms: e.tensor_scalar(out=ms, in0=ss, scalar1=1.0 / D, scalar2=RMS_EPS, op0=ALU.mult, op1=ALU.add), waits=[t1])
                    t3 = cx.op("dve", lambda e, ms=ms, rr=rr: e.reciprocal(out=rr, in_=ms), waits=[t2])
                    t4 = cx.op("act", lambda e, rr=rr, rstd=rstd: e.activation(out=rstd, in_=rr, func=AF.Sqrt), waits=[t3])
                    t5 = cx.op("dve", lambda e, hsl=hsl, rstd=rstd: e.scalar_tensor_tensor(out=hsl, in0=hsl, scalar=rstd, in1=gfin_bc[:, :], op0=ALU.mult, op1=ALU.mult),
                               waits=[t4, t1] + res1[j])
                    outs.append(t5)
                res1 = ffn(1, CI_G1, CI_U1, CI_D1, hbuf, hnT, hn_ticks, wr, bank, fst, on_res=final_norm)
                hnT_free = [cx.last("pe")]
                ty = cx.op("pool", lambda e, t0=t0, hbuf=hbuf: e.dma_start(out=y[t0:t0 + T, :].rearrange("(j p) d -> p j d", p=128), in_=hbuf[:]),
                           waits=list(outs), dma_sem=st_y)
                h_frees[stile % 2] = [ty]
            cx.flush()
    return nc


def _consts():
    bf = ml_dtypes.bfloat16
    ident = np.eye(128, dtype=np.float32)
    s = np.arange(128)[:, None]
    t = np.arange(128)[None, :]
    tri = (s <= t).astype(np.float32)
    maskneg = np.where(t >= s, 0.0, NEG).astype(np.float32)
    return {
        "c_ident_bf": ident.astype(bf), "c_ident_f": ident, "c_tri_f": tri, "c_maskneg": maskneg.astype(bf),
    }


def make_in_maps(inputs, S, n_cores):
    c = _consts()
    f = lambda a: np.ascontiguousarray(np.asarray(a, dtype=np.float32))
    shared = {
        "mix_norm_g": f(inputs["mix_norm_g"]), "ffn_norm_g": f(inputs["ffn_norm_g"]),
        "gm_w_in": f(inputs["gm_w_in"][0]), "gm_ln_g": f(inputs["gm_ln_g"][0]), "gm_ln_b": f(inputs["gm_ln_b"][0]),
        "gm_w_s": f(inputs["gm_w_s"][0]), "gm_b_s": f(inputs["gm_b_s"][0]), "gm_w_out": f(inputs["gm_w_out"][0]),
        "fox_w_qkvf": f(inputs["fox_w_qkvf"][0]), "fox_b_f": f(inputs["fox_b_f"][0]), "fox_w_o": f(inputs["fox_w_o"][0]),
        "ffn_w_gate": f(inputs["ffn_w_gate"]), "ffn_w_up": f(inputs["ffn_w_up"]), "ffn_conv_w": f(inputs["ffn_conv_w"]),
        "ffn_conv_b": f(inputs["ffn_conv_b"]), "ffn_w_down": f(inputs["ffn_w_down"]), "final_norm_g": f(inputs["final_norm_g"]),
    }
    shared.update(c)
    xs = np.asarray(inputs["x"], dtype=np.float32)
    maps = []
    for i in range(n_cores):
        m = dict(shared)
        m["x"] = np.ascontiguousarray(xs[i, :S])
        maps.append(m)
    return maps


_NC_CACHE = {}


def kernel(**inputs):
    xs = np.asarray(inputs["x"])
    B, S, _ = xs.shape
    if S not in _NC_CACHE:
        _NC_CACHE[S] = build(S)
    nc = _NC_CACHE[S]
    maps = make_in_maps(inputs, S, B)
    res = run_bass_kernel_spmd(nc, maps, core_ids=list(range(B)))
    out = np.stack([np.asarray(r["y"], dtype=np.float32) for r in res.results], axis=0)
    return out
```

```python
from contextlib import ExitStack
import numpy as np
import ml_dtypes
import concourse.bass as bass
import concourse.mybir as mybir
from concourse.bass_utils import run_bass_kernel_spmd

F32 = mybir.dt.float32
BF16 = mybir.dt.bfloat16
AF = mybir.ActivationFunctionType
ALU = mybir.AluOpType
AX = mybir.AxisListType

D = 1024
E = 2048
FF = 2048
NH = 16
DH = 64
RMS_EPS = 1e-6
LN_EPS = 1e-5
NEG = -30000.0
USE_PT2 = True
USE_BCAST = True


class Sem:
    def __init__(self, h, name):
        self.h = h
        self.name = name
        self.n = 0


class Ctx:
    CE = ("pe", "act", "dve", "pool")
    ENG = ("pe", "act", "dve", "pool", "sp")

    def __init__(self, nc, es):
        self.nc = nc
        self.es = es
        self.esem = {e: Sem(es.enter_context(nc.semaphore("e_" + e)), "e_" + e) for e in self.CE}
        self.tickn = {e: 0 for e in self.CE}
        self.base = {e: 0 for e in self.CE}
        self.semcount = {e: 0 for e in self.CE}
        self.ops = {e: [] for e in self.ENG}
        self.waited = {e: {} for e in self.ENG}
        self.needed = {e: set() for e in self.CE}
        self.dsems = []
        self.dbase = {}

    def dsem(self, name, barrier=True):
        s = Sem(self.es.enter_context(self.nc.semaphore(name)), name)
        if barrier:
            self.dsems.append(s)
        self.dbase[name] = 0
        return s

    def op(self, eng, fn, waits=(), dma_sem=None, notick=False):
        wl = []
        for w in waits:
            if w is None:
                continue
            kind, a, v = w
            if kind == "e":
                if v <= self.base[a]:
                    continue
                key = a
            else:
                if v <= self.dbase[a.name]:
                    continue
                key = a.name
            if self.waited[eng].get(key, 0) >= v:
                continue
            self.waited[eng][key] = v
            wl.append(w)
            if kind == "e":
                self.needed[a].add(v)
        t = None
        if dma_sem is not None:
            dma_sem.n += 16
            t = ("d", dma_sem, dma_sem.n)
        elif eng in self.esem and not notick and fn is not None:
            self.tickn[eng] += 1
            t = ("e", eng, self.tickn[eng])
        self.ops[eng].append((fn, wl, t))
        return t

    def last(self, eng):
        return ("e", eng, self.tickn[eng])

    def flush(self):
        allw = [self.last(e) for e in self.CE] + [("d", s, s.n) for s in self.dsems]
        for e in self.ENG:
            self.op(e, None, allw)
        rank = {}
        for e in self.CE:
            ids = sorted(self.needed[e])
            rank[e] = {tid: self.semcount[e] + i + 1 for i, tid in enumerate(ids)}
            self.semcount[e] += len(ids)
        ops = self.ops
        esem = self.esem

        def mk(key):
            def body(e):
                for fn, wl, t in ops[key]:
                    for kind, a, v in wl:
                        if kind == "e":
                            e.wait_ge(esem[a].h, rank[a][v])
                        else:
                            e.wait_ge(a.h, v)
                    if fn is None:
                        continue
                    ins = fn(e)
                    if t is not None:
                        if t[0] == "d":
                            ins.then_inc(t[1].h, 16)
                        elif t[2] in rank[t[1]]:
                            ins.then_inc(esem[t[1]].h, 1)
            return body

        with self.nc.Block() as block:
            block.tensor(mk("pe"))
            block.scalar(mk("act"))
            block.vector(mk("dve"))
            block.gpsimd(mk("pool"))
            block.sync(mk("sp"))
        self.ops = {e: [] for e in self.ENG}
        self.waited = {e: {} for e in self.ENG}
        self.needed = {e: set() for e in self.CE}
        for e in self.CE:
            self.base[e] = self.tickn[e]
        for s in self.dsems:
            self.dbase[s.name] = s.n


class Banks:
    def __init__(self, tiles):
        self.tiles = tiles
        self.free = [[] for _ in tiles]
        self.i = 0

    def get(self):
        b = self.i % len(self.tiles)
        self.i += 1
        w = self.free[b]
        self.free[b] = []
        return b, w

    def rel(self, b, tick):
        self.free[b].append(tick)


def build(S, dbg=False):
    nc = bass.Bass("TRN2", target_bir_lowering=False)
    T = min(512, S)
    NSUB = T // 128
    NST = S // T
    NBLK = S // 128
    okind = "ExternalOutput" if dbg else "Internal"

    def din(name, shape, dt=F32):
        return nc.dram_tensor(name, list(shape), dt, kind="ExternalInput").ap()

    x = din("x", [S, D])
    mix_norm_g = din("mix_norm_g", [2, D])
    ffn_norm_g = din("ffn_norm_g", [2, D])
    gm_w_in = din("gm_w_in", [D, 2 * E])
    gm_ln_g = din("gm_ln_g", [E])
    gm_ln_b = din("gm_ln_b", [E])
    gm_w_s = din("gm_w_s", [8, 128, 128])
    gm_b_s = din("gm_b_s", [8, 128])
    gm_w_out = din("gm_w_out", [E, D])
    fox_w_qkvf = din("fox_w_qkvf", [D, 3 * D + NH])
    fox_b_f = din("fox_b_f", [NH])
    fox_w_o = din("fox_w_o", [D, D])
    ffn_w_gate = din("ffn_w_gate", [2, D, FF])
    ffn_w_up = din("ffn_w_up", [2, D, FF])
    ffn_conv_w = din("ffn_conv_w", [2, 3, FF])
    ffn_conv_b = din("ffn_conv_b", [2, FF])
    ffn_w_down = din("ffn_w_down", [2, FF, D])
    final_norm_g = din("final_norm_g", [D])
    c_ident_bf = din("c_ident_bf", [128, 128], BF16)
    c_ident_f = din("c_ident_f", [128, 128])
    c_tri_f = din("c_tri_f", [128, 128])
    c_maskneg = din("c_maskneg", [128, 128], BF16)
    y = nc.dram_tensor("y", [S, D], F32, kind="ExternalOutput").ap()

    wscr = nc.dram_tensor("wscr", [44, 128, 8, 512], BF16).ap()
    h1s = nc.dram_tensor("h1s", [S, D], F32, kind=okind).ap()
    qTs = nc.dram_tensor("qTs", [D, S], BF16, kind=okind).ap()
    kTs = nc.dram_tensor("kTs", [D, S], BF16, kind=okind).ap()
    vs = nc.dram_tensor("vs", [S, D], BF16, kind=okind).ap()
    aTs = nc.dram_tensor("aTs", [D, S], BF16, kind=okind).ap()

    chunks = []
    for n in range(8):
        chunks.append(gm_w_in[:, n * 512:(n + 1) * 512])
    for n in range(2):
        for kh in range(2):
            chunks.append(gm_w_out[kh * 1024:(kh + 1) * 1024, n * 512:(n + 1) * 512])
    for l in range(2):
        base = len(chunks)
        for n in range(4):
            chunks.append(ffn_w_gate[l, :, n * 512:(n + 1) * 512])
        for n in range(4):
            chunks.append(ffn_w_up[l, :, n * 512:(n + 1) * 512])
        for n in range(2):
            for kh in range(2):
                chunks.append(ffn_w_down[l, kh * 1024:(kh + 1) * 1024, n * 512:(n + 1) * 512])
        if l == 0:
            for n in range(6):
                chunks.append(fox_w_qkvf[:, n * 512:(n + 1) * 512])
            for n in range(2):
                chunks.append(fox_w_o[:, n * 512:(n + 1) * 512])
    assert len(chunks) == 44
    CI_WIN, CI_WOUT, CI_G0, CI_U0, CI_D0, CI_QKV, CI_WO, CI_G1, CI_U1, CI_D1 = 0, 8, 12, 16, 20, 24, 30, 32, 36, 40

    es = ExitStack()
    with es:
        cx = Ctx(nc, es)

        def sb(name, shape, dt, stack=es):
            return stack.enter_context(nc.sbuf_tensor(name, list(shape), dt))

        dsem_dbg = cx.dsem("dbgsem")

        def dump(name, ap, shape, waits):
            if not dbg:
                return None
            dt_ = nc.dram_tensor("dbg_" + name, list(shape), F32, kind="ExternalOutput").ap()
            return cx.op("pool", lambda e: e.dma_start(out=dt_, in_=ap), waits=waits, dma_sem=dsem_dbg)

        psbig = es.enter_context(nc.psum_tensor("psbig", [128, 7, 512], F32))
        ps = [psbig[:, i, :] for i in range(7)]
        pT = es.enter_context(nc.psum_tensor("pT", [128, 1024], BF16))

        ident_bf = sb("ident_bf", [128, 128], BF16)
        ident_f = sb("ident_f", [128, 128], F32)
        tri_f = sb("tri_f", [128, 128], F32)
        ones_f = sb("ones_f", [128, 128], F32)
        maskneg = sb("maskneg", [128, 128], BF16)
        COLS = sb("COLS", [128, 192], F32)
        WcT_bf = sb("WcT_bf", [128, 8, 128], BF16)
        BiasT = sb("BiasT", [128, 16, 128], F32)
        bf_bc = sb("bf_bc", [128, NH], F32)
        gfin_bc = sb("gfin_bc", [128, D], F32)
        wf_bf = sb("wf_bf", [128, 8, NH], BF16)
        CUM = sb("CUM", [128, NBLK, NH], F32)
        CARC = sb("CARC", [128, NBLK + 1, NH], F32)
        NSTAT = 3072
        STAT = sb("STAT", [128, NSTAT], F32)
        CCAR = sb("CCAR", [128, 2, 16, 2], F32)
        junk = sb("junk", [128, 2048], BF16)

        stat_i = [0]

        def stat(n=1):
            i = stat_i[0]
            stat_i[0] += n
            assert stat_i[0] <= NSTAT, "STAT overflow"
            return STAT[:, i:i + n]

        def col_mixg(l, k): return COLS[:, l * 8 + k: l * 8 + k + 1]
        def col_ffng(l, k): return COLS[:, 16 + l * 8 + k: 16 + l * 8 + k + 1]
        def col_lng(k): return COLS[:, 32 + k:33 + k]
        def col_lnb(k): return COLS[:, 48 + k:49 + k]
        def col_cb(l, k): return COLS[:, 64 + l * 16 + k: 65 + l * 16 + k]
        def col_cw(l, i, k): return COLS[:, 96 + (l * 3 + i) * 16 + k: 97 + (l * 3 + i) * 16 + k]

        cast_sems = [cx.dsem("cast%d" % i, barrier=False) for i in range(44)]
        ld0 = cx.dsem("ld0")
        ld_pool = cx.dsem("ld_pool")

        t_wf = cx.op("pool", lambda e: e.dma_start(
            out=wf_bf[:], in_=fox_w_qkvf[:, 3 * D:3 * D + NH].rearrange("(k p) n -> p k n", p=128)), dma_sem=ld_pool)
        cast_tick = [None] * 44
        cast_order = [4, 5, 6, 7, 0, 1, 2, 3] + list(range(8, 44))
        for i in cast_order:
            src = chunks[i]
            cast_tick[i] = cx.op("pool", lambda e, i=i, src=src: e.dma_start(
                out=wscr[i], in_=src.rearrange("(k p) n -> p k n", p=128)), dma_sem=cast_sems[i])

        with ExitStack() as p0:
            R1 = sb("R1", [128, 128], F32, p0)
            R2 = sb("R2", [64, 128], F32, p0)
            WS = sb("WS", [128, 8, 128], F32, p0)
            WcT_f = sb("WcT_f", [128, 8, 128], F32, p0)
            BSb = sb("BSb", [128, 1024], F32, p0)
            loads = [
                (ident_bf[:], c_ident_bf), (ident_f[:], c_ident_f), (tri_f[:], c_tri_f), (maskneg[:], c_maskneg),
                (R1[0:16, :], mix_norm_g.rearrange("l (k p) -> (l k) p", p=128)),
                (R1[16:32, :], ffn_norm_g.rearrange("l (k p) -> (l k) p", p=128)),
                (R1[32:48, :], gm_ln_g.rearrange("(k p) -> k p", p=128)),
                (R1[48:64, :], gm_ln_b.rearrange("(k p) -> k p", p=128)),
                (R1[64:96, :], ffn_conv_b.rearrange("l (k p) -> (l k) p", p=128)),
                (R1[96:128, :], ffn_conv_w.rearrange("l i (k p) -> (l i k) p", p=128)[0:32]),
                (R2[0:64, :], ffn_conv_w.rearrange("l i (k p) -> (l i k) p", p=128)[32:96]),
                (WS[:], gm_w_s.rearrange("g t s -> t g s")),
                (BSb[:], gm_b_s.rearrange("g t -> (g t)").partition_broadcast(128)),
                (bf_bc[:], fox_b_f.partition_broadcast(128)),
                (gfin_bc[:], final_norm_g.partition_broadcast(128)),
            ]
            for o, i_ in loads:
                cx.op("sp", lambda e, o=o, i_=i_: e.dma_start(out=o, in_=i_), dma_sem=ld0)
            ld0_all = ("d", ld0, ld0.n)
            t_m1 = cx.op("dve", lambda e: e.memset(ones_f[:], 1.0))
            t_m2 = cx.op("dve", lambda e: e.memset(STAT[:], 0.0))
            t_m3 = cx.op("dve", lambda e: e.memset(CCAR[:], 0.0))
            t_m4 = cx.op("dve", lambda e: e.memset(CARC[:, 0, :], 0.0))
            cx.op("pe", lambda e: e.matmul(ps[0][:, 0:128], lhsT=R1[:, :], rhs=ident_f[:, :], start=True, stop=True),
                  waits=[ld0_all], notick=True)
            t_pe = cx.op("pe", lambda e: e.matmul(ps[0][:, 128:192], lhsT=R2[0:64, :], rhs=ident_f[0:64, 0:64], start=True, stop=True))
            t_cols = cx.op("dve", lambda e: e.tensor_copy(out=COLS[:], in_=ps[0][:, 0:192]), waits=[t_pe])
            for g in range(8):
                b = 1 + (g % 2) * 2
                t_a = cx.op("pe", lambda e, g=g, b=b: e.matmul(ps[b][:, 0:128], lhsT=WS[:, g, :], rhs=ident_f[:, :], start=True, stop=True),
                            waits=[ld0_all, cx.last("dve")])
                t_b = cx.op("dve", lambda e, g=g, b=b: e.tensor_tensor(out=WcT_f[:, g, :], in0=ps[b][:, 0:128], in1=tri_f[:, :], op=ALU.mult), waits=[t_a])
                t_c = cx.op("dve", lambda e, g=g: e.tensor_copy(out=WcT_bf[:, g, :], in_=WcT_f[:, g, :]), waits=[t_b])
                t_d = cx.op("pe", lambda e, g=g, b=b: e.matmul(ps[b + 1][:, 0:128], lhsT=ones_f[:, :], rhs=WcT_f[:, g, :], start=True, stop=True),
                            waits=[t_b, t_m1])
                for half in range(2):
                    k = 2 * g + half
                    cx.op("dve", lambda e, g=g, b=b, k=k: e.scalar_tensor_tensor(
                        out=BiasT[:, k, :], in0=ps[b + 1][:, 0:128], scalar=col_lnb(k), in1=BSb[:, g * 128:(g + 1) * 128],
                        op0=ALU.mult, op1=ALU.add), waits=[t_d, t_cols])
            dump("COLS", COLS[:], [128, 192], [cx.last("dve")])
            dump("WcT", WcT_bf[:], [128, 8, 128], [cx.last("dve")])
            dump("BiasT", BiasT[:], [128, 16, 128], [cx.last("dve")])
            cx.flush()

        class WRing:
            def __init__(self, tiles, sems):
                self.tiles = tiles
                self.sems = sems
                self.free = [None] * len(tiles)
                self.i = 0

            def load(self, ci):
                slot = self.i % len(self.tiles)
                self.i += 1
                tile = self.tiles[slot]
                assert self.free[slot] != "PENDING", "weight ring slot reused before release"
                t = cx.op("sp", lambda e, tile=tile, ci=ci: e.dma_start(out=tile[:], in_=wscr[ci]),
                          waits=[self.free[slot], cast_tick[ci]], dma_sem=self.sems[slot])
                self.free[slot] = "PENDING"
                return slot, tile, t

            def rel(self, slot, tick):
                self.free[slot] = tick

        wsems = [cx.dsem("wslot%d" % i) for i in range(4)]

        if USE_PT2:
            pT2 = psbig[:, 6, :].bitcast(BF16)
            pTs = [pT[:, :], pT2]
        else:
            pTs = [pT[:, :], pT[:, :]]

        def norm_group(hbuf, g0, hnT, waits_per_j, ns):
            s1 = {}

            def stage1(j, ew_=None):
                hsl = hbuf[:, j, :]
                ss = stat(); ms = stat(); rr = stat(); rstd = stat()
                xi = ns["xi"] % len(ns["xs"])
                ns["xi"] += 1
                xsb = ns["xs"][xi]
                ew = list(waits_per_j[j]) if ew_ is None else list(ew_)
                t1 = cx.op("act", lambda e: e.activation(out=junk[:, 0:1024], in_=hsl, func=AF.Square, accum_out=ss), waits=ew)
                t2 = cx.op("dve", lambda e: e.tensor_scalar(out=ms, in0=ss, scalar1=1.0 / D, scalar2=RMS_EPS, op0=ALU.mult, op1=ALU.add), waits=[t1])
                t3 = cx.op("dve", lambda e: e.reciprocal(out=rr, in_=ms), waits=[t2])
                t4 = cx.op("act", lambda e: e.activation(out=rstd, in_=rr, func=AF.Sqrt), waits=[t3])
                t5 = cx.op("dve", lambda e: e.tensor_scalar(out=xsb[:], in0=hsl, scalar1=rstd, scalar2=None, op0=ALU.mult),
                           waits=[t4] + ns["xs_free"][xi] + ew)
                s1[j] = (xi, xsb, t5)

            def stage2(j, extra=()):
                xi, xsb, t5 = s1[j]
                pi = (ns["pi"] % 2) if USE_PT2 else 0
                ns["pi"] += 1
                pTb = pTs[pi]
                tp = None
                for k in range(8):
                    tp = cx.op("pe", lambda e, k=k: e.transpose(pTb[:, k * 128:(k + 1) * 128], xsb[:, k * 128:(k + 1) * 128], ident_bf[:]),
                               waits=[t5] + ns["pT_free"][pi], notick=(k < 7))
                ns["xs_free"][xi] = [tp]
                if USE_BCAST:
                    tl = cx.op("dve", lambda e: e.tensor_tensor(
                        out=hnT[:, :, j * 128:(j + 1) * 128], in0=pTb.rearrange("p (k t) -> p k t", k=8),
                        in1=COLS[:, g0:g0 + 8].unsqueeze(2).broadcast_to([128, 8, 128]), op=ALU.mult), waits=[tp] + list(extra))
                else:
                    for k in range(8):
                        tl = cx.op("dve", lambda e, k=k: e.tensor_scalar(out=hnT[:, k, j * 128:(j + 1) * 128], in0=pTb[:, k * 128:(k + 1) * 128],
                                                                        scalar1=COLS[:, g0 + k:g0 + k + 1], scalar2=None, op0=ALU.mult), waits=[tp])
                ns["pT_free"][pi] = [tl]
                return tl

            if waits_per_j is None:
                return stage1, stage2
            ticks = [None] * NSUB
            stage1(0)
            for j in range(NSUB):
                if j + 1 < NSUB:
                    stage1(j + 1)
                ticks[j] = stage2(j)
            return ticks

        def norm_cb(s1_, s2_, out):
            def cb(j, r):
                s1_(j, r)
                if j >= 1:
                    out[j - 1] = s2_(j - 1)
            return cb

        def new_norm_state(stack, tag):
            return {"xs": [sb("xs%d%s" % (i, tag), [128, D], BF16, stack) for i in range(4)], "xs_free": [[] for _ in range(4)],
                    "pT_free": [[], []], "xi": 0, "pi": 0}

        def ffn(l, ci_g, ci_u, ci_d, hbuf, hnT, hn_ticks, wr, bank, st, on_res=None):
            actT, Abuf, tbuf, sgbuf = st["actT"], st["A"], st["tb"], st["sg"]
            act_ticks = []
            for n in range(4):
                gs, gt, gtk = wr.load(ci_g + n)
                us, ut, utk = wr.load(ci_u + n)
                tl = None
                for f in range(4):
                    fc = n * 4 + f
                    bg, wg = bank.get()
                    tg = None
                    for k in range(8):
                        tg = cx.op("pe", lambda e, k=k, f=f, bg=bg, gt=gt: e.matmul(ps[bg][:, 0:T], lhsT=gt[:, k, f * 128:(f + 1) * 128], rhs=hnT[:, k, :],
                                                                                 start=(k == 0), stop=(k == 7)),
                                   waits=[gtk] + wg + hn_ticks + st["actT_free"], notick=(k < 7))
                    bu, wu = bank.get()
                    tu = None
                    for k in range(8):
                        tu = cx.op("pe", lambda e, k=k, f=f, bu=bu, ut=ut: e.matmul(ps[bu][:, 0:T], lhsT=ut[:, k, f * 128:(f + 1) * 128], rhs=hnT[:, k, :],
                                                                                 start=(k == 0), stop=(k == 7)),
                                   waits=[utk] + wu, notick=(k < 7))
                    tl = tu
                    ai = fc % len(Abuf)
                    A = Abuf[ai]
                    tb = tbuf[fc % len(tbuf)]
                    sg = sgbuf[fc % len(sgbuf)]
                    ta = cx.op("act", lambda e, A=A, bg=bg: e.activation(out=A[:, 2:T + 2], in_=ps[bg][:, 0:T], func=AF.Identity),
                               waits=[tg] + st["A_free"][ai])
                    bank.rel(bg, ta)
                    tc0 = cx.op("dve", lambda e, A=A, fc=fc: e.tensor_copy(out=A[:, 0:2], in_=CCAR[:, l, fc, :]), waits=st["A_free"][ai])
                    t1 = cx.op("dve", lambda e, A=A, tb=tb, fc=fc: e.tensor_scalar(out=tb[:, 0:T], in0=A[:, 2:T + 2], scalar1=col_cw(l, 2, fc), scalar2=col_cb(l, fc),
                                                                               op0=ALU.mult, op1=ALU.add), waits=[ta] + st["tb_free"][fc % len(tbuf)])
                    t2 = cx.op("dve", lambda e, A=A, tb=tb, fc=fc: e.scalar_tensor_tensor(out=tb[:, 0:T], in0=A[:, 1:T + 1], scalar=col_cw(l, 1, fc), in1=tb[:, 0:T],
                                                                                      op0=ALU.mult, op1=ALU.add), waits=[t1, tc0])
                    t3 = cx.op("dve", lambda e, A=A, tb=tb, fc=fc: e.scalar_tensor_tensor(out=tb[:, 0:T], in0=A[:, 0:T], scalar=col_cw(l, 0, fc), in1=tb[:, 0:T],
                                                                                      op0=ALU.mult, op1=ALU.add), waits=[t2])
                    tc1 = cx.op("dve", lambda e, A=A, fc=fc: e.tensor_copy(out=CCAR[:, l, fc, :], in_=A[:, T:T + 2]), waits=[t2])
                    st["A_free"][ai] = [t3, tc1]
                    ts = cx.op("act", lambda e, tb=tb, sg=sg: e.activation(out=sg[:, 0:T], in_=tb[:, 0:T], func=AF.Silu),
                               waits=[t3] + st["sg_free"][fc % len(sgbuf)])
                    st["tb_free"][fc % len(tbuf)] = [ts]
                    tm = cx.op("dve", lambda e, sg=sg, bu=bu, fc=fc: e.tensor_tensor(out=actT[:, fc, :], in0=ps[bu][:, 0:T], in1=sg[:, 0:T], op=ALU.mult),
                               waits=[ts, tu])
                    bank.rel(bu, tm)
                    st["sg_free"][fc % len(sgbuf)] = [tm]
                    act_ticks.append(tm)
                wr.rel(gs, tl)
                wr.rel(us, tl)
            if l == 0 and not st.get("dumped"):
                st["dumped"] = True
                dump("hnT2", hnT[:], [128, 8, T], hn_ticks)
                dump("actT", actT[:], [128, 16, T], act_ticks)
            res_ticks = [[] for _ in range(NSUB)]
            tlast_pe = None
            for n in range(2):
                bks = [bank.get() for _ in range(NSUB)]
                for kh in range(2):
                    dsl, dt_, dtk = wr.load(ci_d + n * 2 + kh)
                    tl = None
                    for j in range(NSUB):
                        bj, wj = bks[j]
                        for k in range(8):
                            tl = cx.op("pe", lambda e, k=k, j=j, bj=bj, dt_=dt_, kh=kh: e.matmul(
                                ps[bj][:, :], lhsT=actT[:, kh * 8 + k, j * 128:(j + 1) * 128], rhs=dt_[:, k, :],
                                start=(kh == 0 and k == 0), stop=(kh == 1 and k == 7)),
                                waits=[dtk] + wj + act_ticks[:(kh + 1) * 8][-8:], notick=not (k == 7))
                            if k == 7 and kh == 1:
                                tr = cx.op("dve", lambda e, j=j, bj=bj, n=n: e.tensor_tensor(
                                    out=hbuf[:, j, n * 512:(n + 1) * 512], in0=ps[bj][:, :], in1=hbuf[:, j, n * 512:(n + 1) * 512], op=ALU.add),
                                    waits=[tl])
                                bank.rel(bj, tr)
                                res_ticks[j].append(tr)
                                if n == 1 and on_res is not None:
                                    on_res(j, res_ticks[j])
                    wr.rel(dsl, tl)
                    tlast_pe = tl
            st["actT_free"] = [tlast_pe]
            return res_ticks

        def new_ffn_state(stack, tag):
            stt = {
                "actT": sb("actT" + tag, [128, 16, T], BF16, stack),
                "A": [sb("A%d%s" % (i, tag), [128, T + 2], F32, stack) for i in range(3)],
                "tb": [sb("tb%d%s" % (i, tag), [128, T], F32, stack) for i in range(2)],
                "sg": [sb("sg%d%s" % (i, tag), [128, T], F32, stack) for i in range(2)],
            }
            stt["A_free"] = [[] for _ in range(3)]
            stt["tb_free"] = [[] for _ in range(2)]
            stt["sg_free"] = [[] for _ in range(2)]
            stt["actT_free"] = []
            return stt

        st_h1 = cx.dsem("st_h1")
        st_q = cx.dsem("st_q")
        st_k = cx.dsem("st_k")
        st_v = cx.dsem("st_v")
        ld_x = [cx.dsem("ld_x%d" % i) for i in range(2)]
        with ExitStack() as p1:
            wtiles = [sb("wr%d" % i, [128, 8, 512], BF16, p1) for i in range(4)]
            wr = WRing(wtiles, wsems)
            hbufs = [sb("hbuf%d" % i, [128, NSUB, D], F32, p1) for i in range(2)]
            ns = new_norm_state(p1, "a")
            hnT = sb("hnT", [128, 8, T], BF16, p1)
            uT = sb("uT", [128, 16, T], BF16, p1)
            vb = sb("vb", [128, NSUB, E], BF16, p1)
            gT = sb("gT", [128, 16, T], BF16, p1)
            tmpg = [sb("tmpg%d" % i, [128, 512], F32, p1) for i in range(2)]
            lpb = sb("lpb", [128, NSUB, 2, NH], F32, p1)
            fst = new_ffn_state(p1, "a")
            bank = Banks(ps[:6] if USE_PT2 else ps)
            tmpg_free = [[], []]
            tmpg_i = [0]
            h_frees = [[], []]
            tx_next = None
            uT_free = []
            vb_free = []
            hnT_free = []
            carc_t = [t_m4]
            gT_free = []

            for stile in range(NST):
                t0 = stile * T
                hbuf = hbufs[stile % 2]

                def load_x(st_):
                    hb_ = hbufs[st_ % 2]
                    return cx.op("sp", lambda e, hb_=hb_, st_=st_: e.dma_start(out=hb_[:], in_=x[st_ * T:(st_ + 1) * T, :].rearrange("(j p) d -> p j d", p=128)),
                                 waits=h_frees[st_ % 2], dma_sem=ld_x[st_ % 2])
                tx = load_x(0) if stile == 0 else tx_next
                if stile + 1 < NST:
                    tx_next = load_x(stile + 1)
                hn_ticks = norm_group(hbuf, 0, hnT, [[tx] + hnT_free for _ in range(NSUB)], ns)
                if stile == 0:
                    dump("hnT", hnT[:], [128, 8, T], hn_ticks)
                vsum = [stat(4) for _ in range(NSUB)]
                v_ticks = [[] for _ in range(NSUB)]
                tl = None
                for n in range(4, 8):
                    sl, wt, wtk = wr.load(CI_WIN + n)
                    for j in range(NSUB):
                        b, wb_ = bank.get()
                        for k in range(8):
                            tl = cx.op("pe", lambda e, k=k, j=j, b=b, wt=wt: e.matmul(ps[b][:, :], lhsT=hnT[:, k, j * 128:(j + 1) * 128], rhs=wt[:, k, :],
                                                                                   start=(k == 0), stop=(k == 7)),
                                       waits=[wtk, hn_ticks[j]] + wb_, notick=(k < 7))
                        tg_ = cx.op("act", lambda e, j=j, b=b, n=n: e.activation(out=vb[:, j, (n - 4) * 512:(n - 3) * 512], in_=ps[b][:, :], func=AF.Gelu_apprx_tanh,
                                                                             accum_out=vsum[j][:, n - 4:n - 3]), waits=[tl] + vb_free)
                        bank.rel(b, tg_)
                        v_ticks[j].append(tg_)
                    wr.rel(sl, tl)
                vh_ticks = []
                for j in range(NSUB):
                    ssq = stat()
                    tot = stat()
                    mean = stat()
                    msq = stat()
                    var = stat()
                    rr = stat()
                    rstd = stat()
                    nmr = stat()
                    ta = cx.op("act", lambda e, j=j, ssq=ssq: e.activation(out=junk[:, :], in_=vb[:, j, :], func=AF.Square, accum_out=ssq),
                               waits=v_ticks[j])
                    tb_ = cx.op("dve", lambda e, j=j, tot=tot: e.tensor_reduce(out=tot, in_=vsum[j], axis=AX.X, op=ALU.add), waits=v_ticks[j])
                    tc = cx.op("dve", lambda e, tot=tot, mean=mean: e.tensor_scalar(out=mean, in0=tot, scalar1=1.0 / E, scalar2=None, op0=ALU.mult), waits=[tb_])
                    td = cx.op("dve", lambda e, mean=mean, msq=msq: e.tensor_tensor(out=msq, in0=mean, in1=mean, op=ALU.mult), waits=[tc])
                    te = cx.op("dve", lambda e, ssq=ssq, msq=msq, var=var: e.scalar_tensor_tensor(out=var, in0=ssq, scalar=1.0 / E, in1=msq, op0=ALU.mult, op1=ALU.subtract),
                               waits=[ta, td])
                    tf = cx.op("dve", lambda e, var=var: e.tensor_scalar(out=var, in0=var, scalar1=LN_EPS, scalar2=None, op0=ALU.add), waits=[te])
                    tg2 = cx.op("dve", lambda e, var=var, rr=rr: e.reciprocal(out=rr, in_=var), waits=[tf])
                    th = cx.op("act", lambda e, rr=rr, rstd=rstd: e.activation(out=rstd, in_=rr, func=AF.Sqrt), waits=[tg2])
                    ti = cx.op("dve", lambda e, mean=mean, rstd=rstd, nmr=nmr: e.scalar_tensor_tensor(out=nmr, in0=mean, scalar=-1.0, in1=rstd, op0=ALU.mult, op1=ALU.mult),
                               waits=[th, tc])
                    tj = cx.op("dve", lambda e, j=j, rstd=rstd, nmr=nmr: e.tensor_scalar(out=vb[:, j, :], in0=vb[:, j, :], scalar1=rstd, scalar2=nmr, op0=ALU.mult, op1=ALU.add),
                               waits=[ti, ta])
                    vh_ticks.append(tj)
                u_ticks = []
                g_tick = {}

                def spatial_group(q4):
                    for j in range(NSUB):
                        b, wb_ = bank.get()
                        tl_ = None
                        for c4 in range(4):
                            cidx = q4 * 4 + c4
                            tl_ = cx.op("pe", lambda e, j=j, cidx=cidx, c4=c4, b=b: e.matmul(
                                ps[b][:, c4 * 128:(c4 + 1) * 128], lhsT=vb[:, j, cidx * 128:(cidx + 1) * 128], rhs=WcT_bf[:, cidx // 2, :], start=True, stop=True),
                                waits=[vh_ticks[j]] + wb_, notick=(c4 < 3))
                        ti_ = tmpg_i[0] % 2
                        tmpg_i[0] += 1
                        tmp = tmpg[ti_]
                        tt = None
                        for c4 in range(4):
                            cidx = q4 * 4 + c4
                            tt = cx.op("dve", lambda e, cidx=cidx, c4=c4, b=b, tmp=tmp: e.scalar_tensor_tensor(
                                out=tmp[:, c4 * 128:(c4 + 1) * 128], in0=ps[b][:, c4 * 128:(c4 + 1) * 128], scalar=col_lng(cidx), in1=BiasT[:, cidx, :],
                                op0=ALU.mult, op1=ALU.add), waits=[tl_] + tmpg_free[ti_])
                        bank.rel(b, tt)
                        tm = cx.op("dve", lambda e, q4=q4, j=j, tmp=tmp: e.tensor_tensor(
                            out=gT[:, q4 * 4:(q4 + 1) * 4, j * 128:(j + 1) * 128],
                            in0=tmp[:, :].rearrange("p (c t) -> p c t", c=4), in1=uT[:, q4 * 4:(q4 + 1) * 4, j * 128:(j + 1) * 128], op=ALU.mult),
                            waits=[tt] + u_ticks[q4 * 4:(q4 + 1) * 4] + gT_free)
                        tmpg_free[ti_] = [tm]
                        g_tick[(j, q4)] = tm

                for n in range(4):
                    sl, wt, wtk = wr.load(CI_WIN + n)
                    for f in range(4):
                        fc = n * 4 + f
                        b, wb_ = bank.get()
                        for k in range(8):
                            tl = cx.op("pe", lambda e, k=k, f=f, b=b, wt=wt: e.matmul(ps[b][:, 0:T], lhsT=wt[:, k, f * 128:(f + 1) * 128], rhs=hnT[:, k, :],
                                                                                   start=(k == 0), stop=(k == 7)),
                                       waits=[wtk] + hn_ticks + wb_, notick=(k < 7))
                        tg_ = cx.op("act", lambda e, fc=fc, b=b: e.activation(out=uT[:, fc, :], in_=ps[b][:, 0:T], func=AF.Gelu_apprx_tanh),
                                    waits=[tl] + uT_free)
                        bank.rel(b, tg_)
                        u_ticks.append(tg_)
                    wr.rel(sl, tl)
                    if n >= 1:
                        spatial_group(n - 1)
                hnT_free = [tl]
                spatial_group(3)
                g_ticks = [g_tick[(j, q4)] for j in range(NSUB) for q4 in range(4)]
                if stile == 0:
                    dump("vhat", vb[:], [128, NSUB, E], vh_ticks)
                    dump("uT", uT[:], [128, 16, T], u_ticks)
                res0 = [[] for _ in range(NSUB)]
                n_s1, n_s2 = norm_group(hbuf, 16, hnT, None, ns)
                hn_f = [None] * NSUB
                n_cb = norm_cb(n_s1, n_s2, hn_f)
                for n in range(2):
                    bks = [bank.get() for _ in range(NSUB)]
                    for kh in range(2):
                        sl, wt, wtk = wr.load(CI_WOUT + n * 2 + kh)
                        for j in range(NSUB):
                            bj, wj = bks[j]
                            for k in range(8):
                                tl = cx.op("pe", lambda e, k=k, j=j, bj=bj, wt=wt, kh=kh: e.matmul(
                                    ps[bj][:, :], lhsT=gT[:, kh * 8 + k, j * 128:(j + 1) * 128], rhs=wt[:, k, :],
                                    start=(kh == 0 and k == 0), stop=(kh == 1 and k == 7)),
                                    waits=[wtk] + wj + g_ticks[j * 4:(j + 1) * 4], notick=(k < 7))
                            if kh == 1:
                                tr = cx.op("dve", lambda e, j=j, bj=bj, n=n, hbuf=hbuf: e.tensor_tensor(
                                    out=hbuf[:, j, n * 512:(n + 1) * 512], in0=ps[bj][:, :], in1=hbuf[:, j, n * 512:(n + 1) * 512], op=ALU.add), waits=[tl])
                                bank.rel(bj, tr)
                                res0[j].append(tr)
                                if n == 1:
                                    n_cb(j, res0[j])
                        wr.rel(sl, tl)
                gT_free = [tl]
                if stile == 0:
                    dump("gT", gT[:], [128, 16, T], g_ticks)
                    dump("hmix", hbufs[0][:], [128, NSUB, D], [t for r_ in res0 for t in r_])
                hn_f[NSUB - 1] = n_s2(NSUB - 1)
                hn_ticks = list(hn_f)
                m_s1, m_s2 = norm_group(hbuf, 8, hnT, None, ns)
                hn_m = [None] * NSUB
                res1 = ffn(0, CI_G0, CI_U0, CI_D0, hbuf, hnT, hn_ticks, wr, bank, fst, on_res=norm_cb(m_s1, m_s2, hn_m))
                hnT_free = [cx.last("pe")]
                allres = [t for r in res1 for t in r]
                th1 = cx.op("pool", lambda e, t0=t0, hbuf=hbuf: e.dma_start(out=h1s[t0:t0 + T, :].rearrange("(j p) d -> p j d", p=128), in_=hbuf[:]),
                            waits=allres, dma_sem=st_h1)
                hn_m[NSUB - 1] = m_s2(NSUB - 1)
                hn_ticks = list(hn_m)
                h_frees[stile % 2] = [th1] + hn_ticks
                qk_ticks = {0: [], 1: []}
                for n in range(4):
                    sl, wt, wtk = wr.load(CI_QKV + n)
                    for f in range(4):
                        fc = (n % 2) * 4 + f
                        b, wb_ = bank.get()
                        for k in range(8):
                            tl = cx.op("pe", lambda e, k=k, f=f, b=b, wt=wt: e.matmul(ps[b][:, 0:T], lhsT=wt[:, k, f * 128:(f + 1) * 128], rhs=hnT[:, k, :],
                                                                                   start=(k == 0), stop=(k == 7)),
                                       waits=[wtk] + hn_ticks + wb_, notick=(k < 7))
                        dst = uT[:, (n // 2) * 8 + fc, :]
                        tcp = cx.op("act", lambda e, dst=dst, b=b: e.activation(out=dst, in_=ps[b][:, 0:T], func=AF.Identity), waits=[tl] + g_ticks)
                        bank.rel(b, tcp)
                        qk_ticks[n // 2].append(tcp)
                    wr.rel(sl, tl)
                tq = cx.op("pool", lambda e, t0=t0: e.dma_start(out=qTs[:, t0:t0 + T].rearrange("(k p) t -> p k t", p=128), in_=uT[:, 0:8, :]),
                           waits=qk_ticks[0], dma_sem=st_q)
                tk = cx.op("pool", lambda e, t0=t0: e.dma_start(out=kTs[:, t0:t0 + T].rearrange("(k p) t -> p k t", p=128), in_=uT[:, 8:16, :]),
                           waits=qk_ticks[1], dma_sem=st_k)
                uT_free = [tq, tk]
                v_st_ticks = []
                for n in range(2):
                    sl, wt, wtk = wr.load(CI_QKV + 4 + n)
                    for j in range(NSUB):
                        b, wb_ = bank.get()
                        for k in range(8):
                            tl = cx.op("pe", lambda e, k=k, j=j, b=b, wt=wt: e.matmul(ps[b][:, :], lhsT=hnT[:, k, j * 128:(j + 1) * 128], rhs=wt[:, k, :],
                                                                                   start=(k == 0), stop=(k == 7)),
                                       waits=[wtk, hn_ticks[j]] + wb_, notick=(k < 7))
                        tcp = cx.op("act", lambda e, j=j, n=n, b=b: e.activation(out=vb[:, j, n * 512:(n + 1) * 512], in_=ps[b][:, :], func=AF.Identity),
                                    waits=[tl, cx.last("pe")])
                        bank.rel(b, tcp)
                        v_st_ticks.append(tcp)
                    wr.rel(sl, tl)
                tv = cx.op("pool", lambda e, t0=t0: e.dma_start(out=vs[t0:t0 + T, :].rearrange("(j p) d -> p j d", p=128), in_=vb[:, :, 0:D]),
                           waits=v_st_ticks, dma_sem=st_v)
                vb_free = [tv]
                for j in range(NSUB):
                    blk = stile * NSUB + j
                    b, wb_ = bank.get()
                    for k in range(8):
                        tl = cx.op("pe", lambda e, k=k, j=j, b=b: e.matmul(ps[b][:, 0:NH], lhsT=hnT[:, k, j * 128:(j + 1) * 128], rhs=wf_bf[:, k, :],
                                                                        start=(k == 0), stop=(k == 7)),
                                   waits=[hn_ticks[j], t_wf] + wb_, notick=(k < 7))
                    xf = lpb[:, j, 0, :]
                    lp = lpb[:, j, 1, :]
                    t1 = cx.op("dve", lambda e, b=b, xf=xf: e.tensor_tensor(out=xf, in0=ps[b][:, 0:NH], in1=bf_bc[:, :], op=ALU.add), waits=[tl, cx.last("pe")])
                    t2 = cx.op("act", lambda e, xf=xf: e.activation(out=xf, in_=xf, func=AF.Exp, scale=-1.0), waits=[t1])
                    t3 = cx.op("act", lambda e, xf=xf, lp=lp: e.activation(out=lp, in_=xf, func=AF.Ln, bias=1.0), waits=[t2])
                    t4 = None
                    cx.op("pe", lambda e, b=b, lp=lp: e.matmul(ps[b][:, 32:32 + NH], lhsT=tri_f[:, :], rhs=lp, start=True, stop=True), waits=[t3, t1], notick=True)
                    t4 = cx.op("pe", lambda e, b=b, lp=lp: e.matmul(ps[b][:, 64:64 + NH], lhsT=ones_f[:, :], rhs=lp, start=True, stop=True))
                    t5 = cx.op("dve", lambda e, b=b, blk=blk: e.tensor_tensor(out=CUM[:, blk, :], in0=CARC[:, blk, :], in1=ps[b][:, 32:32 + NH], op=ALU.subtract),
                               waits=[t4] + carc_t)
                    t6 = cx.op("dve", lambda e, b=b, blk=blk: e.tensor_tensor(out=CARC[:, blk + 1, :], in0=CARC[:, blk, :], in1=ps[b][:, 64:64 + NH], op=ALU.subtract),
                               waits=[t4] + carc_t)
                    carc_t = [t6]
                    bank.rel(b, t6)
                hnT_free = [cx.last("pe")]
            cx.flush()

        ld_k = [cx.dsem("ld_k%d" % i) for i in range(2)]
        ld_q = [cx.dsem("ld_q%d" % i) for i in range(2)]
        ld_v = [cx.dsem("ld_v%d" % i) for i in range(2)]
        st_a = [cx.dsem("st_a%d" % i) for i in range(4)]
        TQ = min(1024, S)
        NQ = S // TQ
        BW = min(512, TQ)
        NB = TQ // BW
        KBQ = TQ // 128
        NSB = 3
        with ExitStack() as p2:
            KT = [sb("KT%d" % i, [128, S], BF16, p2) for i in range(2)]
            QZ = [[sb("QZ%d_%d" % (i, hh), [128, S], BF16, p2) for hh in range(2)] for i in range(2)]
            VA = [sb("VA%d" % i, [128, NBLK, 128], BF16, p2) for i in range(2)]
            NPT = 4
            PT = [sb("PT%d" % i, [128, TQ], BF16, p2) for i in range(NPT)]
            biasH = [sb("biasH%d" % i, [128, NQ, NBLK], F32, p2) for i in range(2)]
            Osb = [sb("Osb%d" % i, [64, 512], F32, p2) for i in range(4)]
            Lsb = [sb("Lsb%d" % i, [64, 512], F32, p2) for i in range(4)]
            ast = [sb("ast%d" % i, [64, 512], BF16, p2) for i in range(4)]
            tmo = []
            tmv = []
            for i in range(2):
                tmv.append(cx.op("dve", lambda e, i=i: e.memset(VA[i][:, :, DH:128], 1.0)))
                cx.op("dve", lambda e, i=i: e.memset(QZ[i][0][64:128, :], 0.0))
                tmo.append(cx.op("dve", lambda e, i=i: e.memset(QZ[i][1][0:64, :], 0.0)))
            SB = [psbig[:, 2 * b_:2 * b_ + NB, :] for b_ in range(NSB)]
            SBf = [SB[b_].rearrange("p a b -> p (a b)") for b_ in range(NSB)]
            OBs = [psbig[:, 6, :], pT[:, :].bitcast(F32)]
            sfree = [[] for _ in range(NSB)]
            s_i = [0]
            pt_free = [[] for _ in range(NPT)]
            pt_i = [0]
            biasH_free = [[], []]
            osb_free = [[] for _ in range(4)]
            osb_i = [0]
            ofree = [[] for _ in range(NB)]
            ast_free = [[] for _ in range(4)]
            ast_i = [0]
            buf_free = [[], []]
            va_free = [[], []]
            pvq = []
            PV_LAG = 2
            last_pe_box = [None]
            for hp in range(NH // 2):
                bi = hp % 2
                tkl = cx.op("sp", lambda e, hp=hp, bi=bi: e.dma_start(out=KT[bi][:], in_=kTs[hp * 128:(hp + 1) * 128, :]), waits=buf_free[bi], dma_sem=ld_k[bi])
                cx.op("sp", lambda e, hp=hp, bi=bi: e.dma_start(out=QZ[bi][0][0:64, :], in_=qTs[hp * 128:hp * 128 + 64, :]),
                      waits=buf_free[bi] + [tmo[bi]], dma_sem=ld_q[bi])
                tql = cx.op("sp", lambda e, hp=hp, bi=bi: e.dma_start(out=QZ[bi][1][64:128, :], in_=qTs[hp * 128 + 64:hp * 128 + 128, :]),
                            waits=buf_free[bi] + [tmo[bi]], dma_sem=ld_q[bi])
                tvls = []
                for hh_ in range(2):
                    h_ = hp * 2 + hh_
                    tvls.append(cx.op("sp", lambda e, h_=h_: e.dma_start(
                        out=VA[h_ % 2][:, :, 0:DH], in_=vs[:, h_ * 64:(h_ + 1) * 64].rearrange("(b p) d -> p b d", p=128)),
                        waits=va_free[h_ % 2] + [tmv[h_ % 2]], dma_sem=ld_v[h_ % 2]))
                last_pe = None
                for hh in range(2):
                    h = hp * 2 + hh
                    vbuf = VA[h % 2]
                    tvl = tvls[hh]
                    bH = biasH[h % 2]
                    tbias = None
                    for qp in range(NQ):
                        nkb = (qp + 1) * KBQ
                        cref_blk = qp * KBQ + KBQ // 2
                        tbias = cx.op("dve", lambda e, bH=bH, h=h, nkb=nkb, qp=qp, cref_blk=cref_blk: e.tensor_scalar(
                            out=bH[:, qp, 0:nkb], in0=CUM[:, 0:nkb, h], scalar1=-1.0, scalar2=CARC[:, cref_blk, h:h + 1], op0=ALU.mult, op1=ALU.add),
                            waits=biasH_free[h % 2])
                    for qp in range(NQ):
                        q0 = qp * TQ
                        nkb = (qp + 1) * KBQ
                        bt = bH[:, qp, :]
                        lastj = [max(j for j in range(nkb) if max(j - qp * KBQ, 0) * 128 < (nb + 1) * BW) for nb in range(NB)]
                        exp_ticks = []
                        pv_last = [None] * NB
                        for j in range(nkb):
                            jj = j - qp * KBQ
                            c0 = max(jj, 0) * 128
                            sbi = s_i[0] % NSB
                            s_i[0] += 1
                            tqk = None
                            for nb in range(NB):
                                lo = max(c0, nb * BW)
                                hi = (nb + 1) * BW
                                if lo >= hi:
                                    continue
                                tqk = cx.op("pe", lambda e, sbi=sbi, nb=nb, lo=lo, hi=hi, j=j, bi=bi, hh=hh, q0=q0: e.matmul(
                                    SB[sbi][:, nb, lo - nb * BW:BW], lhsT=KT[bi][:, j * 128:(j + 1) * 128], rhs=QZ[bi][hh][:, q0 + lo:q0 + hi], start=True, stop=True),
                                    waits=[tkl, tql] + sfree[sbi], notick=(jj >= 0 or nb < NB - 1))
                            if jj >= 0:
                                nbm = c0 // BW
                                off = c0 - nbm * BW
                                tqk = cx.op("pe", lambda e, sbi=sbi, nbm=nbm, off=off: e.matmul(
                                    SB[sbi][:, nbm, off:off + 128], lhsT=ident_bf[:, :], rhs=maskneg[:, :], start=False, stop=True))
                            pi = pt_i[0] % NPT
                            pt_i[0] += 1
                            te = cx.op("act", lambda e, pi=pi, sbi=sbi, c0=c0, bt=bt, j=j: e.activation(
                                out=PT[pi][:, c0:TQ], in_=SBf[sbi][:, c0:TQ], func=AF.Exp, scale=DH ** -0.5, bias=bt[:, j:j + 1]),
                                waits=[tqk, tbias] + pt_free[pi])
                            sfree[sbi] = [te]
                            exp_ticks.append(te)

                            def do_pv(pi_=pi, te_=te, c0_=c0, j_=j, vbuf=vbuf, tvl=tvl, lastj=lastj, pv_last=pv_last):
                                tpv = None
                                for nb in range(NB):
                                    lo = max(c0_, nb * BW)
                                    hi = (nb + 1) * BW
                                    if lo >= hi:
                                        continue
                                    tpv = cx.op("pe", lambda e, nb=nb, lo=lo, hi=hi, st_=(j_ == lastj[nb]): e.matmul(
                                        OBs[nb][:, lo - nb * BW:BW], lhsT=vbuf[:, j_, :], rhs=PT[pi_][:, lo:hi], start=(j_ == 0), stop=st_),
                                        waits=[te_, tvl] + (ofree[nb] if j_ == 0 else []))
                                    pv_last[nb] = tpv
                                pt_free[pi_] = [tpv]
                                last_pe_box[0] = tpv
                            pvq.append(do_pv)
                            while len(pvq) > PV_LAG:
                                pvq.pop(0)()
                        if qp == NQ - 1:
                            biasH_free[h % 2] = [exp_ticks[-1]]

                        def do_norm(h=h, q0=q0, pv_last=pv_last):
                            for nb in range(NB):
                                oi = osb_i[0] % 4
                                osb_i[0] += 1
                                tc1 = cx.op("dve", lambda e, nb=nb, oi=oi: e.tensor_copy(out=Osb[oi][:, 0:BW], in_=OBs[nb][0:64, 0:BW]),
                                            waits=[pv_last[nb]] + osb_free[oi])
                                tc2 = cx.op("dve", lambda e, nb=nb, oi=oi: e.tensor_copy(out=Lsb[oi][:, 0:BW], in_=OBs[nb][64:128, 0:BW]),
                                            waits=[pv_last[nb]] + osb_free[oi])
                                ofree[nb] = [tc1, tc2]
                                trc = cx.op("dve", lambda e, oi=oi: e.reciprocal(out=Lsb[oi][:, 0:BW], in_=Lsb[oi][:, 0:BW]), waits=[tc2])
                                ai = ast_i[0] % 4
                                ast_i[0] += 1
                                tmu = cx.op("dve", lambda e, ai=ai, oi=oi: e.tensor_tensor(out=ast[ai][:, 0:BW], in0=Osb[oi][:, 0:BW], in1=Lsb[oi][:, 0:BW], op=ALU.mult),
                                            waits=[trc, tc1] + ast_free[ai])
                                osb_free[oi] = [tmu]
                                tst = cx.op("pool", lambda e, ai=ai, nb=nb: e.dma_start(out=aTs[h * 64:(h + 1) * 64, q0 + nb * BW:q0 + (nb + 1) * BW], in_=ast[ai][:, 0:BW]),
                                            waits=[tmu], dma_sem=st_a[ai])
                                ast_free[ai] = [tst]
                        pvq.append(do_norm)
                    while pvq:
                        pvq.pop(0)()
                    last_pe = last_pe_box[0]
                    va_free[h % 2] = [last_pe]
                buf_free[bi] = [last_pe]
            cx.flush()

        ld_h = [cx.dsem("ld_h%d" % i) for i in range(2)]
        ld_a = cx.dsem("ld_a")
        st_y = cx.dsem("st_y")
        with ExitStack() as p3:
            wtiles = [sb("wq%d" % i, [128, 8, 512], BF16, p3) for i in range(4)]
            wr = WRing(wtiles, wsems)
            hbufs = [sb("hbuf3%d" % i, [128, NSUB, D], F32, p3) for i in range(2)]
            aT = sb("aT", [128, 8, T], BF16, p3)
            ns = new_norm_state(p3, "b")
            hnT = sb("hnT3", [128, 8, T], BF16, p3)
            fst = new_ffn_state(p3, "b")
            bank = Banks(ps[:6] if USE_PT2 else ps)
            h_frees = [[], []]
            aT_free = []
            hnT_free = []
            for stile in range(NST):
                t0 = stile * T
                hbuf = hbufs[stile % 2]
                th = cx.op("sp", lambda e, t0=t0, hbuf=hbuf: e.dma_start(out=hbuf[:], in_=h1s[t0:t0 + T, :].rearrange("(j p) d -> p j d", p=128)),
                           waits=h_frees[stile % 2], dma_sem=ld_h[stile % 2])
                ta_ = cx.op("sp", lambda e, t0=t0: e.dma_start(out=aT[:], in_=aTs[:, t0:t0 + T].rearrange("(k p) t -> p k t", p=128)),
                            waits=aT_free, dma_sem=ld_a)
                res0 = [[] for _ in range(NSUB)]
                tl = None
                n_s1, n_s2 = norm_group(hbuf, 24, hnT, None, ns)
                hn_f = [None] * NSUB
                n_cb = norm_cb(n_s1, n_s2, hn_f)
                for n in range(2):
                    sl, wt, wtk = wr.load(CI_WO + n)
                    for j in range(NSUB):
                        b, wb_ = bank.get()
                        for k in range(8):
                            tl = cx.op("pe", lambda e, k=k, j=j, b=b, wt=wt: e.matmul(ps[b][:, :], lhsT=aT[:, k, j * 128:(j + 1) * 128], rhs=wt[:, k, :],
                                                                                   start=(k == 0), stop=(k == 7)),
                                       waits=[wtk, ta_] + wb_, notick=(k < 7))
                        tr = cx.op("dve", lambda e, j=j, b=b, n=n, hbuf=hbuf: e.tensor_tensor(
                            out=hbuf[:, j, n * 512:(n + 1) * 512], in0=ps[b][:, :], in1=hbuf[:, j, n * 512:(n + 1) * 512], op=ALU.add), waits=[tl, th])
                        bank.rel(b, tr)
                        res0[j].append(tr)
                        if n == 1:
                            n_cb(j, res0[j])
                    wr.rel(sl, tl)
                aT_free = [tl]
                hn_f[NSUB - 1] = n_s2(NSUB - 1)
                hn_ticks = list(hn_f)
                outs = []

                def final_norm(j, rj, hbuf=hbuf, outs=outs):
                    res1 = {j: rj}
                    ss = stat()
                    ms = stat()
                    rr = stat()
                    rstd = stat()
                    hsl = hbuf[:, j, :]
                    t1 = cx.op("act", lambda e, hsl=hsl, ss=ss: e.activation(out=junk[:, 0:1024], in_=hsl, func=AF.Square, accum_out=ss),
                               waits=res1[j])
                    t2 = cx.op("dve", lambda e, ss=ss, ms=ms: e.tensor_scalar(out=ms, in0=ss, scalar1=1.0 / D, scalar2=RMS_EPS, op0=ALU.mult, op1=ALU.add), waits=[t1])
                    t3 = cx.op("dve", lambda e, ms=ms, rr=rr: e.reciprocal(out=rr, in_=ms), waits=[t2])
                    t4 = cx.op("act", lambda e, rr=rr, rstd=rstd: e.activation(out=rstd, in_=rr, func=AF.Sqrt), waits=[t3])
                    t5 = cx.op("dve", lambda e, hsl=hsl, rstd=rstd: e.scalar_tensor_tensor(out=hsl, in0=hsl, scalar=rstd, in1=gfin_bc[:, :], op0=ALU.mult, op1=ALU.mult),
                               waits=[t4, t1] + res1[j])
                    outs.append(t5)
                res1 = ffn(1, CI_G1, CI_U1, CI_D1, hbuf, hnT, hn_ticks, wr, bank, fst, on_res=final_norm)
                hnT_free = [cx.last("pe")]
                ty = cx.op("pool", lambda e, t0=t0, hbuf=hbuf: e.dma_start(out=y[t0:t0 + T, :].rearrange("(j p) d -> p j d", p=128), in_=hbuf[:]),
                           waits=list(outs), dma_sem=st_y)
                h_frees[stile % 2] = [ty]
            cx.flush()
    return nc


def _consts():
    bf = ml_dtypes.bfloat16
    ident = np.eye(128, dtype=np.float32)
    s = np.arange(128)[:, None]
    t = np.arange(128)[None, :]
    tri = (s <= t).astype(np.float32)
    maskneg = np.where(t >= s, 0.0, NEG).astype(np.float32)
    return {
        "c_ident_bf": ident.astype(bf), "c_ident_f": ident, "c_tri_f": tri, "c_maskneg": maskneg.astype(bf),
    }


def make_in_maps(inputs, S, n_cores):
    c = _consts()
    f = lambda a: np.ascontiguousarray(np.asarray(a, dtype=np.float32))
    shared = {
        "mix_norm_g": f(inputs["mix_norm_g"]), "ffn_norm_g": f(inputs["ffn_norm_g"]),
        "gm_w_in": f(inputs["gm_w_in"][0]), "gm_ln_g": f(inputs["gm_ln_g"][0]), "gm_ln_b": f(inputs["gm_ln_b"][0]),
        "gm_w_s": f(inputs["gm_w_s"][0]), "gm_b_s": f(inputs["gm_b_s"][0]), "gm_w_out": f(inputs["gm_w_out"][0]),
        "fox_w_qkvf": f(inputs["fox_w_qkvf"][0]), "fox_b_f": f(inputs["fox_b_f"][0]), "fox_w_o": f(inputs["fox_w_o"][0]),
        "ffn_w_gate": f(inputs["ffn_w_gate"]), "ffn_w_up": f(inputs["ffn_w_up"]), "ffn_conv_w": f(inputs["ffn_conv_w"]),
        "ffn_conv_b": f(inputs["ffn_conv_b"]), "ffn_w_down": f(inputs["ffn_w_down"]), "final_norm_g": f(inputs["final_norm_g"]),
    }
    shared.update(c)
    xs = np.asarray(inputs["x"], dtype=np.float32)
    maps = []
    for i in range(n_cores):
        m = dict(shared)
        m["x"] = np.ascontiguousarray(xs[i, :S])
        maps.append(m)
    return maps


_NC_CACHE = {}


def kernel(**inputs):
    xs = np.asarray(inputs["x"])
    B, S, _ = xs.shape
    if S not in _NC_CACHE:
        _NC_CACHE[S] = build(S)
    nc = _NC_CACHE[S]
    maps = make_in_maps(inputs, S, B)
    res = run_bass_kernel_spmd(nc, maps, core_ids=list(range(B)))
    out = np.stack([np.asarray(r["y"], dtype=np.float32) for r in res.results], axis=0)
    return out
```
